# Optimizing a Trainium2 kernel written in Bass

```python
import math
import jax
import jax.numpy as jnp
from jax import lax
import numpy as np

D_MODEL = 1024
BATCH = 8
SEQ = 2048
DEPTH = 2
DEC_BATCH = 128
DEC_SEQ = 4
PAST_LEN = 16384
PAGE_SIZE = 128

D_MIX = D_MODEL
HEAD_DIM = 64
A_HEADS = 6
B_HEADS = 6
C_HEADS = 4
A_WIDTH = A_HEADS * HEAD_DIM
B_WIDTH = B_HEADS * HEAD_DIM
C_WIDTH = C_HEADS * HEAD_DIM
A_DECAY_LORA = 32
A_AAA_LORA = 32
A_GATE_LORA = 64
A_COLS = 3 * A_WIDTH + A_DECAY_LORA + A_AAA_LORA + A_GATE_LORA
A_SPLITS = (A_WIDTH, 2 * A_WIDTH, 3 * A_WIDTH, 3 * A_WIDTH + A_DECAY_LORA, 3 * A_WIDTH + A_DECAY_LORA + A_AAA_LORA)
B_QKV = 3 * B_WIDTH
B_COLS = B_QKV + 2 * B_HEADS + B_WIDTH
C_COLS = 4 * C_WIDTH
IN_COLS = A_COLS + B_COLS + C_COLS
CONV_K = 4
CHUNK = 64
FF_HIDDEN = ((8 * D_MODEL // 3 + 255) // 256) * 256
N_ADA = 6
RMS_EPS = 1e-6
L2_EPS = 1e-6
GN_EPS = 64e-5

kernel_name = 'hybrid_rwkv7_gdn_hgrn2_adaln_step'


def rms_norm(x, w, eps=RMS_EPS):
    xf = x.astype(jnp.float32)
    y = xf * lax.rsqrt(jnp.mean(xf * xf, axis=-1, keepdims=True) + eps)
    return (y * w.astype(jnp.float32)).astype(x.dtype)


def l2_normalize(x, eps=L2_EPS):
    xf = x.astype(jnp.float32)
    return (xf * lax.rsqrt(jnp.sum(xf * xf, axis=-1, keepdims=True) + eps)).astype(x.dtype)


def group_norm_heads(y, w, b, eps=GN_EPS):
    B, T, H, D = y.shape
    mu = jnp.mean(y, axis=-1, keepdims=True)
    var = jnp.mean(jnp.square(y - mu), axis=-1, keepdims=True)
    yn = ((y - mu) * lax.rsqrt(var + eps)).reshape(B, T, H * D)
    return yn * w.astype(jnp.float32) + b.astype(jnp.float32)


def causal_depthwise_conv(x, buf, w):
    T = x.shape[1]
    xp = jnp.concatenate([buf.astype(x.dtype), x], axis=1)
    y = sum(xp[:, j:j + T] * w[j] for j in range(CONV_K))
    return y, xp[:, -(CONV_K - 1):]


def pad_time(a, C):
    pad = (-a.shape[1]) % C
    if pad == 0:
        return a
    widths = [(0, 0)] * a.ndim
    widths[1] = (0, pad)
    return jnp.pad(a, widths)


def to_chunks(a, C):
    B, Tp = a.shape[:2]
    a = a.reshape((B, Tp // C, C) + a.shape[2:])
    return jnp.swapaxes(jnp.swapaxes(a, 2, 3), 0, 1)


def from_chunks(o, T):
    N, B, H, C, D = o.shape
    return jnp.swapaxes(jnp.swapaxes(o, 0, 1), 2, 3).reshape(B, N * C, H, D)[:, :T]


def rwkv7_mixer(p_a, p_prev, S0, mu, w0, w2, a0, a2, g2, k_k, k_a, r_k, ln_w, ln_b):
    B, T, _ = p_a.shape
    f32 = jnp.float32
    prev = jnp.concatenate([p_prev.astype(p_a.dtype), p_a[:, :-1]], axis=1)
    xs = p_a + mu * (prev - p_a)
    r, k, v, xw, xa, xg = jnp.split(xs, A_SPLITS, axis=-1)
    w = -jax.nn.softplus(-(w0 + jnp.tanh(xw) @ w2)) - 0.5
    decay = jnp.exp(-jnp.exp(w.astype(f32)))
    a = jax.nn.sigmoid(a0 + xa @ a2)
    g = jax.nn.sigmoid(xg) @ g2
    hd = lambda t: t.reshape(B, T, A_HEADS, HEAD_DIM)
    kk = l2_normalize(hd(k * k_k))
    k = k * (1 + (a - 1) * k_a)
    r_h, k_h, v_h, a_h = hd(r), hd(k), hd(v), hd(a)
    seq = tuple(jnp.swapaxes(t.astype(f32), 0, 1) for t in (r_h, hd(decay), k_h, v_h, -kk, kk * a_h))

    def step(S, inp):
        rt, dt, kt, vt, at, bt = inp
        sa = jnp.einsum('bhvk,bhk->bhv', S, at)
        S = S * dt[:, :, None, :] + sa[..., None] * bt[:, :, None, :] + vt[..., None] * kt[:, :, None, :]
        return S, jnp.einsum('bhvk,bhk->bhv', S, rt)

    S, y = lax.scan(step, S0.astype(f32), seq)
    y = group_norm_heads(jnp.swapaxes(y, 0, 1), ln_w, ln_b)
    bonus = (jnp.sum(r_h * k_h * r_k, axis=-1, keepdims=True) * v_h).reshape(B, T, A_WIDTH)
    out = (y + bonus.astype(f32)) * g.astype(f32)
    return out.astype(p_a.dtype), S.astype(S0.dtype)


def gated_delta_chunked(q, k, v, g, beta, S0):
    T = q.shape[1]
    C = min(CHUNK, T)
    f32 = jnp.float32
    scale = q.shape[-1] ** -0.5
    qc, kc, vc, gc, bc = (to_chunks(pad_time(t, C), C) for t in
                          (q.astype(f32) * scale, k.astype(f32), v.astype(f32), g.astype(f32), beta.astype(f32)))
    gcum = jnp.cumsum(gc, axis=-1)
    tril = jnp.tril(jnp.ones((C, C), bool))
    strict = jnp.tril(jnp.ones((C, C), bool), -1)
    eye = jnp.eye(C, dtype=f32)
    diff = gcum[..., :, None] - gcum[..., None, :]
    decay = jnp.where(tril, jnp.exp(jnp.where(tril, diff, 0.0)), 0.0)
    kb = kc * bc[..., None]
    A = jnp.where(strict, jnp.einsum('nbhid,nbhjd->nbhij', kb, kc) * decay, 0.0)
    IA = A + eye
    u = lax.linalg.triangular_solve(IA, vc * bc[..., None], left_side=True, lower=True, unit_diagonal=True)
    w = lax.linalg.triangular_solve(IA, kb * jnp.exp(gcum)[..., None], left_side=True, lower=True, unit_diagonal=True)

    def step(S, inp):
        qi, ki, ui, wi, gi, di = inp
        v_new = ui - jnp.einsum('bhcd,bhde->bhce', wi, S)
        o_inter = jnp.einsum('bhcd,bhde->bhce', qi * jnp.exp(gi)[..., None], S)
        qk = jnp.einsum('bhid,bhjd->bhij', qi, ki) * di
        o = o_inter + jnp.einsum('bhij,bhje->bhie', qk, v_new)
        gl = gi[..., -1]
        S = S * jnp.exp(gl)[..., None, None] + jnp.einsum(
            'bhcd,bhce->bhde', ki * jnp.exp(gl[..., None] - gi)[..., None], v_new)
        return S, o

    S, o = lax.scan(step, S0.astype(f32), (qc, kc, u, w, gcum, decay))
    return from_chunks(o, T), S.astype(S0.dtype)


def gated_deltanet_mixer(p_b, conv_buf, S0, conv_w, A_log, dt_bias, norm_w):
    B, T, _ = p_b.shape
    f32 = jnp.float32
    qkv, beta_pre, a_pre, z = jnp.split(p_b, [B_QKV, B_QKV + B_HEADS, B_QKV + 2 * B_HEADS], axis=-1)
    qkv, new_buf = causal_depthwise_conv(qkv, conv_buf, conv_w)
    q, k, v = jnp.split(jax.nn.silu(qkv), 3, axis=-1)
    hd = lambda t: t.reshape(B, T, B_HEADS, HEAD_DIM)
    beta = jax.nn.sigmoid(beta_pre.astype(f32))
    g = -jnp.exp(A_log.astype(f32)) * jax.nn.softplus(a_pre.astype(f32) + dt_bias.astype(f32))
    o, S = gated_delta_chunked(l2_normalize(hd(q)), l2_normalize(hd(k)), hd(v), g, beta, S0)
    o = rms_norm(o, norm_w) * jax.nn.silu(hd(z)).astype(f32)
    return o.reshape(B, T, B_WIDTH).astype(p_b.dtype), new_buf, S


def hgrn2_chunked(q, k, v, logf, S0):
    T = q.shape[1]
    C = min(CHUNK, T)
    f32 = jnp.float32
    qc, kc, vc, fc = (to_chunks(pad_time(t.astype(f32), C), C) for t in (q, k, v, logf))
    bcum = jnp.cumsum(fc, axis=-2)
    tril = jnp.tril(jnp.ones((C, C), bool))[:, :, None]

    def step(S, inp):
        qi, ki, vi, bi = inp
        o_inter = jnp.einsum('bhtd,bhde->bhte', qi * jnp.exp(bi), S)
        diff = bi[:, :, :, None, :] - bi[:, :, None, :, :]
        dec = jnp.where(tril, jnp.exp(jnp.where(tril, diff, 0.0)), 0.0)
        A = jnp.einsum('bhtd,bhsd,bhtsd->bhts', qi, ki, dec)
        o = o_inter + jnp.einsum('bhts,bhse->bhte', A, vi)
        bl = bi[:, :, -1:, :]
        S = jnp.exp(bl[:, :, 0, :])[..., None] * S + jnp.einsum('bhsd,bhse->bhde', ki * jnp.exp(bl - bi), vi)
        return S, o

    S, o = lax.scan(step, S0.astype(f32), (qc, kc, vc, bcum))
    return from_chunks(o, T), S.astype(S0.dtype)


def hgrn2_mixer(p_c, S0, lb, norm_w):
    B, T, _ = p_c.shape
    f32 = jnp.float32
    q, f, i, z = jnp.split(p_c, 4, axis=-1)
    ff = f.astype(f32)
    logf = jnp.log(lb + (1.0 - lb) * jax.nn.sigmoid(ff))
    k = (1.0 - lb) * jax.nn.sigmoid(-ff)
    hd = lambda t: t.reshape(B, T, C_HEADS, HEAD_DIM)
    o, S = hgrn2_chunked(hd(jax.nn.silu(q)), hd(k), hd(i), hd(logf), S0)
    o = rms_norm(o, norm_w) * jax.nn.sigmoid(hd(z)).astype(f32)
    return o.reshape(B, T, C_WIDTH).astype(p_c.dtype), S


def trunk_layer(x, c_act, h_prev, S_a, conv_b, S_b, S_c,
                w_ada, b_ada, norm_mix_w, w_in,
                rwkv_mu, rwkv_w0, rwkv_w2, rwkv_a0, rwkv_a2, rwkv_g2, rwkv_k_k, rwkv_k_a, rwkv_r_k,
                rwkv_ln_w, rwkv_ln_b,
                gdn_conv_w, gdn_A_log, gdn_dt_bias, gdn_norm_w,
                lb, hgrn_norm_w,
                w_out, norm_ffn_w, w_ffn_in, w_ffn_out):
    mod = (c_act @ w_ada + b_ada)[:, None, :]
    shift_m, scale_m, gate_m, shift_f, scale_f, gate_f = jnp.split(mod, N_ADA, axis=-1)
    h = rms_norm(x, norm_mix_w) * (1 + scale_m) + shift_m
    p = h @ w_in
    p_a, p_b, p_c = jnp.split(p, [A_COLS, A_COLS + B_COLS], axis=-1)
    p_prev = (h_prev.astype(h.dtype) @ w_in[:, :A_COLS])[:, None, :]
    y_a, S_a = rwkv7_mixer(p_a, p_prev, S_a, rwkv_mu, rwkv_w0, rwkv_w2, rwkv_a0, rwkv_a2, rwkv_g2,
                           rwkv_k_k, rwkv_k_a, rwkv_r_k, rwkv_ln_w, rwkv_ln_b)
    y_b, conv_b, S_b = gated_deltanet_mixer(p_b, conv_b, S_b, gdn_conv_w, gdn_A_log, gdn_dt_bias, gdn_norm_w)
    y_c, S_c = hgrn2_mixer(p_c, S_c, lb, hgrn_norm_w)
    x = x + gate_m * (jnp.concatenate([y_a, y_b, y_c], axis=-1) @ w_out)
    h2 = rms_norm(x, norm_ffn_w) * (1 + scale_f) + shift_f
    gate, up = jnp.split(h2 @ w_ffn_in, 2, axis=-1)
    x = x + gate_f * ((jax.nn.silu(gate) * up) @ w_ffn_out)
    return x, (h[:, -1].astype(h_prev.dtype), S_a, conv_b.astype(h_prev.dtype), S_b, S_c)


def setup_inputs(seed: int = 0) -> dict:
    key = jax.random.key(seed)
    ks = jax.random.split(key, 36)
    f32 = jnp.float32
    L = DEPTH

    def nrm(i, shape, scale):
        return scale * jax.random.normal(ks[i], shape, f32)

    def uni(i, shape, lo, hi):
        return jax.random.uniform(ks[i], shape, f32, lo, hi)

    dt = jnp.exp(uni(27, (L, B_HEADS), math.log(1e-3), math.log(1e-1)))
    return {
        'x_prompt': nrm(0, (BATCH, SEQ, D_MODEL), 1.0),
        'x_sample': nrm(1, (DEC_BATCH, DEC_SEQ, D_MODEL), 1.0),
        'c_prompt': nrm(2, (BATCH, D_MODEL), 1.0),
        'c_sample': nrm(3, (DEC_BATCH, D_MODEL), 1.0),
        'state_rwkv_shift': nrm(4, (L, DEC_BATCH, D_MODEL), 1.0),
        'state_rwkv': nrm(5, (L, DEC_BATCH, A_HEADS, HEAD_DIM, HEAD_DIM), 0.5),
        'state_gdn_conv': nrm(6, (L, DEC_BATCH, CONV_K - 1, B_QKV), 1.0),
        'state_gdn': nrm(7, (L, DEC_BATCH, B_HEADS, HEAD_DIM, HEAD_DIM), 0.1),
        'state_hgrn': nrm(8, (L, DEC_BATCH, C_HEADS, HEAD_DIM, HEAD_DIM), 0.5),
        'w_ada': nrm(9, (L, D_MODEL, N_ADA * D_MODEL), 0.5 * D_MODEL ** -0.5),
        'b_ada': nrm(10, (L, N_ADA * D_MODEL), 0.01),
        'norm_mix_w': 1.0 + nrm(11, (L, D_MODEL), 0.05),
        'w_in': nrm(12, (L, D_MODEL, IN_COLS), D_MODEL ** -0.5),
        'rwkv_mu': uni(13, (L, A_COLS), 0.0, 1.0),
        'rwkv_w0': -1.0 + nrm(14, (L, A_WIDTH), 0.5),
        'rwkv_w2': nrm(15, (L, A_DECAY_LORA, A_WIDTH), A_DECAY_LORA ** -0.5),
        'rwkv_a0': nrm(16, (L, A_WIDTH), 0.1),
        'rwkv_a2': nrm(17, (L, A_AAA_LORA, A_WIDTH), A_AAA_LORA ** -0.5),
        'rwkv_g2': nrm(18, (L, A_GATE_LORA, A_WIDTH), A_GATE_LORA ** -0.5),
        'rwkv_k_k': 0.85 + nrm(19, (L, A_WIDTH), 0.05),
        'rwkv_k_a': 1.0 + nrm(20, (L, A_WIDTH), 0.05),
        'rwkv_r_k': nrm(21, (L, A_HEADS, HEAD_DIM), 0.1),
        'rwkv_ln_w': 1.0 + nrm(22, (L, A_WIDTH), 0.05),
        'rwkv_ln_b': nrm(23, (L, A_WIDTH), 0.01),
        'gdn_conv_w': nrm(24, (L, CONV_K, B_QKV), CONV_K ** -0.5),
        'gdn_A_log': jnp.log(uni(25, (L, B_HEADS), 1.0, 16.0)),
        'gdn_dt_bias': dt + jnp.log(-jnp.expm1(-dt)),
        'gdn_norm_w': 1.0 + nrm(26, (L, HEAD_DIM), 0.05),
        'hgrn_lb_logits': nrm(28, (L, C_WIDTH), 1.0),
        'hgrn_norm_w': 1.0 + nrm(29, (L, HEAD_DIM), 0.05),
        'w_out': nrm(30, (L, D_MIX, D_MODEL), D_MIX ** -0.5),
        'norm_ffn_w': 1.0 + nrm(31, (L, D_MODEL), 0.05),
        'w_ffn_in': nrm(32, (L, D_MODEL, 2 * FF_HIDDEN), D_MODEL ** -0.5),
        'w_ffn_out': nrm(33, (L, FF_HIDDEN, D_MODEL), FF_HIDDEN ** -0.5),
        'final_norm_w': 1.0 + nrm(34, (D_MODEL,), 0.05),
    }


def reference(x_prompt, x_sample, c_prompt, c_sample,
              state_rwkv_shift, state_rwkv, state_gdn_conv, state_gdn, state_hgrn,
              w_ada, b_ada, norm_mix_w, w_in,
              rwkv_mu, rwkv_w0, rwkv_w2, rwkv_a0, rwkv_a2, rwkv_g2, rwkv_k_k, rwkv_k_a, rwkv_r_k,
              rwkv_ln_w, rwkv_ln_b,
              gdn_conv_w, gdn_A_log, gdn_dt_bias, gdn_norm_w,
              hgrn_lb_logits, hgrn_norm_w,
              w_out, norm_ffn_w, w_ffn_in, w_ffn_out, final_norm_w):
    gam = jax.nn.softmax(hgrn_lb_logits.astype(jnp.float32), axis=0)
    lb_all = jnp.cumsum(gam, axis=0) - gam[0]

    def run_trunk(x, c, shift0, rwkv0, conv0, gdn0, hgrn0):
        c_act = jax.nn.silu(c)
        new = ([], [], [], [], [])
        for l in range(DEPTH):
            x, st = trunk_layer(
                x, c_act, shift0[l], rwkv0[l], conv0[l], gdn0[l], hgrn0[l],
                w_ada[l], b_ada[l], norm_mix_w[l], w_in[l],
                rwkv_mu[l], rwkv_w0[l], rwkv_w2[l], rwkv_a0[l], rwkv_a2[l], rwkv_g2[l],
                rwkv_k_k[l], rwkv_k_a[l], rwkv_r_k[l], rwkv_ln_w[l], rwkv_ln_b[l],
                gdn_conv_w[l], gdn_A_log[l], gdn_dt_bias[l], gdn_norm_w[l],
                lb_all[l], hgrn_norm_w[l],
                w_out[l], norm_ffn_w[l], w_ffn_in[l], w_ffn_out[l])
            for acc, s in zip(new, st):
                acc.append(s)
        y = rms_norm(x, final_norm_w)
        return y, [jnp.stack(acc) for acc in new]

    B = x_prompt.shape[0]
    y_prompt, (p_shift, p_rwkv, p_conv, p_gdn, p_hgrn) = run_trunk(
        x_prompt, c_prompt,
        jnp.zeros((DEPTH, B, D_MODEL), state_rwkv_shift.dtype),
        jnp.zeros((DEPTH, B, A_HEADS, HEAD_DIM, HEAD_DIM), state_rwkv.dtype),
        jnp.zeros((DEPTH, B, CONV_K - 1, B_QKV), state_gdn_conv.dtype),
        jnp.zeros((DEPTH, B, B_HEADS, HEAD_DIM, HEAD_DIM), state_gdn.dtype),
        jnp.zeros((DEPTH, B, C_HEADS, HEAD_DIM, HEAD_DIM), state_hgrn.dtype))
    y_sample, (s_shift, s_rwkv, s_conv, s_gdn, s_hgrn) = run_trunk(
        x_sample, c_sample, state_rwkv_shift, state_rwkv, state_gdn_conv, state_gdn, state_hgrn)
    return (y_prompt, y_sample, p_shift, p_rwkv, p_conv, p_gdn, p_hgrn, s_shift, s_rwkv, s_conv, s_gdn, s_hgrn)
```

```python
import contextlib
import numpy as np
import concourse.bass as bass
import concourse.mybir as mybir
from concourse.bass_utils import run_bass_kernel_spmd

F32 = mybir.dt.float32
BF16 = mybir.dt.bfloat16
ALU = mybir.AluOpType
AF = mybir.ActivationFunctionType

NCORES = 8
D = 1024
KC = 8
TOKP = 2048
NSEQ = 16
TSS = 4
TOKS = 64
DEPTH = 2
A_COLS = 1280
B_COLS = 1548
C_COLS = 1024
IN_COLS = 3852
FF = 2816
NJ = 22
BLK = 128
FBLK = 256
C0 = 0.6065306597126334
BIG = 1.0e4
SEM_LIMIT = 12000
SB_LO = 16512
SB_HI = 229376

DBG_AT = None
TRACE_LINES = False
FUSE_WAIT = True
DMA_SLOTS = 8
CHAIN_BANKS = {0: [0, 1], 1: [2, 3], 2: [4, 5]}
OP_LINES = {}


class Sched:
    ENGS = ('pe', 'act', 'dve', 'pool', 'sp')

    def __init__(self, nc, stack):
        self.nc = nc
        self.stack = stack
        self.prog = {e: [] for e in self.ENGS}
        self.stream_sem = {}
        self.stream_cnt = {}
        self.last_w = {}
        self.reads = {}
        self.seen = {e: {} for e in self.ENGS}
        self.snap = {}
        self.dma_rr = {}
        self.nsem = 0
        self.n_ops = 0
        self.disabled = False
        self.max_ops = None
        self.trace = None

    def _new_sem(self, stream):
        s = self.stack.enter_context(self.nc.semaphore("s%d" % self.nsem))
        self.nsem += 1
        self.stream_sem.setdefault(stream, []).append(s)
        self.stream_cnt[stream] = 0

    def _event(self, stream, inc):
        if stream not in self.stream_sem or self.stream_cnt[stream] + inc > SEM_LIMIT:
            self._new_sem(stream)
        self.stream_cnt[stream] += inc
        return (stream, len(self.stream_sem[stream]) - 1, self.stream_cnt[stream])

    @staticmethod
    def _key(a):
        return a if isinstance(a, str) else a.name

    def op(self, eng, fn, reads=(), writes=(), dma_class=None):
        if self.disabled or (self.max_ops is not None and self.n_ops >= self.max_ops):
            return None
        if dma_class is not None:
            k = self.dma_rr.get((eng, dma_class), 0)
            self.dma_rr[(eng, dma_class)] = k + 1
            stream = (eng, dma_class, k % DMA_SLOTS)
            inc = 16
        else:
            stream = eng
            inc = 1
        rk = [self._key(r) for r in reads]
        wk = [self._key(w) for w in writes]
        deps = []
        if dma_class is not None and stream in self.stream_sem and self.stream_cnt[stream] > 0:
            deps.append(((stream, len(self.stream_sem[stream]) - 1, self.stream_cnt[stream]), 'slot'))
        for k in rk:
            if k in self.last_w:
                deps.append((self.last_w[k], 'raw'))
        for k in wk:
            if k in self.last_w:
                deps.append((self.last_w[k], 'waw'))
            for ev in self.reads.get(k, ()):
                deps.append((ev, 'war'))
        waits = []
        seen = self.seen[eng]
        for ev, kind in deps:
            st, ep, val = ev
            if st == eng and dma_class is None:
                if eng == 'pe' or kind == 'war':
                    continue
            sk = (st, ep)
            if seen.get(sk, 0) >= val:
                continue
            seen[sk] = val
            waits.append((self.stream_sem[st][ep], val))
            snap = self.snap.get(ev)
            if snap:
                for k2, v2 in snap.items():
                    if seen.get(k2, 0) < v2:
                        seen[k2] = v2
        ev = self._event(stream, inc)
        sem = self.stream_sem[stream][ev[1]]
        self.snap[ev] = dict(seen)
        if TRACE_LINES:
            OP_LINES.setdefault(eng, []).append(getattr(fn, '_ln', None))
        if self.trace is not None:
            import sys as _sys
            self.trace.append((self.n_ops, eng, _sys._getframe(2).f_lineno))
        self.prog[eng].append((waits, fn, sem, inc))
        for k in wk:
            self.last_w[k] = ev
            self.reads[k] = []
        for k in rk:
            if k not in wk:
                self.reads.setdefault(k, []).append(ev)
        self.n_ops += 1
        return ev

    def barrier(self):
        evs = []
        for stream, sems in self.stream_sem.items():
            ep = len(sems) - 1
            if self.stream_cnt[stream] > 0:
                evs.append((stream, ep, self.stream_cnt[stream]))
        for e in self.ENGS:
            waits = []
            for st, ep, val in evs:
                if st == e:
                    continue
                sk = (st, ep)
                if self.seen[e].get(sk, 0) >= val:
                    continue
                self.seen[e][sk] = val
                waits.append((self.stream_sem[st][ep], val))
            if waits:
                self.prog[e].append((waits, None, None, 0))

    def emit(self, block):
        prog = self.prog

        def run(e, engobj):
            for waits, fn, sem, inc in prog[e]:
                if fn is None:
                    for s, v in waits:
                        engobj.wait_ge(s, v)
                    continue
                fuse = FUSE_WAIT and inc == 1 and len(waits) > 0
                for s, v in (waits[:-1] if fuse else waits):
                    engobj.wait_ge(s, v)
                ins = fn(engobj)
                if fuse:
                    ins._wait_ge(waits[-1][0], waits[-1][1])
                ins.then_inc(sem, inc)

        @block.tensor
        def _(eng):
            run('pe', eng)

        @block.scalar
        def _(eng):
            run('act', eng)

        @block.vector
        def _(eng):
            run('dve', eng)

        @block.gpsimd
        def _(eng):
            run('pool', eng)

        @block.sync
        def _(eng):
            run('sp', eng)


def _dtsize(dt):
    return 2 if dt == BF16 else 4


class KB:
    def __init__(self, nc, stack):
        self.nc = nc
        self.S = Sched(nc, stack)
        self.top = SB_LO
        self.uid = 0
        self.banks = [stack.enter_context(nc.psum_tensor("pb%d" % i, [128, 512], F32)) for i in range(8)]
        self.free_banks = list(range(8))
        self.chain = None
        self.chains = {}
        self.chain_bank_cnt = {}

    def emit(self, eng, fn, reads=(), writes=(), dma_class=None):
        if TRACE_LINES:
            import sys as _sys
            fn._ln = _sys._getframe(2).f_lineno
        if self.chain is None:
            self.S.op(eng, fn, reads=reads, writes=writes, dma_class=dma_class)
        else:
            self.chains[self.chain].append((eng, fn, list(reads), list(writes), dma_class))

    @contextlib.contextmanager
    def in_chain(self, k):
        assert self.chain is None
        self.chain = k
        self.chains.setdefault(k, [])
        try:
            yield
        finally:
            self.chain = None

    def flush_chains(self):
        lists = [self.chains[k] for k in sorted(self.chains)]
        self.chains = {}
        pos = [0] * len(lists)
        remaining = sum(len(x) for x in lists)
        while remaining:
            for i, lst in enumerate(lists):
                if pos[i] < len(lst):
                    eng, fn, reads, writes, dc = lst[pos[i]]
                    pos[i] += 1
                    remaining -= 1
                    self.S.op(eng, fn, reads=reads, writes=writes, dma_class=dc)

    def sb(self, name, shape, dt=F32):
        size = int(np.prod(shape[1:])) * _dtsize(dt)
        size = (size + 31) // 32 * 32
        off = self.top
        self.top += size
        assert self.top <= SB_HI, "SBUF overflow at %s: %d" % (name, self.top)
        self.uid += 1
        return self.nc.alloc_sbuf_tensor_at("%s_%d" % (name, self.uid), list(shape), dt, offset=off)

    def mark(self):
        return self.top

    def release(self, m):
        self.S.barrier()
        self.top = m

    @contextlib.contextmanager
    def bank(self):
        if self.chain is not None:
            c = self.chain_bank_cnt.get(self.chain, 0)
            self.chain_bank_cnt[self.chain] = c + 1
            bl = CHAIN_BANKS[self.chain]
            yield self.banks[bl[c % len(bl)]]
            return
        idx = self.free_banks.pop(0)
        try:
            yield self.banks[idx]
        finally:
            self.free_banks.append(idx)

    def mm(self, out, lhsT, rhs, start=True, stop=True):
        self.emit('pe', lambda e: e.matmul(out, lhsT=lhsT, rhs=rhs, start=start, stop=stop),
                  reads=[lhsT, rhs], writes=[out])

    def tr(self, out, in_, ident):
        self.emit('pe', lambda e: e.transpose(out, in_, ident), reads=[in_, ident], writes=[out])

    def act(self, out, in_, func, bias=None, scale=None):
        reads = [in_]
        kw = {}
        if bias is not None:
            kw['bias'] = bias
            if not isinstance(bias, (int, float)):
                reads.append(bias)
        if scale is not None:
            kw['scale'] = scale
            if not isinstance(scale, (int, float)):
                reads.append(scale)
        self.emit('act', lambda e: e.activation(out, in_, func, **kw), reads=reads, writes=[out])

    def cp(self, out, in_, eng='dve'):
        if eng == 'act':
            self.emit('act', lambda e: e.copy(out, in_), reads=[in_], writes=[out])
        else:
            self.emit(eng, lambda e: e.tensor_copy(out, in_), reads=[in_], writes=[out])

    def tt(self, out, a, b, op, eng='dve'):
        self.emit(eng, lambda e: e.tensor_tensor(out, a, b, op), reads=[a, b], writes=[out])

    def ts(self, out, a, s1, op0, s2=None, op1=None, eng='dve'):
        reads = [a] + [s for s in (s1, s2) if s is not None and not isinstance(s, (int, float))]
        if op1 is None:
            self.emit(eng, lambda e: e.tensor_scalar(out, a, s1, None, op0), reads=reads, writes=[out])
        else:
            self.emit(eng, lambda e: e.tensor_scalar(out, a, s1, s2, op0, op1), reads=reads, writes=[out])

    def stt(self, out, in0, scalar, in1, op0, op1):
        reads = [in0, in1] + ([] if isinstance(scalar, (int, float)) else [scalar])
        self.emit('dve', lambda e: e.scalar_tensor_tensor(out, in0, scalar, in1, op0, op1),
                  reads=reads, writes=[out])

    def scan(self, out, d0, d1):
        self.emit('dve', lambda e: e.tensor_tensor_scan(out, d0, d1, 0.0, ALU.mult, ALU.add),
                  reads=[d0, d1], writes=[out])

    def recip(self, out, in_):
        self.emit('dve', lambda e: e.reciprocal(out, in_), reads=[in_], writes=[out])

    def memset(self, out, val, eng='dve'):
        self.emit(eng, lambda e: e.memset(out, val), writes=[out])

    def dma(self, out, in_, eng='sp', cls='io', rd=None, wr=None):
        self.emit(eng, lambda e: e.dma_start(out=out, in_=in_),
                  reads=[in_] if rd is None else rd, writes=[out] if wr is None else wr, dma_class=cls)


def bc(ap, shape):
    return ap.to_broadcast(list(shape))


def _fm(v, nch):
    return np.ascontiguousarray(np.asarray(v, np.float32).reshape(nch, 128).T)


PV_LAYER = [('nmw', 8), ('nfw', 8), ('bada', 48), ('mu', 10), ('w0', 3), ('a0', 3), ('kk', 3), ('ka', 3),
            ('rk', 3), ('lnw', 3), ('lnb', 3), ('convw', 36), ('alog', 1), ('dtb', 1), ('gnw', 1),
            ('lbl', 2), ('hnw', 1), ('lora', 384)]
PV_OFF = {}
_o = 0
for _l in range(DEPTH):
    for _n, _w in PV_LAYER:
        PV_OFF[(_n, _l)] = (_o, _w)
        _o += _w
PV_OFF[('fnw', 0)] = (_o, 8)
_o += 8
NPV = _o


def pack_pvec(inp):
    pv = np.zeros((128, NPV), np.float32)

    def put(name, l, arr):
        o, w = PV_OFF[(name, l)]
        assert arr.shape == (128, w), (name, arr.shape, w)
        pv[:, o:o + w] = arr

    for l in range(DEPTH):
        put('nmw', l, _fm(inp['norm_mix_w'][l], 8))
        put('nfw', l, _fm(inp['norm_ffn_w'][l], 8))
        put('bada', l, _fm(inp['b_ada'][l], 48))
        put('mu', l, _fm(inp['rwkv_mu'][l], 10))
        put('w0', l, _fm(inp['rwkv_w0'][l], 3))
        put('a0', l, _fm(inp['rwkv_a0'][l], 3))
        put('kk', l, _fm(inp['rwkv_k_k'][l], 3))
        put('ka', l, _fm(inp['rwkv_k_a'][l], 3))
        put('rk', l, _fm(np.asarray(inp['rwkv_r_k'][l]).reshape(384), 3))
        put('lnw', l, _fm(inp['rwkv_ln_w'][l], 3))
        put('lnb', l, _fm(inp['rwkv_ln_b'][l], 3))
        cw = np.asarray(inp['gdn_conv_w'][l], np.float32).reshape(4, 9, 128)
        put('convw', l, np.ascontiguousarray(cw.transpose(2, 0, 1)).reshape(128, 36))
        t = np.zeros((128, 1), np.float32)
        t[0:6, 0] = inp['gdn_A_log'][l]
        put('alog', l, t)
        t = np.zeros((128, 1), np.float32)
        t[0:6, 0] = inp['gdn_dt_bias'][l]
        put('dtb', l, t)
        put('gnw', l, np.tile(np.asarray(inp['gdn_norm_w'][l], np.float32), 2).reshape(128, 1))
        put('lbl', l, _fm(inp['hgrn_lb_logits'][l], 2))
        put('hnw', l, np.tile(np.asarray(inp['hgrn_norm_w'][l], np.float32), 2).reshape(128, 1))
        lo = np.zeros((128, 384), np.float32)
        lo[0:32] = inp['rwkv_w2'][l]
        lo[32:64] = inp['rwkv_a2'][l]
        lo[64:128] = inp['rwkv_g2'][l]
        put('lora', l, lo)
    put('fnw', 0, _fm(inp['final_norm_w'], 8))
    return pv


CST_ITEMS = [('ident', 128), ('bones', 128), ('ones', 128), ('ident2', 64),
             ('mask5_p', 320), ('IU_p', 64), ('BSL_p', 64), ('BIU_p', 64), ('BS_p', 64),
             ('mask5_s', 320), ('IU_s', 64), ('BSL_s', 64), ('BIU_s', 64), ('BS_s', 64),
             ('ind', 16), ('osel', 6 * 64), ('selb', 3 * 128), ('rm_p', 384), ('rm_s', 192)]
CST_OFF = {}
_o = 0
for _n, _w in CST_ITEMS:
    CST_OFF[_n] = (_o, _w)
    _o += _w
NCST = _o


def make_consts():
    c = np.zeros((128, NCST), np.float32)

    def put(name, arr):
        o, w = CST_OFF[name]
        assert arr.shape[1] == w, (name, arr.shape, w)
        c[0:arr.shape[0], o:o + w] = arr

    def rep(a):
        return np.concatenate([a, a], axis=0)

    put('ident', np.eye(128, dtype=np.float32))
    p = np.arange(128)
    put('bones', (p[:, None] // 64 == p[None, :] // 64).astype(np.float32))
    put('ones', np.ones((128, 128), np.float32))
    put('ident2', rep(np.eye(64, dtype=np.float32)))
    i = np.arange(64)
    for v, seg in (('p', 64), ('s', 4)):
        same = (i[:, None] // seg == i[None, :] // seg)
        SU = ((i[:, None] < i[None, :]) & same).astype(np.float32)
        IU = ((i[:, None] <= i[None, :]) & same).astype(np.float32)
        SL = ((i[:, None] > i[None, :]) & same).astype(np.float32)
        put('mask5_' + v, rep(np.concatenate([SU, SU, IU, IU, SL], axis=1)))
        put('IU_' + v, rep(IU))
        put('BSL_' + v, rep(BIG * (1.0 - SL)))
        put('BIU_' + v, rep(BIG * (1.0 - IU)))
        put('BS_' + v, rep(same.astype(np.float32)))
    put('ind', rep((i[:, None] // 4 == np.arange(16)[None, :]).astype(np.float32)))
    osel = np.zeros((6, 6, 64), np.float32)
    for hh in range(6):
        osel[hh, hh, :] = 1.0
    put('osel', osel.reshape(6, 6 * 64))
    selb = np.zeros((6, 3, 128), np.float32)
    for hp in range(3):
        for m in range(128):
            selb[2 * hp + m // 64, hp, m] = 1.0
    put('selb', selb.reshape(6, 3 * 128))
    col = np.arange(384)
    put('rm_p', np.tile((col % 64 != 0).astype(np.float32)[None, :], (128, 1)))
    col = np.arange(192)
    put('rm_s', np.tile((col % 4 != 0).astype(np.float32)[None, :], (128, 1)))
    return c


class Blk:
    def __init__(self, kind, idx, n=None):
        self.kind = kind
        self.idx = idx
        if kind == 'p':
            self.n = BLK if n is None else n
            self.nsq = 1
            self.tb = self.n
            self.ntile = self.n // 64
            self.nseg = 1
            self.tok0 = idx * self.n
            self.w = 3 + self.n
            self.v = 'p'
            self.levels = 5
        else:
            self.n = TOKS
            self.nsq = NSEQ
            self.tb = TSS
            self.ntile = 1
            self.nseg = NSEQ
            self.tok0 = TOKP
            self.w = NSEQ * 7
            self.v = 's'
            self.levels = 1
        self.seglen = 64 // self.nseg


def build_program(dbg_names=(), stop_after=None):
    nc = bass.Bass("TRN2", target_bir_lowering=False)
    dr = {}

    def din(name, shape):
        dr[name] = nc.dram_tensor(name, list(shape), F32, kind="ExternalInput").ap()

    def dout(name, shape):
        dr[name] = nc.dram_tensor(name, list(shape), F32, kind="ExternalOutput").ap()

    din('xp', [TOKP, D]); din('xs', [TOKS, D]); din('c17', [17, D])
    din('shift_s', [DEPTH, NSEQ, D]); din('srw', [DEPTH, NSEQ, 6, 64, 64]); din('sconv', [DEPTH, NSEQ, 3, 1152])
    din('sgdn', [DEPTH, NSEQ, 6, 64, 64]); din('shg', [DEPTH, NSEQ, 4, 64, 64])
    din('w_ada', [DEPTH, D, 6 * D]); din('w_in', [DEPTH, D, IN_COLS]); din('w_out', [DEPTH, D, D])
    din('w_fi', [DEPTH, D, 2 * FF]); din('w_fo', [DEPTH, FF, D])
    din('pvec', [128, NPV]); din('cst', [128, NCST])
    dout('yp', [TOKP, D]); dout('ys', [TOKS, D])
    dout('o_pshift', [DEPTH, 1, D]); dout('o_prw', [DEPTH, 1, 6, 64, 64]); dout('o_pconv', [DEPTH, 1, 3, 1152])
    dout('o_pgdn', [DEPTH, 1, 6, 64, 64]); dout('o_phg', [DEPTH, 1, 4, 64, 64])
    dout('o_sshift', [DEPTH, NSEQ, D]); dout('o_srw', [DEPTH, NSEQ, 6, 64, 64]); dout('o_sconv', [DEPTH, NSEQ, 3, 1152])
    dout('o_sgdn', [DEPTH, NSEQ, 6, 64, 64]); dout('o_shg', [DEPTH, NSEQ, 4, 64, 64])
    xscr = nc.dram_tensor("xscr", [128, KC, TOKP + TOKS], F32, kind="Internal").ap()
    dbg_list = []
    dbg_t = {}
    for dn, dshape in dbg_names:
        dbg_t[dn] = nc.dram_tensor("dbg_" + dn, list(dshape), F32, kind="ExternalOutput").ap()
    out_keys = []

    with contextlib.ExitStack() as stack:
        B = KB(nc, stack)
        S = B.S
        okc = [0]

        def odma(dst, src):
            okc[0] += 1
            key = "out#%d" % okc[0]
            B.dma(dst, src, cls='out', wr=[key])

        if stop_after is not None and stop_after.startswith('#'):
            S.max_ops = int(stop_after[1:])

        def stage(name):
            if stop_after is not None and stop_after == name:
                S.disabled = True

        def dbg(name, ap, shape=None):
            if name in dbg_t and name not in dbg_list:
                dbg_list.append(name)
                odma(dbg_t[name], ap)

        cst = B.sb("cst", [128, NCST])
        pvec = B.sb("pvec", [128, NPV])
        modT = B.sb("modT", [128, DEPTH, 48, 17])
        AMt = B.sb("AMt", [128, DEPTH, 8, 17])
        AFt = B.sb("AFt", [128, DEPTH, 8, 17])
        LBt = B.sb("LBt", [128, DEPTH, 2])
        OMLt = B.sb("OMLt", [128, DEPTH, 2])
        B.dma(cst[:], dr['cst'])
        B.dma(pvec[:], dr['pvec'])

        def C(name, rows=128, c0=0, c1=None):
            o, w = CST_OFF[name]
            c1 = w if c1 is None else c1
            return cst[0:rows, o + c0:o + c1]

        def PV(name, l, rows=128, c0=0, c1=None):
            o, w = PV_OFF[(name, l)]
            c1 = w if c1 is None else c1
            return pvec[0:rows, o + c0:o + c1]

        ident = C('ident')
        ones = C('ones')
        bones = C('bones')
        evac_rr = [0]

        def evac(out, in_):
            evac_rr[0] += 1
            B.cp(out, in_, eng=('dve' if evac_rr[0] % 3 == 0 else 'act'))

        def rsqrt_ln(out, in_, scale=1.0, bias=0.0):
            B.act(out, in_, AF.Ln, bias=bias, scale=scale)
            B.act(out, out, AF.Exp, scale=-0.5)

        def sig_finish(t, eng='dve'):
            B.ts(t, t, 1.0, ALU.add, eng=eng)
            B.recip(t, t)

        def silu_exp(out, x, tmp):
            B.act(tmp, x, AF.Exp, scale=-1.0)
            sig_finish(tmp)
            B.tt(out, x, tmp, ALU.mult)

        m0 = B.mark()
        c17t = B.sb("c17t", [17, D])
        cact = B.sb("cact", [17, D])
        cT = B.sb("cT", [128, KC, 17])
        wada = [B.sb("wada%d" % i, [128, KC, 512]) for i in range(4)]
        msm = [B.sb("msm%d" % i, [17, 512]) for i in range(2)]
        B.dma(c17t[:], dr['c17'])
        B.act(cact[:], c17t[:], AF.Silu)
        with B.bank() as pb:
            for kc in range(KC):
                B.tr(pb[:, kc * 17:(kc + 1) * 17], cact[0:17, kc * 128:(kc + 1) * 128], ident[0:17, 0:17])
            B.cp(cT[:].rearrange("p k s -> p (k s)"), pb[:, 0:KC * 17])
        stage('p1')
        nblk = 0
        for l in range(DEPTH):
            wsrc = dr['w_ada'][l].rearrange("(kc p) n -> p kc n", p=128)
            for cb in range(12):
                wt = wada[nblk % 4]
                ms = msm[nblk % 2]
                B.dma(wt[:], wsrc[:, :, cb * 512:(cb + 1) * 512], cls='w')
                nblk += 1
                with B.bank() as pb:
                    for kc in range(KC):
                        B.mm(pb[0:17, :], cT[:, kc, :], wt[:, kc, :], start=(kc == 0), stop=(kc == KC - 1))
                    B.cp(ms[:], pb[0:17, :], eng='act')
                with B.bank() as pb:
                    for q in range(4):
                        B.tr(pb[:, q * 17:(q + 1) * 17], ms[0:17, q * 128:(q + 1) * 128], ident[0:17, 0:17])
                    B.tt(modT[:, l, 4 * cb:4 * cb + 4, :], pb[:, 0:68].rearrange("p (q s) -> p q s", s=17),
                         bc(PV('bada', l, c0=4 * cb, c1=4 * cb + 4).unsqueeze(2), [128, 4, 17]), ALU.add)
                stage('p2_%d_%d' % (l, cb))
            B.ts(AMt[:, l], modT[:, l, 8:16, :], 1.0, ALU.add)
            B.tt(AMt[:, l], AMt[:, l], bc(PV('nmw', l).unsqueeze(2), [128, 8, 17]), ALU.mult)
            B.ts(AFt[:, l], modT[:, l, 32:40, :], 1.0, ALU.add)
            B.tt(AFt[:, l], AFt[:, l], bc(PV('nfw', l).unsqueeze(2), [128, 8, 17]), ALU.mult)
        B.memset(LBt[:], 0.0)
        B.tt(LBt[:, 1, :], PV('lbl', 1), PV('lbl', 0), ALU.subtract)
        B.act(LBt[:, 1, :], LBt[:, 1, :], AF.Sigmoid)
        B.ts(OMLt[:], LBt[:], -1.0, ALU.mult, 1.0, ALU.add)
        dbg('modT', modT[:].rearrange("p l m s -> p (l m s)"), [128, DEPTH * 48 * 17])
        B.release(m0)
        stage('prologue')

        def MOD(l, which):
            if which == 'AM':
                return AMt[:, l]
            if which == 'AF':
                return AFt[:, l]
            lo = {'BM': 0, 'GM': 16, 'BF': 24, 'GF': 40}[which]
            return modT[:, l, lo:lo + 8, :]

        def seqcols(blk):
            return (0, 1) if blk.kind == 'p' else (1, 17)

        def v3(ap, blk):
            return ap.rearrange("p (s t) -> p s t", t=blk.tb)

        def rms_stats(xb, n, sq, rstd):
            with B.bank() as pb:
                for kc in range(KC):
                    B.act(sq[kc % 2][:, 0:n], xb[:, kc, 0:n], AF.Square)
                    B.mm(pb[:, 0:n], ones, sq[kc % 2][:, 0:n], start=(kc == 0), stop=(kc == KC - 1))
                rsqrt_ln(rstd[:, 0:n], pb[:, 0:n], scale=1.0 / D, bias=1e-6)

        def norm_mod(blk, xb, n, l, Aw, Bw, hdst_fn, sq, rstd, tmpn):
            rms_stats(xb, n, sq, rstd)
            s0, s1 = seqcols(blk)
            A = MOD(l, Aw)
            Bm = MOD(l, Bw)
            for kc in range(KC):
                tn = tmpn[kc % 2]
                B.tt(tn[:, 0:n], xb[:, kc, 0:n], rstd[:, 0:n], ALU.mult, eng='pool')
                if blk.kind == 'p':
                    B.act(hdst_fn(kc), tn[:, 0:n].rearrange("p (s t) -> p s t", s=1), AF.Identity,
                          bias=Bm[:, kc, s0:s1], scale=A[:, kc, s0:s1])
                else:
                    t3 = v3(tn[:, 0:n], blk)
                    B.tt(t3, t3, bc(A[:, kc, s0:s1].unsqueeze(2), [128, blk.nsq, blk.tb]), ALU.mult)
                    B.tt(hdst_fn(kc), t3, bc(Bm[:, kc, s0:s1].unsqueeze(2), [128, blk.nsq, blk.tb]), ALU.add)

        def gate_res(blk, xdst, pbsrc, l, Gw, kc, n, tmpn):
            s0, s1 = seqcols(blk)
            G = MOD(l, Gw)
            if blk.kind == 'p':
                B.stt(xdst, pbsrc, G[:, kc, s0:s1], xdst, ALU.mult, ALU.add)
            else:
                tn = tmpn[kc % 2]
                t3 = v3(tn[:, 0:n], blk)
                B.tt(t3, v3(pbsrc, blk), bc(G[:, kc, s0:s1].unsqueeze(2), [128, blk.nsq, blk.tb]), ALU.mult)
                B.tt(xdst, xdst, tn[:, 0:n], ALU.add)

        prompt_blocks = [Blk('p', i) for i in range(TOKP // BLK)]
        sample_block = Blk('s', 0)
        ffn_blocks = [Blk('p', i, n=FBLK) for i in range(TOKP // FBLK)] + [Blk('s', 0)]

        def xkey(phase, lay, tok0, n):
            return ["xscr#%d" % g for g in range(tok0 // 64, (tok0 + n + 63) // 64)]

        hsl = [slice(0, 64), slice(64, 128)]

        for l in range(DEPTH):
            mph = B.mark()
            WIN = [B.sb("w_inA", [128, KC, A_COLS], BF16), B.sb("w_inB", [128, KC, B_COLS], BF16),
                   B.sb("w_inC", [128, KC, C_COLS], BF16)]
            WIN_OFF = [0, A_COLS, A_COLS + B_COLS, IN_COLS]
            w_out_sb = B.sb("w_out", [128, KC, D], BF16)
            wsrc = dr['w_in'][l].rearrange("(kc p) n -> p kc n", p=128)
            for gi in range(3):
                for kc in range(0, KC, 4):
                    B.dma(WIN[gi][:, kc:kc + 4, :], wsrc[:, kc:kc + 4, WIN_OFF[gi]:WIN_OFF[gi + 1]], eng='pool', cls='w')

            def w_in_ap(kc, c0, m):
                gi = 0 if c0 < WIN_OFF[1] else (1 if c0 < WIN_OFF[2] else 2)
                assert c0 + m <= WIN_OFF[gi + 1]
                return WIN[gi][:, kc, c0 - WIN_OFF[gi]:c0 - WIN_OFF[gi] + m]
            wsrc = dr['w_out'][l].rearrange("(kc p) n -> p kc n", p=128)
            for kc in range(0, KC, 4):
                B.dma(w_out_sb[:, kc:kc + 4, :], wsrc[:, kc:kc + 4, :], eng='pool', cls='w')

            xb = B.sb("xb", [128, KC, BLK])
            STAGE = B.sb("STAGE", [128, 1152])
            xtm = STAGE[:, 0:D]
            sq = [B.sb("sq%d" % i, [128, BLK]) for i in range(2)]
            rstd = B.sb("rstd", [128, BLK])
            tmpn = [B.sb("tmpn%d" % i, [128, BLK]) for i in range(2)]
            hT_p = B.sb("hT_p", [128, KC, 8 + BLK], BF16)
            hT_s = B.sb("hT_s", [128, KC, NSEQ * 5], BF16)
            PB = B.sb("PB", [128, 10, 3 + BLK])
            ZB = B.sb("ZB", [128, 3, BLK])
            BETA = B.sb("BETA", [6, BLK])
            APRE = B.sb("APRE", [6, BLK])
            yT = B.sb("yT", [128, 8, BLK], BF16)
            SL3 = [B.sb("slot%d" % i, [128, 3, BLK]) for i in range(12)]
            XS = B.sb("XS", [128, 10, BLK])
            TL = B.sb("TL", [128, BLK])
            class _TS:
                pass
            NT = BLK // 64
            TSets = []
            for ci in range(3):
                t_ = _TS()
                t_.TM = B.sb("TM", [128, NT, 4, 64])
                t_.A4 = B.sb("A4", [128, NT, 4, 64])
                t_.NM = B.sb("NM", [128, NT, 64])
                t_.Pm = B.sb("Pm", [128, NT, 64])
                t_.NQ = [B.sb("NQ%d" % i, [128, NT, 2, 64]) for i in range(2)]
                t_.WTs = B.sb("WTs", [128, NT, 64])
                t_.Xs = B.sb("Xs", [128, NT, 64])
                t_.U0s = B.sb("U0s", [128, NT, 64])
                t_.U0Ts = t_.U0s
                t_.Us = B.sb("Us", [128, 64])
                t_.UTs = B.sb("UTs", [128, 64])
                t_.SCP = B.sb("SCP", [128, NT, 4])
                t_.SCD = B.sb("SCD", [128, NT, 4])
                t_.GSL = B.sb("GSL", [128, NT, 64])
                t_.GUI = B.sb("GUI", [128, NT, 64])
                t_.Qs = B.sb("Qs", [128, NT, 64])
                t_.KB3 = B.sb("KB3", [128, NT, 3, 64])
                t_.EBC = B.sb("EBC", [128, NT, 64])
                t_.QTg = B.sb("QTg", [128, NT, 64])
                TSets.append(t_)
            TB = _TS()

            def use_set(ci):
                TB.__dict__.update(TSets[ci].__dict__)
            use_set(0)
            SVM = B.sb("SVM", [128, 32, 64])
            UM = SVM[:, 0:16, :]
            VM = SVM[:, 16:32, :]
            Mp = {}
            for mx, npair in (('a', 3), ('b', 3), ('c', 2)):
                for hp in range(npair):
                    Mp[(mx, hp)] = B.sb("Mp_%s%d" % (mx, hp), [128, 1, 64])
            Ms = B.sb("Ms", [128, NSEQ, 64])
            STG = SVM[0:64, :, :].rearrange("p (s h) k -> p s h k", h=2)
            OSH = xtm[0:16, :]
            OCV = STAGE[0:48, :]
            G6 = B.sb("G6", [6, BLK])
            GC6 = B.sb("GC6", [6, BLK])
            BE6 = B.sb("BE6", [6, BLK])
            EA6 = B.sb("EA6", [6, 1])

            NEGB = B.sb("NEGB", [128, 6])
            B.ts(NEGB[:, 0:3], PV('w0', l), -1.0, ALU.mult)
            B.ts(NEGB[:, 3:6], PV('a0', l), -1.0, ALU.mult)
            B.memset(TL[:], 0.0, eng='pool')
            B.memset(hT_p[:], 0.0)
            for key in Mp:
                B.memset(Mp[key][:], 0.0, eng='pool')

            def load_x_block(blk):
                n = blk.n
                if l == 0:
                    if blk.kind == 'p':
                        for ti in range(n // 128):
                            B.dma(xtm[:], dr['xp'][blk.tok0 + ti * 128: blk.tok0 + (ti + 1) * 128, :])
                            for g in range(2):
                                with B.bank() as pb:
                                    for q in range(4):
                                        kc = 4 * g + q
                                        B.tr(pb[:, q * 128:(q + 1) * 128], xtm[:, kc * 128:(kc + 1) * 128], ident)
                                    B.cp(xb[:, 4 * g:4 * g + 4, ti * 128:(ti + 1) * 128],
                                         pb[:, 0:512].rearrange("p (q t) -> p q t", q=4), eng=('act' if g else 'dve'))
                    else:
                        B.dma(xtm[0:64, :], dr['xs'])
                        with B.bank() as pb:
                            for kc in range(KC):
                                B.tr(pb[:, kc * 64:(kc + 1) * 64], xtm[0:64, kc * 128:(kc + 1) * 128], ident[0:64, 0:64])
                            B.cp(xb[:, :, 0:64], pb[:, 0:512].rearrange("p (q t) -> p q t", q=8))
                else:
                    B.dma(xb[:, :, 0:n], xscr[:, :, blk.tok0:blk.tok0 + n], rd=xkey(0, l, blk.tok0, n))

            def PBv(blk, j0, j1, off):
                if blk.kind == 'p':
                    return PB[:, j0:j1, off:off + blk.tb].unsqueeze(2)
                return PB[:, j0:j1, 0:blk.w].rearrange("p j (s u) -> p j s u", u=7)[:, :, :, off:off + blk.tb]

            def c4(ap, blk):
                return ap.rearrange("p j (s t) -> p j s t", t=blk.tb)

            def proj(blk, cols, dst_fn):
                c0, m = cols
                with B.bank() as pb:
                    src = hT_p if blk.kind == 'p' else hT_s
                    N = 4 + blk.n if blk.kind == 'p' else NSEQ * 5
                    for kc in range(KC):
                        B.mm(pb[0:m, 0:N], w_in_ap(kc, c0, m), src[:, kc, 0:N],
                             start=(kc == 0), stop=(kc == KC - 1))
                    dst_fn(pb)

            def proj_to_PB(blk, cols, j, halo):
                def f(pb):
                    m = cols[1]
                    if blk.kind == 'p':
                        evac(PB[0:m, j, 3 - halo:3 + blk.n], pb[0:m, 3 - halo:3 + blk.n])
                    else:
                        h = 1 if halo == 1 else 0
                        src = pb[0:m, 0:NSEQ * 5].rearrange("p (s u) -> p s u", u=5)[:, :, 1 - h:5]
                        dst = PB[0:m, j, 0:blk.w].rearrange("p (s u) -> p s u", u=7)[:, :, 3 - h:7]
                        evac(dst, src)
                proj(blk, cols, f)

            def proj_to(blk, cols, dst):
                def f(pb):
                    m = cols[1]
                    if blk.kind == 'p':
                        evac(dst, pb[0:m, 3:3 + blk.n])
                    else:
                        src = pb[0:m, 0:NSEQ * 5].rearrange("p (s u) -> p s u", u=5)[:, :, 1:5]
                        evac(dst.rearrange("p (s t) -> p s t", t=4), src)
                proj(blk, cols, f)

            def wy_double(blk, N0, Q0):
                nt = blk.ntile
                B.tt(TB.Pm[:, 0:nt, :], Q0, bc(C('ident2').unsqueeze(1), [128, nt, 64]), ALU.add)
                Ncur = N0
                Qcur = Q0
                for lev in range(blk.levels):
                    last = (lev == blk.levels - 1)
                    nq = TB.NQ[lev % 2]
                    with B.bank() as pb:
                        pv = pb[:, 0:nt * 128].rearrange("p (n a t) -> p n a t", n=nt, a=2)
                        for ti in range(nt):
                            for h2 in range(2):
                                hs = hsl[h2]
                                B.mm(pv[hs, ti, 0, :], Qcur[hs, ti, :], Ncur[hs, ti, :])
                                if not last:
                                    B.mm(pv[hs, ti, 1, :], Ncur[hs, ti, :], Qcur[hs, ti, :])
                        if last:
                            evac(nq[:, 0:nt, 0, :], pv[:, :, 0, :])
                        else:
                            evac(nq[:, 0:nt], pv)
                    with B.bank() as pb:
                        pv = pb[:, 0:nt * 64].rearrange("p (n t) -> p n t", n=nt)
                        for ti in range(nt):
                            for h2 in range(2):
                                hs = hsl[h2]
                                B.mm(pv[hs, ti, :], nq[hs, ti, 0, :], TB.Pm[hs, ti, :])
                        B.tt(TB.Pm[:, 0:nt, :], TB.Pm[:, 0:nt, :], pv, ALU.add)
                    Ncur = nq[:, 0:nt, 0, :]
                    Qcur = nq[:, 0:nt, 1, :]

            def seq_step(blk, M, WT, U0, U0T, ArbT, y_extra, rt_fn, b_tm, upd_extra, DC_fn, mode, ydst,
                         Vmask=None):
                nseg = blk.nseg
                sl = blk.seglen
                if WT is not None:
                    if blk.kind == 'p':
                        with B.bank() as pb:
                            for h2 in range(2):
                                hs = hsl[h2]
                                B.mm(pb[hs, 0:64], WT[hs, :], M[hs, 0, :])
                            B.tt(TB.Us[:], pb[:, 0:64], U0[:], ALU.add)
                    else:
                        with B.bank() as pb:
                            for h2 in range(2):
                                hs = hsl[h2]
                                for sg in range(nseg):
                                    B.mm(pb[hs, sg * sl:(sg + 1) * sl], M[hs, sg, :], WT[hs, sg * sl:(sg + 1) * sl])
                            B.tt(TB.UTs[:], pb[:, 0:64], U0T[:], ALU.add)
                        with B.bank() as pb:
                            for h2 in range(2):
                                hs = hsl[h2]
                                B.mm(pb[hs, 0:64], TB.UTs[hs, :], ident[hs, hs])
                            evac(TB.Us[:], pb[:, 0:64])
                with B.bank() as pb:
                    for h2 in range(2):
                        hs = hsl[h2]
                        first = True
                        if WT is not None:
                            B.mm(pb[hs, 0:64], TB.Us[hs, :], ArbT[hs, :], start=True, stop=False)
                            first = False
                        if y_extra is not None:
                            B.mm(pb[hs, 0:64], y_extra[0][hs, :], y_extra[1][hs, :], start=first, stop=False)
                            first = False
                        for sg in range(nseg):
                            cc = slice(sg * sl, (sg + 1) * sl)
                            B.mm(pb[hs, cc], M[hs, sg, :], rt_fn(cc)[hs, :], start=False, stop=True)
                    evac(ydst, pb[:, 0:64])
                nb = (nseg * 64 + 511) // 512
                with B.bank() as pb0, B.bank() as pb1:
                    pbs = [pb0, pb1]
                    if blk.kind == 's' and WT is not None:
                        B.tt(UM[:], bc(TB.Us[:].unsqueeze(1), [128, 16, 64]),
                             bc(C('ind').unsqueeze(2), [128, 16, 64]), ALU.mult, eng='pool')
                    for h2 in range(2):
                        hs = hsl[h2]
                        for half in range(nb):
                            ncols = min(512, nseg * 64 - half * 512)
                            g0 = half * 8
                            first = True
                            if WT is not None:
                                if blk.kind == 'p':
                                    rhsU = TB.Us[hs, :]
                                else:
                                    rhsU = UM[hs, g0:g0 + 8, :].rearrange("p s v -> p (s v)")
                                B.mm(pbs[half][hs, 0:ncols], b_tm[hs, :], rhsU, start=True, stop=(upd_extra is None))
                                first = False
                            if upd_extra is not None:
                                lt, rv = upd_extra
                                if blk.kind == 'p':
                                    rhsV = rv[hs, :]
                                else:
                                    rhsV = Vmask[hs, g0:g0 + 8, :].rearrange("p s v -> p (s v)")
                                B.mm(pbs[half][hs, 0:ncols], lt[hs, :], rhsV, start=first, stop=True)
                    DC = DC_fn()
                    for half in range(nb):
                        nsg = min(8, nseg - half * 8)
                        g0 = half * 8
                        Mv = M[:, g0:g0 + nsg, :]
                        pv = pbs[half][:, 0:nsg * 64].rearrange("p (s v) -> p s v", v=64)
                        DCb = bc(DC[:, g0:g0 + nsg].unsqueeze(2), [128, nsg, 64])
                        if mode == 'rwkv' and blk.kind == 'p':
                            B.tt(M[:, 0, :], M[:, 0, :], pbs[half][:, 0:64], ALU.add)
                            B.ts(M[:, 0, :], M[:, 0, :], DC[:, 0:1], ALU.mult)
                        elif mode == 'rwkv':
                            B.tt(Mv, Mv, pv, ALU.add)
                            B.tt(Mv, Mv, DCb, ALU.mult, eng='pool')
                        elif blk.kind == 'p':
                            B.stt(M[:, 0, :], M[:, 0, :], DC[:, 0:1], pbs[half][:, 0:64], ALU.mult, ALU.add)
                        else:
                            B.tt(Mv, Mv, DCb, ALU.mult, eng='pool')
                            B.tt(Mv, Mv, pv, ALU.add)

            def segend(ap2d, blk, cs):
                t = ap2d[:, cs]
                return t.rearrange("p (s u) -> p s u", u=blk.seglen)[:, :, blk.seglen - 1]

            def vmask_build(v_tm):
                B.tt(VM[:], bc(v_tm.unsqueeze(1), [128, 16, 64]),
                     bc(C('ind').unsqueeze(2), [128, 16, 64]), ALU.mult, eng='pool')

            def state_io_in(mx, hp, src):
                if mx == 'a':
                    for h2 in range(2):
                        B.dma(STG[:, :, h2, :], src[:, 2 * hp + h2, :, :].rearrange("s v k -> v s k"))
                    for g in range(2):
                        with B.bank() as pb:
                            for q in range(8):
                                sg = 8 * g + q
                                B.tr(pb[:, q * 64:(q + 1) * 64], STG[:, sg].rearrange("p h k -> p (h k)"),
                                     ident[0:64, 0:64])
                            evac(Ms[:, 8 * g:8 * g + 8, :].rearrange("p s v -> p (s v)"), pb[:, 0:512])
                else:
                    for h2 in range(2):
                        B.dma(Ms[64 * h2:64 * h2 + 64, :, :], src[:, 2 * hp + h2, :, :].rearrange("s k v -> k s v"))

            def state_io_out(hp, M, nsq, dst):
                for h2 in range(2):
                    odma(dst[:, 2 * hp + h2, :, :].rearrange("s k v -> k s v"), M[64 * h2:64 * h2 + 64, 0:nsq, :])

            def state_out_rwkv(hp, M, nsq, dst):
                so = hp if nsq == 1 else 0
                for g in range((nsq + 3) // 4):
                    ns = min(4, nsq - 4 * g)
                    with B.bank() as pb:
                        for q in range(ns):
                            B.tr(pb[0:64, q * 128:(q + 1) * 128], M[:, 4 * g + q, :], ident)
                        evac(STG[:, so + 4 * g:so + 4 * g + ns].rearrange("p s h k -> p (s h k)"), pb[0:64, 0:ns * 128])
                for h2 in range(2):
                    odma(dst[:, 2 * hp + h2, :, :].rearrange("s v k -> v s k"), STG[:, so:so + nsq, h2, :])

            def run_pairs(blk, fn, npair):
                if blk.kind == 's':
                    for hp in range(npair):
                        use_set(0)
                        fn(hp)
                    return
                for hp in range(npair):
                    use_set(hp)
                    with B.in_chain(hp):
                        fn(hp)
                B.flush_chains()
                use_set(0)

            def process_block(blk, is_last):
                n = blk.n
                v = blk.v
                T = blk.tb
                nsq = blk.nsq
                load_x_block(blk)
                if blk.kind == 'p':
                    hdst = lambda kc: hT_p[:, kc, 3:3 + n].rearrange("p (s t) -> p s t", s=1)
                else:
                    B.dma(OSH[:], dr['shift_s'][l])
                    with B.bank() as pb:
                        for kc in range(KC):
                            B.tr(pb[:, kc * 16:(kc + 1) * 16], OSH[0:16, kc * 128:(kc + 1) * 128], ident[0:16, 0:16])
                        B.cp(hT_s[:].rearrange("p k (s u) -> p k s u", u=5)[:, :, :, 0],
                             pb[:, 0:128].rearrange("p (k s) -> p k s", s=16))
                    hdst = lambda kc: hT_s[:, kc, :].rearrange("p (s u) -> p s u", u=5)[:, :, 1:5]
                if (l, blk.kind, blk.idx) == DBG_AT:
                    dbg('xin', xb[:].rearrange("p c t -> p (c t)"))
                norm_mod(blk, xb, n, l, 'AM', 'BM', hdst, sq, rstd, tmpn)
                if (l, blk.kind, blk.idx) == DBG_AT:
                    dbg('rstd', rstd[:])
                stage('h_%s%d_%d' % (blk.kind, blk.idx, l))

                for j in range(10):
                    proj_to_PB(blk, (128 * j, 128), j, 1)
                (SG, AA, G_, KK, K2, BBt, CUM, EP, EM, EPV, T1) = SL3[0:11]
                KT, BT_, AT = K2, BBt, KK
                cur = PBv(blk, 0, 10, 3)
                prv = PBv(blk, 0, 10, 2)
                XS4 = c4(XS[:, :, 0:n], blk)
                mu4 = bc(PV('mu', l).unsqueeze(2).unsqueeze(3), [128, 10, nsq, T])
                B.tt(XS4, prv, cur, ALU.subtract, eng='pool')
                B.tt(XS4, XS4, mu4, ALU.mult)
                B.tt(XS4, XS4, cur, ALU.add, eng='pool')
                if blk.kind == 'p' and blk.idx == 0 and l == 0:
                    dbg('XS0', XS[:].rearrange("p j t -> p (j t)"), [128, 10 * BLK])
                if (l, blk.kind, blk.idx) == DBG_AT:
                    dbg('XSa', XS[:].rearrange("p j t -> p (j t)"))
                B.act(TL[0:32, 0:n], XS[0:32, 9, 0:n], AF.Exp, scale=2.0)
                B.act(TL[64:128, 0:n], XS[64:128, 9, 0:n], AF.Exp, scale=-1.0)
                B.ts(TL[:, 0:n], TL[:, 0:n], 1.0, ALU.add)
                B.recip(TL[0:32, 0:n], TL[0:32, 0:n])
                B.recip(TL[64:128, 0:n], TL[64:128, 0:n])
                B.ts(TL[0:32, 0:n], TL[0:32, 0:n], -2.0, ALU.mult, 1.0, ALU.add)
                lora = PV('lora', l)
                for c in range(3):
                    with B.bank() as pb:
                        B.mm(pb[:, 0:n], lora[0:32, c * 128:(c + 1) * 128], TL[0:32, 0:n])
                        B.act(SG[:, c, 0:n], pb[:, 0:n], AF.Exp, bias=NEGB[:, c:c + 1], scale=-1.0)
                    with B.bank() as pb:
                        B.mm(pb[:, 0:n], lora[32:64, c * 128:(c + 1) * 128], XS[32:64, 9, 0:n])
                        B.act(AA[:, c, 0:n], pb[:, 0:n], AF.Exp, bias=NEGB[:, 3 + c:4 + c], scale=-1.0)
                    with B.bank() as pb:
                        B.mm(pb[:, 0:n], lora[64:128, c * 128:(c + 1) * 128], TL[64:128, 0:n])
                        evac(G_[:, c, 0:n], pb[:, 0:n])
                sig_finish(SG[:, :, 0:n])
                sig_finish(AA[:, :, 0:n], eng='pool')
                for c in range(3):
                    B.scan(CUM[:, c, 0:n], C('rm_' + v, c1=n), SG[:, c, 0:n])
                B.act(EP[:, :, 0:n], CUM[:, :, 0:n], AF.Exp, scale=-C0)
                B.act(EM[:, :, 0:n], CUM[:, :, 0:n], AF.Exp, scale=C0)
                B.tt(EPV[:, :, 0:n], CUM[:, :, 0:n], SG[:, :, 0:n], ALU.subtract, eng='pool')
                B.act(EPV[:, :, 0:n], EPV[:, :, 0:n], AF.Exp, scale=-C0)
                r_ = XS[:, 0:3, 0:n]
                k_ = XS[:, 3:6, 0:n]

                def b3(name):
                    return bc(PV(name, l).unsqueeze(2), [128, 3, n])
                B.tt(KK[:, :, 0:n], k_, b3('kk'), ALU.mult)
                B.act(T1[:, :, 0:n], KK[:, :, 0:n], AF.Square)
                for c in range(3):
                    with B.bank() as pb:
                        B.mm(pb[:, 0:n], bones, T1[:, c, 0:n])
                        rsqrt_ln(T1[:, c, 0:n], pb[:, 0:n], bias=1e-6)
                B.tt(KK[:, :, 0:n], KK[:, :, 0:n], T1[:, :, 0:n], ALU.mult)
                B.tt(K2[:, :, 0:n], AA[:, :, 0:n], b3('ka'), ALU.mult, eng='pool')
                B.tt(K2[:, :, 0:n], K2[:, :, 0:n], b3('ka'), ALU.subtract, eng='pool')
                B.stt(K2[:, :, 0:n], K2[:, :, 0:n], 1.0, k_, ALU.add, ALU.mult)
                B.tt(BBt[:, :, 0:n], KK[:, :, 0:n], AA[:, :, 0:n], ALU.mult, eng='pool')
                BON = AA
                B.tt(T1[:, :, 0:n], r_, K2[:, :, 0:n], ALU.mult)
                B.tt(T1[:, :, 0:n], T1[:, :, 0:n], b3('rk'), ALU.mult, eng='pool')
                for c in range(3):
                    with B.bank() as pb:
                        B.mm(pb[:, 0:n], bones, T1[:, c, 0:n])
                        B.tt(BON[:, c, 0:n], pb[:, 0:n], XS[:, 6 + c, 0:n], ALU.mult)
                RT = XS
                B.tt(RT[:, 0:3, 0:n], r_, EP[:, :, 0:n], ALU.mult)
                B.tt(KT[:, :, 0:n], K2[:, :, 0:n], EM[:, :, 0:n], ALU.mult, eng='pool')
                B.tt(BT_[:, :, 0:n], BBt[:, :, 0:n], EM[:, :, 0:n], ALU.mult)
                B.stt(AT[:, :, 0:n], KK[:, :, 0:n], -1.0, EPV[:, :, 0:n], ALU.mult, ALU.mult)
                YR = SG
                stage('rwkvprep_%s%d_%d' % (blk.kind, blk.idx, l))
                def rwkv_pair(hp):
                    nt = blk.ntile
                    if blk.kind == 's':
                        state_io_in('a', hp, dr['srw'][l])
                        Mst = Ms
                    else:
                        Mst = Mp[('a', hp)]
                    css = [slice(64 * ti, 64 * ti + 64) for ti in range(nt)]
                    with B.bank() as pb:
                        pv = pb[:, 0:nt * 256].rearrange("p (n q c) -> p n q c", n=nt, q=4)
                        for ti in range(nt):
                            for h2 in range(2):
                                hs = hsl[h2]
                                for q, src in enumerate((AT, BT_, KT, None)):
                                    s_ap = XS[hs, 6 + hp, css[ti]] if src is None else src[hs, hp, css[ti]]
                                    B.mm(pv[hs, ti, q, :], s_ap, ident[hs, hs])
                        evac(TB.TM[:, 0:nt], pv)
                    with B.bank() as pb1, B.bank() as pb2:
                        p4 = pb1[:, 0:nt * 256].rearrange("p (n a t) -> p n a t", n=nt, a=4)
                        pn = pb2[:, 0:nt * 64].rearrange("p (n t) -> p n t", n=nt)
                        for ti in range(nt):
                            for h2 in range(2):
                                hs = hsl[h2]
                                at = AT[hs, hp, css[ti]]
                                bt = BT_[hs, hp, css[ti]]
                                kt = KT[hs, hp, css[ti]]
                                rt = RT[hs, hp, css[ti]]
                                B.mm(p4[hs, ti, 0, :], bt, at)
                                B.mm(p4[hs, ti, 1, :], kt, at)
                                B.mm(p4[hs, ti, 2, :], bt, rt)
                                B.mm(p4[hs, ti, 3, :], kt, rt)
                                B.mm(pn[hs, ti, :], at, bt)
                        m4 = C('mask5_' + v, c1=256).rearrange("p (a t) -> p a t", a=4)
                        B.tt(TB.A4[:, 0:nt], p4, bc(m4.unsqueeze(1), [128, nt, 4, 64]), ALU.mult)
                        B.tt(TB.NM[:, 0:nt], pn, bc(C('mask5_' + v, c0=256, c1=320).unsqueeze(1), [128, nt, 64]), ALU.mult)
                    wy_double(blk, TB.NM[:, 0:nt, :], TB.A4[:, 0:nt, 0, :])
                    with B.bank() as pb:
                        pv = pb[:, 0:nt * 64].rearrange("p (n t) -> p n t", n=nt)
                        for ti in range(nt):
                            for h2 in range(2):
                                hs = hsl[h2]
                                B.mm(pv[hs, ti, :], TB.TM[hs, ti, 0, :], TB.Pm[hs, ti, :])
                        evac(TB.WTs[:, 0:nt], pv)
                    with B.bank() as pb:
                        pv = pb[:, 0:nt * 64].rearrange("p (n t) -> p n t", n=nt)
                        for ti in range(nt):
                            for h2 in range(2):
                                hs = hsl[h2]
                                B.mm(pv[hs, ti, :], TB.A4[hs, ti, 1, :], TB.TM[hs, ti, 3, :])
                        evac(TB.Xs[:, 0:nt], pv)
                    with B.bank() as pb:
                        pv = pb[:, 0:nt * 64].rearrange("p (n t) -> p n t", n=nt)
                        for ti in range(nt):
                            for h2 in range(2):
                                hs = hsl[h2]
                                if blk.kind == 'p':
                                    B.mm(pv[hs, ti, :], TB.Pm[hs, ti, :], TB.Xs[hs, ti, :])
                                else:
                                    B.mm(pv[hs, ti, :], TB.Xs[hs, ti, :], TB.Pm[hs, ti, :])
                        evac(TB.U0s[:, 0:nt], pv)
                    if blk.kind == 's':
                        vmask_build(TB.TM[:, 0, 3, :])
                    for ti in range(nt):
                        cs = css[ti]
                        seq_step(blk, Mst, TB.WTs[:, ti, :], TB.U0s[:, ti, :], TB.U0Ts[:, ti, :],
                                 ArbT=TB.A4[:, ti, 2, :],
                                 y_extra=(TB.TM[:, ti, 3, :], TB.A4[:, ti, 3, :]),
                                 rt_fn=lambda cc, hp=hp, cs=cs: RT[:, hp, cs][:, cc],
                                 b_tm=TB.TM[:, ti, 1, :],
                                 upd_extra=(TB.TM[:, ti, 2, :], TB.TM[:, ti, 3, :]),
                                 DC_fn=lambda hp=hp, cs=cs: segend(EP[:, hp, :], blk, cs),
                                 mode='rwkv', ydst=YR[:, hp, cs], Vmask=VM)
                    if blk.kind == 's':
                        state_out_rwkv(hp, Ms, NSEQ, dr['o_srw'][l])
                    elif is_last:
                        state_out_rwkv(hp, Mst, 1, dr['o_prw'][l])

                bb0 = A_COLS
                for j in range(9):
                    proj_to_PB(blk, (bb0 + 128 * j, 128), j, 3)
                proj_to(blk, (bb0 + 1152, 6), BETA[0:6, 0:n])
                proj_to(blk, (bb0 + 1158, 6), APRE[0:6, 0:n])
                for j in range(3):
                    proj_to(blk, (bb0 + 1164 + 128 * j, 128), ZB[:, j, 0:n])
                run_pairs(blk, rwkv_pair, 3)
                stage('rwkvchain_%s%d_%d' % (blk.kind, blk.idx, l))
                for c in range(3):
                    with B.bank() as pb:
                        B.mm(pb[:, 0:n], bones, YR[:, c, 0:n])
                        B.act(CUM[:, c, 0:n], pb[:, 0:n], AF.Copy, scale=1.0 / 64)
                mean = CUM[:, :, 0:n]
                B.tt(YR[:, :, 0:n], YR[:, :, 0:n], mean, ALU.subtract, eng='pool')
                B.act(T1[:, :, 0:n], YR[:, :, 0:n], AF.Square)
                for c in range(3):
                    with B.bank() as pb:
                        B.mm(pb[:, 0:n], bones, T1[:, c, 0:n])
                        rsqrt_ln(T1[:, c, 0:n], pb[:, 0:n], scale=1.0 / 64, bias=64e-5)
                B.tt(YR[:, :, 0:n], YR[:, :, 0:n], T1[:, :, 0:n], ALU.mult)
                B.tt(YR[:, :, 0:n], YR[:, :, 0:n], b3('lnw'), ALU.mult, eng='pool')
                B.tt(YR[:, :, 0:n], YR[:, :, 0:n], b3('lnb'), ALU.add)
                B.tt(YR[:, :, 0:n], YR[:, :, 0:n], BON[:, :, 0:n], ALU.add, eng='pool')
                if (l, blk.kind, blk.idx) == DBG_AT:
                    dbg('YRa', YR[:].rearrange("p c t -> p (c t)"))
                    dbg('T1a', T1[:].rearrange("p c t -> p (c t)"))
                    dbg('Ga', G_[:].rearrange("p c t -> p (c t)"))
                B.tt(yT[:, 0:3, 0:n], YR[:, :, 0:n], G_[:, :, 0:n], ALU.mult)

                stage('rwkv_%s%d_%d' % (blk.kind, blk.idx, l))
                if blk.kind == 's':
                    B.dma(OCV[:], dr['sconv'][l].rearrange("s j c -> (s j) c"))
                    for g in range(3):
                        with B.bank() as pb:
                            for q in range(3):
                                j = 3 * g + q
                                B.tr(pb[:, q * 48:(q + 1) * 48], OCV[0:48, j * 128:(j + 1) * 128], ident[0:48, 0:48])
                            dst = PB[:, 3 * g:3 * g + 3, 0:blk.w].rearrange("p j (s u) -> p j s u", u=7)[:, :, :, 0:3]
                            evac(dst, pb[:, 0:144].rearrange("p (j s u) -> p j s u", j=3, u=3))
                if blk.kind == 's' or is_last:
                    nr = 3 * nsq
                    for g in range(3):
                        with B.bank() as pb:
                            for q in range(3):
                                j = 3 * g + q
                                if blk.kind == 'p':
                                    src = PB[:, j, n:n + 3]
                                else:
                                    src = TL[:, 48 * (j % 2):48 * (j % 2) + 48]
                                    B.cp(src.rearrange("p (s u) -> p s u", u=3),
                                         PB[:, j, 0:blk.w].rearrange("p (s u) -> p s u", u=7)[:, :, 4:7], eng='pool')
                                B.tr(pb[0:nr, q * 128:(q + 1) * 128], src, ident)
                            evac(OCV[0:nr, 384 * g:384 * g + 384], pb[0:nr, 0:384])
                    dst = dr['o_sconv'][l] if blk.kind == 's' else dr['o_pconv'][l]
                    odma(dst.rearrange("s j c -> (s j) c"), OCV[0:nr, :])
                (QKV0, QKV1, QKV2, CA0, CA1, CA2, SQ0, SQ1) = SL3[0:8]
                cw = PV('convw', l).rearrange("p (j c) -> p j c", j=4)
                ACC = [QKV0, QKV1, QKV2]
                TMPc = [CA0, CA1, CA2]
                for g in range(3):
                    acc4 = c4(ACC[g][:, :, 0:n], blk)
                    tmp4 = c4(TMPc[g][:, :, 0:n], blk)
                    for j in range(4):
                        src = PBv(blk, 3 * g, 3 * g + 3, j)
                        wj = bc(cw[:, j, 3 * g:3 * g + 3].unsqueeze(2).unsqueeze(3), [128, 3, nsq, T])
                        if j == 0:
                            B.tt(acc4, src, wj, ALU.mult, eng=('pool' if g == 1 else 'dve'))
                        else:
                            B.tt(tmp4, src, wj, ALU.mult, eng='pool')
                            B.tt(acc4, acc4, tmp4, ALU.add)
                    silu_exp(ACC[g][:, :, 0:n], ACC[g][:, :, 0:n], TMPc[g][:, :, 0:n])
                Qg, Kg, Vg = QKV0, QKV1, QKV2
                for arr, sc_, bi_ in ((Qg, 64.0, 64e-6), (Kg, 1.0, 1e-6)):
                    B.act(SQ0[:, :, 0:n], arr[:, :, 0:n], AF.Square)
                    for c in range(3):
                        with B.bank() as pb:
                            B.mm(pb[:, 0:n], bones, SQ0[:, c, 0:n])
                            rsqrt_ln(SQ1[:, c, 0:n], pb[:, 0:n], scale=sc_, bias=bi_)
                    B.tt(arr[:, :, 0:n], arr[:, :, 0:n], SQ1[:, :, 0:n], ALU.mult)
                B.act(BE6[:, 0:n], BETA[0:6, 0:n], AF.Exp, scale=-1.0)
                sig_finish(BE6[:, 0:n])
                B.act(G6[:, 0:n], APRE[0:6, 0:n], AF.Exp, bias=PV('dtb', l, rows=6))
                B.act(G6[:, 0:n], G6[:, 0:n], AF.Ln, bias=1.0)
                B.act(EA6[:], PV('alog', l, rows=6), AF.Exp)
                B.ts(G6[:, 0:n], G6[:, 0:n], EA6[:, 0:1], ALU.mult, -1.0, ALU.mult)
                B.scan(GC6[:, 0:n], C('rm_' + v, rows=6, c1=n), G6[:, 0:n])
                silu_exp(ZB[:, :, 0:n], ZB[:, :, 0:n], CA0[:, :, 0:n])
                OR = SL3[8]
                stage('gdnprep_%s%d_%d' % (blk.kind, blk.idx, l))
                def gdn_pair(hp):
                    nt = blk.ntile
                    Mst = Ms if blk.kind == 's' else Mp[('b', hp)]
                    if blk.kind == 's':
                        state_io_in('b', hp, dr['sgdn'][l])
                    css = [slice(64 * ti, 64 * ti + 64) for ti in range(nt)]
                    SCP = TB.SCP[:, 0:nt, :]
                    SCD = TB.SCD[:, 0:nt, :]
                    with B.bank() as pb:
                        pv = pb[:, 0:nt * 4].rearrange("p (n q) -> p n q", n=nt)
                        for ti in range(nt):
                            for h2 in range(2):
                                hs = hsl[h2]
                                hh = 2 * hp + h2
                                for q, src in enumerate((BE6, GC6, G6)):
                                    B.mm(pv[hs, ti, q:q + 1], src[0:6, css[ti]], ident[0:6, hh:hh + 1])
                        evac(SCP[:, :, 0:3], pv[:, :, 0:3])
                    with B.bank() as pb:
                        pv = pb[:, 0:nt * 4].rearrange("p (n q) -> p n q", n=nt)
                        for ti in range(nt):
                            for h2 in range(2):
                                hs = hsl[h2]
                                B.mm(pv[hs, ti, 0:1], C('BS_' + v)[hs, :], TB.SCP[hs, ti, 2:3])
                        B.tt(SCD[:, :, 2:3], pv[:, :, 0:1], SCP[:, :, 1:2], ALU.subtract)
                    B.act(SCD[:, :, 2:3], SCD[:, :, 2:3], AF.Exp)
                    B.act(SCD[:, :, 3:4], SCP[:, :, 1:2], AF.Exp)
                    B.stt(SCD[:, :, 0:1], SCP[:, :, 0:1], -1.0, SCD[:, :, 3:4], ALU.mult, ALU.mult)
                    B.ts(SCD[:, :, 1:2], SCP[:, :, 0:1], -1.0, ALU.mult)
                    GSL = TB.GSL[:, 0:nt, :]
                    GUI = TB.GUI[:, 0:nt, :]
                    NMv = TB.NM[:, 0:nt, :]

                    def b64(ap):
                        return bc(ap, [128, nt, 64])
                    with B.bank() as pb:
                        p3 = pb[:, 0:nt * 192].rearrange("p (n a t) -> p n a t", n=nt, a=3)
                        for ti in range(nt):
                            for h2 in range(2):
                                hs = hsl[h2]
                                hh = 2 * hp + h2
                                B.mm(p3[hs, ti, 0, :], C('osel', rows=6, c0=hh * 64, c1=hh * 64 + 64), GC6[0:6, css[ti]])
                                B.mm(p3[hs, ti, 1, :], Kg[hs, hp, css[ti]], Kg[hs, hp, css[ti]])
                                B.mm(p3[hs, ti, 2, :], Kg[hs, hp, css[ti]], Qg[hs, hp, css[ti]])
                        B.tt(GSL, p3[:, :, 0, :], b64(SCP[:, :, 1:2]), ALU.subtract)
                        B.tt(GUI, GSL, bc(C('BIU_' + v).unsqueeze(1), [128, nt, 64]), ALU.subtract)
                        B.tt(GSL, GSL, bc(C('BSL_' + v).unsqueeze(1), [128, nt, 64]), ALU.add, eng='pool')
                        B.act(GSL, GSL, AF.Exp, scale=-1.0)
                        B.act(GUI, GUI, AF.Exp)
                        B.tt(NMv, p3[:, :, 1, :], b64(SCD[:, :, 1:2]), ALU.mult)
                        B.tt(NMv, NMv, GSL, ALU.mult)
                        B.tt(GUI, p3[:, :, 2, :], GUI, ALU.mult)
                    with B.bank() as pb:
                        pv = pb[:, 0:nt * 64].rearrange("p (n t) -> p n t", n=nt)
                        for ti in range(nt):
                            for h2 in range(2):
                                hs = hsl[h2]
                                B.mm(pv[hs, ti, :], TB.NM[hs, ti, :], ident[hs, hs])
                        evac(TB.Qs[:, 0:nt], pv)
                    wy_double(blk, NMv, TB.Qs[:, 0:nt, :])
                    with B.bank() as pb:
                        pv = pb[:, 0:nt * 128].rearrange("p (n q c) -> p n q c", n=nt, q=2)
                        for ti in range(nt):
                            for h2 in range(2):
                                hs = hsl[h2]
                                B.mm(pv[hs, ti, 0, :], Kg[hs, hp, css[ti]], ident[hs, hs])
                                B.mm(pv[hs, ti, 1, :], Vg[hs, hp, css[ti]], ident[hs, hs])
                        evac(TB.TM[:, 0:nt, 0:2, :], pv)
                    B.tt(TB.KB3[:, 0:nt, 0, :], TB.TM[:, 0:nt, 0, :], b64(SCD[:, :, 0:1]), ALU.mult, eng='pool')
                    B.tt(TB.KB3[:, 0:nt, 1, :], TB.TM[:, 0:nt, 1, :], b64(SCP[:, :, 0:1]), ALU.mult)
                    B.tt(TB.KB3[:, 0:nt, 2, :], TB.TM[:, 0:nt, 0, :], b64(SCD[:, :, 2:3]), ALU.mult, eng='pool')
                    with B.bank() as pb:
                        pv = pb[:, 0:nt * 64].rearrange("p (n t) -> p n t", n=nt)
                        for ti in range(nt):
                            for h2 in range(2):
                                hs = hsl[h2]
                                B.mm(pv[hs, ti, :], TB.KB3[hs, ti, 0, :], TB.Pm[hs, ti, :])
                        evac(TB.WTs[:, 0:nt], pv)
                    with B.bank() as pb:
                        pv = pb[:, 0:nt * 64].rearrange("p (n t) -> p n t", n=nt)
                        for ti in range(nt):
                            for h2 in range(2):
                                hs = hsl[h2]
                                if blk.kind == 'p':
                                    B.mm(pv[hs, ti, :], TB.Pm[hs, ti, :], TB.KB3[hs, ti, 1, :])
                                else:
                                    B.mm(pv[hs, ti, :], TB.KB3[hs, ti, 1, :], TB.Pm[hs, ti, :])
                        evac(TB.U0s[:, 0:nt], pv)
                    with B.bank() as pb:
                        pv = pb[:, 0:nt * 64].rearrange("p (n t) -> p n t", n=nt)
                        for ti in range(nt):
                            B.mm(pv[:, ti, :], C('selb', rows=6, c0=hp * 128, c1=hp * 128 + 128), GC6[0:6, css[ti]])
                        B.act(TB.EBC[:, 0:nt], pv, AF.Exp)
                    B.tt(TB.QTg[:, 0:nt], Qg[:, hp, 0:nt * 64].rearrange("p (n t) -> p n t", n=nt), TB.EBC[:, 0:nt], ALU.mult)
                    for ti in range(nt):
                        cs = css[ti]
                        seq_step(blk, Mst, TB.WTs[:, ti, :], TB.U0s[:, ti, :], TB.U0Ts[:, ti, :],
                                 ArbT=TB.GUI[:, ti, :],
                                 y_extra=None,
                                 rt_fn=lambda cc, ti=ti: TB.QTg[:, ti, cc],
                                 b_tm=TB.KB3[:, ti, 2, :],
                                 upd_extra=None,
                                 DC_fn=lambda ti=ti: TB.EBC[:, ti, :].rearrange("p (s u) -> p s u", u=blk.seglen)[:, :, blk.seglen - 1],
                                 mode='gdn', ydst=OR[:, hp, cs])
                    if blk.kind == 's':
                        state_io_out(hp, Ms, NSEQ, dr['o_sgdn'][l])
                    elif is_last:
                        state_io_out(hp, Mst, 1, dr['o_pgdn'][l])

                cc0 = A_COLS + B_COLS
                for j in range(8):
                    proj_to(blk, (cc0 + 128 * j, 128), PB[:, j, 0:n])
                run_pairs(blk, gdn_pair, 3)
                stage('gdnchain_%s%d_%d' % (blk.kind, blk.idx, l))
                B.act(SQ0[:, :, 0:n], OR[:, :, 0:n], AF.Square)
                for c in range(3):
                    with B.bank() as pb:
                        B.mm(pb[:, 0:n], bones, SQ0[:, c, 0:n])
                        rsqrt_ln(SQ1[:, c, 0:n], pb[:, 0:n], scale=1.0 / 64, bias=1e-6)
                B.tt(OR[:, :, 0:n], OR[:, :, 0:n], SQ1[:, :, 0:n], ALU.mult)
                B.ts(OR[:, :, 0:n], OR[:, :, 0:n], PV('gnw', l), ALU.mult, eng='pool')
                if (l, blk.kind, blk.idx) == DBG_AT:
                    dbg('ORa', OR[:].rearrange("p c t -> p (c t)"))
                    dbg('ZBa', ZB[:].rearrange("p c t -> p (c t)"))
                B.tt(yT[:, 3:6, 0:n], OR[:, :, 0:n], ZB[:, :, 0:n], ALU.mult)

                stage('gdn_%s%d_%d' % (blk.kind, blk.idx, l))
                (SF, LF, KH_, QS, BC_, QTL, KTL, QH) = [t[:, 0:2, 0:n] for t in SL3[0:8]]
                (KHAT, OH, HT1) = [t[:, 0:2, 0:n] for t in SL3[8:11]]
                lb2 = bc(LBt[:, l, :].unsqueeze(2), [128, 2, n])
                oml2 = bc(OMLt[:, l, :].unsqueeze(2), [128, 2, n])
                B.act(SF, PB[:, 2:4, 0:n], AF.Exp, scale=-1.0)
                sig_finish(SF)
                B.tt(LF, SF, oml2, ALU.mult, eng='pool')
                B.tt(LF, LF, lb2, ALU.add)
                B.act(LF, LF, AF.Ln)
                B.ts(KH_, SF, -1.0, ALU.mult, 1.0, ALU.add, eng='pool')
                B.tt(KH_, KH_, oml2, ALU.mult)
                silu_exp(QS, PB[:, 0:2, 0:n], QS)
                for c in range(2):
                    B.scan(BC_[:, c, :], C('rm_' + v, c1=n), LF[:, c, :])
                sl = blk.seglen
                nsg_all = n // sl
                mid = sl // 2 - 1

                def sv(ap):
                    return ap.rearrange("p c (s u) -> p c s u", u=sl)
                B.tt(sv(HT1), sv(BC_), bc(sv(BC_)[:, :, :, mid:mid + 1], [128, 2, nsg_all, sl]), ALU.subtract, eng='pool')
                B.act(QTL, HT1, AF.Exp)
                B.tt(QTL, QTL, QS, ALU.mult)
                B.act(KTL, HT1, AF.Exp, scale=-1.0)
                B.tt(KTL, KTL, KH_, ALU.mult, eng='pool')
                EB = SL3[11][:, 0:2, 0:n]
                B.act(EB, BC_, AF.Exp)
                B.tt(QH, QS, EB, ALU.mult)
                B.tt(sv(HT1), bc(sv(BC_)[:, :, :, sl - 1:sl], [128, 2, nsg_all, sl]), sv(BC_), ALU.subtract, eng='pool')
                B.act(HT1, HT1, AF.Exp)
                B.tt(KHAT, KH_, HT1, ALU.mult)
                def hgrn_pair(hp):
                    nt = blk.ntile
                    if blk.kind == 's':
                        state_io_in('c', hp, dr['shg'][l])
                        Mst = Ms
                    else:
                        Mst = Mp[('c', hp)]
                    css = [slice(64 * ti, 64 * ti + 64) for ti in range(nt)]
                    with B.bank() as pb:
                        pv = pb[:, 0:nt * 128].rearrange("p (n q c) -> p n q c", n=nt, q=2)
                        for ti in range(nt):
                            for h2 in range(2):
                                hs = hsl[h2]
                                B.mm(pv[hs, ti, 0, :], KHAT[hs, hp, css[ti]], ident[hs, hs])
                                B.mm(pv[hs, ti, 1, :], PB[hs, 4 + hp, css[ti]], ident[hs, hs])
                        evac(TB.TM[:, 0:nt, 0:2, :], pv)
                    with B.bank() as pb:
                        pv = pb[:, 0:nt * 64].rearrange("p (n t) -> p n t", n=nt)
                        for ti in range(nt):
                            for h2 in range(2):
                                hs = hsl[h2]
                                B.mm(pv[hs, ti, :], KTL[hs, hp, css[ti]], QTL[hs, hp, css[ti]])
                        B.ts(TB.GUI[:, 0:nt], pv, 3.0e38, ALU.min, -3.0e38, ALU.max)
                        B.tt(TB.GUI[:, 0:nt], TB.GUI[:, 0:nt], bc(C('IU_' + v).unsqueeze(1), [128, nt, 64]), ALU.mult)
                    if blk.kind == 's':
                        vmask_build(TB.TM[:, 0, 1, :])
                    for ti in range(nt):
                        cs = css[ti]
                        seq_step(blk, Mst, None, None, None,
                                 ArbT=None,
                                 y_extra=(TB.TM[:, ti, 1, :], TB.GUI[:, ti, :]),
                                 rt_fn=lambda cc, hp=hp, cs=cs: QH[:, hp, cs][:, cc],
                                 b_tm=None,
                                 upd_extra=(TB.TM[:, ti, 0, :], TB.TM[:, ti, 1, :]),
                                 DC_fn=lambda hp=hp, cs=cs: segend(EB[:, hp, :], blk, cs),
                                 mode='gdn', ydst=OH[:, hp, cs], Vmask=VM)
                    if blk.kind == 's':
                        state_io_out(hp, Ms, NSEQ, dr['o_shg'][l])
                    elif is_last:
                        state_io_out(hp, Mst, 1, dr['o_phg'][l])

                stage('hgrnprep_%s%d_%d' % (blk.kind, blk.idx, l))
                run_pairs(blk, hgrn_pair, 2)
                stage('hgrnchain_%s%d_%d' % (blk.kind, blk.idx, l))
                SQh = SL3[0][:, 0:2, 0:n]
                SQg = SL3[1][:, 0:2, 0:n]
                B.act(SQh, OH, AF.Square)
                for c in range(2):
                    with B.bank() as pb:
                        B.mm(pb[:, 0:n], bones, SQh[:, c, :])
                        rsqrt_ln(SQg[:, c, :], pb[:, 0:n], scale=1.0 / 64, bias=1e-6)
                B.tt(OH, OH, SQg, ALU.mult)
                B.ts(OH, OH, PV('hnw', l), ALU.mult, eng='pool')
                B.act(SQh, PB[:, 6:8, 0:n], AF.Exp, scale=-1.0)
                sig_finish(SQh)
                if (l, blk.kind, blk.idx) == DBG_AT:
                    dbg('OHa', SL3[9][:].rearrange("p c t -> p (c t)"))
                    dbg('xba', xb[:].rearrange("p c t -> p (c t)"))
                B.tt(yT[:, 6:8, 0:n], OH, SQh, ALU.mult)

                if blk.kind == 's' or is_last:
                    HL = TL[:, :]
                    for kc in range(KC):
                        if blk.kind == 'p':
                            srcl = hT_p[:, kc, 3 + n - 1:3 + n]
                        else:
                            srcl = hT_s[:, kc, :].rearrange("p (s u) -> p s u", u=5)[:, :, 4]
                        B.cp(HL[:, kc * 16:kc * 16 + nsq], srcl, eng='pool')
                    for half in range(2):
                        with B.bank() as pb:
                            for q in range(4):
                                kc = 4 * half + q
                                B.tr(pb[0:nsq, q * 128:(q + 1) * 128], HL[:, kc * 16:kc * 16 + nsq], ident)
                            evac(OSH[0:nsq, 512 * half:512 * half + 512], pb[0:nsq, 0:512])
                    dst = dr['o_sshift'][l] if blk.kind == 's' else dr['o_pshift'][l]
                    odma(dst, OSH[0:nsq, :])
                for m in range(KC):
                    with B.bank() as pb:
                        for kc in range(KC):
                            B.mm(pb[:, 0:n], w_out_sb[:, kc, m * 128:(m + 1) * 128], yT[:, kc, 0:n],
                                 start=(kc == 0), stop=(kc == KC - 1))
                        gate_res(blk, xb[:, m, 0:n], pb[:, 0:n], l, 'GM', m, n, tmpn)
                B.dma(xscr[:, :, blk.tok0:blk.tok0 + n], xb[:, :, 0:n], wr=xkey(0, l, blk.tok0, n))
                stage('blk_%s%d_%d' % (blk.kind, blk.idx, l))

            for bi, blk in enumerate(prompt_blocks):
                process_block(blk, bi == len(prompt_blocks) - 1)
                B.cp(hT_p[:, :, 0:3], hT_p[:, :, BLK:BLK + 3], eng='pool')
            process_block(sample_block, True)
            B.release(mph)
            stage('mix_%d' % l)

            fph = B.mark()
            JG = 6
            NG = (NJ + JG - 1) // JG
            WFI = [B.sb("w_fi%d" % g, [128, KC, 2, (min(NJ, g * JG + JG) - g * JG) * 128], BF16) for g in range(NG)]
            WFO = [B.sb("w_fo%d" % g, [128, min(NJ, g * JG + JG) - g * JG, D], BF16) for g in range(NG)]
            wsrc = dr['w_fi'][l].rearrange("(kc p) n -> p kc n", p=128)
            for g in range(NG):
                j0, j1 = g * JG, min(NJ, g * JG + JG)
                w = (j1 - j0) * 128
                for part in range(2):
                    B.dma(WFI[g][:, :, part, 0:w], wsrc[:, :, part * FF + j0 * 128:part * FF + j0 * 128 + w],
                          eng='pool', cls='w')
            wsrc = dr['w_fo'][l].rearrange("(j p) n -> p j n", p=128)
            for g in range(NG):
                j0, j1 = g * JG, min(NJ, g * JG + JG)
                B.dma(WFO[g][:, 0:j1 - j0, :], wsrc[:, j0:j1, :], eng='pool', cls='w')
            xb2 = B.sb("xb2", [128, KC, FBLK])
            h2T = B.sb("h2T", [128, KC, FBLK], BF16)
            actT = B.sb("actT", [128, NJ, FBLK], BF16)
            sq2 = [B.sb("sq2_%d" % i, [128, FBLK]) for i in range(2)]
            rstd2 = B.sb("rstd2", [128, FBLK])
            tmp2 = [B.sb("tmp2_%d" % i, [128, FBLK]) for i in range(2)]
            sgt = [B.sb("sgt%d" % i, [128, FBLK]) for i in range(2)]
            last_layer = (l == DEPTH - 1)
            if last_layer:
                yfin = B.sb("yfin", [128, KC, FBLK])
                ytm = B.sb("ytm", [128, D])
            for fb in ffn_blocks:
                n = fb.n
                B.dma(xb2[:, :, 0:n], xscr[:, :, fb.tok0:fb.tok0 + n], rd=xkey(1, l, fb.tok0, n))
                hdst = lambda kc, fb=fb, n=n: v3(h2T[:, kc, 0:n], fb)
                norm_mod(fb, xb2, n, l, 'AF', 'BF', hdst, sq2, rstd2, tmp2)
                for j in range(NJ):
                    with B.bank() as pg, B.bank() as pu:
                        for kc in range(KC):
                            B.mm(pg[:, 0:n], WFI[j // JG][:, kc, 0, (j % JG) * 128:(j % JG + 1) * 128], h2T[:, kc, 0:n],
                                 start=(kc == 0), stop=(kc == KC - 1))
                        for kc in range(KC):
                            B.mm(pu[:, 0:n], WFI[j // JG][:, kc, 1, (j % JG) * 128:(j % JG + 1) * 128], h2T[:, kc, 0:n],
                                 start=(kc == 0), stop=(kc == KC - 1))
                        B.act(sgt[j % 2][:, 0:n], pg[:, 0:n], AF.Silu)
                        B.tt(actT[:, j, 0:n], sgt[j % 2][:, 0:n], pu[:, 0:n], ALU.mult)
                for m in range(KC):
                    with B.bank() as pb:
                        for j in range(NJ):
                            B.mm(pb[:, 0:n], WFO[j // JG][:, j % JG, m * 128:(m + 1) * 128], actT[:, j, 0:n],
                                 start=(j == 0), stop=(j == NJ - 1))
                        gate_res(fb, xb2[:, m, 0:n], pb[:, 0:n], l, 'GF', m, n, tmp2)
                if not last_layer:
                    B.dma(xscr[:, :, fb.tok0:fb.tok0 + n], xb2[:, :, 0:n], wr=xkey(1, l, fb.tok0, n))
                else:
                    rms_stats(xb2, n, sq2, rstd2)
                    for kc in range(KC):
                        B.stt(yfin[:, kc, 0:n], xb2[:, kc, 0:n], PV('fnw', 0, c0=kc, c1=kc + 1), rstd2[:, 0:n],
                              ALU.mult, ALU.mult)
                    nt = (n + 127) // 128
                    for ti in range(nt):
                        r = min(128, n - 128 * ti)
                        for g in range(2):
                            with B.bank() as pb:
                                for q in range(4):
                                    kc = 4 * g + q
                                    B.tr(pb[0:r, q * 128:(q + 1) * 128], yfin[:, kc, 128 * ti:128 * ti + r], ident)
                                evac(ytm[0:r, 512 * g:512 * g + 512], pb[0:r, 0:512])
                        if fb.kind == 'p':
                            odma(dr['yp'][fb.tok0 + 128 * ti:fb.tok0 + 128 * ti + r, :], ytm[0:r, :])
                        else:
                            odma(dr['ys'], ytm[0:r, :])
            B.release(fph)
            stage('ffn_%d' % l)

        S.disabled = False
        S.barrier()
        with nc.Block() as block:
            S.emit(block)
    build_program.dbg_list = dbg_list
    build_program.n_ops = S.n_ops
    build_program.sbuf_top = B.top
    return nc


INPUT_ORDER = None


def make_in_maps(inp):
    pv = pack_pvec(inp)
    cst = make_consts()
    f = lambda a: np.ascontiguousarray(np.asarray(a, np.float32))
    shared = {
        'w_ada': f(inp['w_ada']), 'w_in': f(inp['w_in']), 'w_out': f(inp['w_out']),
        'w_fi': f(inp['w_ffn_in']), 'w_fo': f(inp['w_ffn_out']), 'pvec': pv, 'cst': cst,
    }
    maps = []
    for c in range(NCORES):
        s0, s1 = NSEQ * c, NSEQ * (c + 1)
        m = dict(shared)
        m['xp'] = f(inp['x_prompt'][c])
        m['xs'] = f(np.asarray(inp['x_sample'][s0:s1]).reshape(TOKS, D))
        m['c17'] = f(np.concatenate([np.asarray(inp['c_prompt'][c:c + 1]), np.asarray(inp['c_sample'][s0:s1])], axis=0))
        m['shift_s'] = f(np.asarray(inp['state_rwkv_shift'])[:, s0:s1])
        m['srw'] = f(np.asarray(inp['state_rwkv'])[:, s0:s1])
        m['sconv'] = f(np.asarray(inp['state_gdn_conv'])[:, s0:s1])
        m['sgdn'] = f(np.asarray(inp['state_gdn'])[:, s0:s1])
        m['shg'] = f(np.asarray(inp['state_hgrn'])[:, s0:s1])
        maps.append(m)
    return maps


_NC_CACHE = {}


def kernel(**inputs):
    inp = {k: np.asarray(v) for k, v in inputs.items()}
    maps = make_in_maps(inp)
    if 'nc' not in _NC_CACHE:
        _NC_CACHE['nc'] = build_program()
    nc = _NC_CACHE['nc']
    res = run_bass_kernel_spmd(nc, maps, core_ids=list(range(NCORES)))
    R = res.results

    def cat(name, axis):
        return np.concatenate([np.asarray(R[c][name], np.float32) for c in range(NCORES)], axis=axis)

    y_prompt = np.stack([np.asarray(R[c]['yp'], np.float32) for c in range(NCORES)], axis=0)
    y_sample = cat('ys', 0).reshape(NCORES * NSEQ, TSS, D)
    p_shift = cat('o_pshift', 1)
    p_rwkv = cat('o_prw', 1)
    p_conv = cat('o_pconv', 1)
    p_gdn = cat('o_pgdn', 1)
    p_hgrn = cat('o_phg', 1)
    s_shift = cat('o_sshift', 1)
    s_rwkv = cat('o_srw', 1)
    s_conv = cat('o_sconv', 1)
    s_gdn = cat('o_sgdn', 1)
    s_hgrn = cat('o_shg', 1)
    return (y_prompt, y_sample, p_shift, p_rwkv, p_conv, p_gdn, p_hgrn,
            s_shift, s_rwkv, s_conv, s_gdn, s_hgrn)
```

```python
import contextlib
import numpy as np
import concourse.bass as bass
import concourse.mybir as mybir
from concourse.bass_utils import run_bass_kernel_spmd

F32 = mybir.dt.float32
BF16 = mybir.dt.bfloat16
ALU = mybir.AluOpType
AF = mybir.ActivationFunctionType

NCORES = 8
D = 1024
KC = 8
TOKP = 2048
NSEQ = 16
TSS = 4
TOKS = 64
DEPTH = 2
A_COLS = 1280
B_COLS = 1548
C_COLS = 1024
IN_COLS = 3852
FF = 2816
NJ = 22
BLK = 128
FBLK = 256
C0 = 0.6065306597126334
BIG = 1.0e4
SEM_LIMIT = 12000
SB_LO = 16512
SB_HI = 229376

DBG_AT = None
TRACE_LINES = False
FUSE_WAIT = True
DMA_SLOTS = 8
CHAIN_BANKS = {0: [0, 1], 1: [2, 3], 2: [4, 5]}
OP_LINES = {}


class Sched:
    ENGS = ('pe', 'act', 'dve', 'pool', 'sp')

    def __init__(self, nc, stack):
        self.nc = nc
        self.stack = stack
        self.prog = {e: [] for e in self.ENGS}
        self.stream_sem = {}
        self.stream_cnt = {}
        self.last_w = {}
        self.reads = {}
        self.seen = {e: {} for e in self.ENGS}
        self.snap = {}
        self.dma_rr = {}
        self.nsem = 0
        self.n_ops = 0
        self.disabled = False
        self.max_ops = None
        self.trace = None

    def _new_sem(self, stream):
        s = self.stack.enter_context(self.nc.semaphore("s%d" % self.nsem))
        self.nsem += 1
        self.stream_sem.setdefault(stream, []).append(s)
        self.stream_cnt[stream] = 0

    def _event(self, stream, inc):
        if stream not in self.stream_sem or self.stream_cnt[stream] + inc > SEM_LIMIT:
            self._new_sem(stream)
        self.stream_cnt[stream] += inc
        return (stream, len(self.stream_sem[stream]) - 1, self.stream_cnt[stream])

    @staticmethod
    def _key(a):
        return a if isinstance(a, str) else a.name

    def op(self, eng, fn, reads=(), writes=(), dma_class=None):
        if self.disabled or (self.max_ops is not None and self.n_ops >= self.max_ops):
            return None
        if dma_class is not None:
            k = self.dma_rr.get((eng, dma_class), 0)
            self.dma_rr[(eng, dma_class)] = k + 1
            stream = (eng, dma_class, k % DMA_SLOTS)
            inc = 16
        else:
            stream = eng
            inc = 1
        rk = [self._key(r) for r in reads]
        wk = [self._key(w) for w in writes]
        deps = []
        if dma_class is not None and stream in self.stream_sem and self.stream_cnt[stream] > 0:
            deps.append(((stream, len(self.stream_sem[stream]) - 1, self.stream_cnt[stream]), 'slot'))
        for k in rk:
            if k in self.last_w:
                deps.append((self.last_w[k], 'raw'))
        for k in wk:
            if k in self.last_w:
                deps.append((self.last_w[k], 'waw'))
            for ev in self.reads.get(k, ()):
                deps.append((ev, 'war'))
        waits = []
        seen = self.seen[eng]
        for ev, kind in deps:
            st, ep, val = ev
            if st == eng and dma_class is None:
                if eng == 'pe' or kind == 'war':
                    continue
            sk = (st, ep)
            if seen.get(sk, 0) >= val:
                continue
            seen[sk] = val
            waits.append((self.stream_sem[st][ep], val))
            snap = self.snap.get(ev)
            if snap:
                for k2, v2 in snap.items():
                    if seen.get(k2, 0) < v2:
                        seen[k2] = v2
        ev = self._event(stream, inc)
        sem = self.stream_sem[stream][ev[1]]
        self.snap[ev] = dict(seen)
        if TRACE_LINES:
            OP_LINES.setdefault(eng, []).append(getattr(fn, '_ln', None))
        if self.trace is not None:
            import sys as _sys
            self.trace.append((self.n_ops, eng, _sys._getframe(2).f_lineno))
        self.prog[eng].append((waits, fn, sem, inc))
        for k in wk:
            self.last_w[k] = ev
            self.reads[k] = []
        for k in rk:
            if k not in wk:
                self.reads.setdefault(k, []).append(ev)
        self.n_ops += 1
        return ev

    def barrier(self):
        evs = []
        for stream, sems in self.stream_sem.items():
            ep = len(sems) - 1
            if self.stream_cnt[stream] > 0:
                evs.append((stream, ep, self.stream_cnt[stream]))
        for e in self.ENGS:
            waits = []
            for st, ep, val in evs:
                if st == e:
                    continue
                sk = (st, ep)
                if self.seen[e].get(sk, 0) >= val:
                    continue
                self.seen[e][sk] = val
                waits.append((self.stream_sem[st][ep], val))
            if waits:
                self.prog[e].append((waits, None, None, 0))

    def emit(self, block):
        prog = self.prog

        def run(e, engobj):
            for waits, fn, sem, inc in prog[e]:
                if fn is None:
                    for s, v in waits:
                        engobj.wait_ge(s, v)
                    continue
                fuse = FUSE_WAIT and inc == 1 and len(waits) > 0
                for s, v in (waits[:-1] if fuse else waits):
                    engobj.wait_ge(s, v)
                ins = fn(engobj)
                if fuse:
                    ins._wait_ge(waits[-1][0], waits[-1][1])
                ins.then_inc(sem, inc)

        @block.tensor
        def _(eng):
            run('pe', eng)

        @block.scalar
        def _(eng):
            run('act', eng)

        @block.vector
        def _(eng):
            run('dve', eng)

        @block.gpsimd
        def _(eng):
            run('pool', eng)

        @block.sync
        def _(eng):
            run('sp', eng)


def _dtsize(dt):
    return 2 if dt == BF16 else 4


class KB:
    def __init__(self, nc, stack):
        self.nc = nc
        self.S = Sched(nc, stack)
        self.top = SB_LO
        self.uid = 0
        self.banks = [stack.enter_context(nc.psum_tensor("pb%d" % i, [128, 512], F32)) for i in range(8)]
        self.free_banks = list(range(8))
        self.chain = None
        self.chains = {}
        self.chain_bank_cnt = {}

    def emit(self, eng, fn, reads=(), writes=(), dma_class=None):
        if TRACE_LINES:
            import sys as _sys
            fn._ln = _sys._getframe(2).f_lineno
        if self.chain is None:
            self.S.op(eng, fn, reads=reads, writes=writes, dma_class=dma_class)
        else:
            self.chains[self.chain].append((eng, fn, list(reads), list(writes), dma_class))

    @contextlib.contextmanager
    def in_chain(self, k):
        assert self.chain is None
        self.chain = k
        self.chains.setdefault(k, [])
        try:
            yield
        finally:
            self.chain = None

    def flush_chains(self):
        lists = [self.chains[k] for k in sorted(self.chains)]
        self.chains = {}
        pos = [0] * len(lists)
        remaining = sum(len(x) for x in lists)
        while remaining:
            for i, lst in enumerate(lists):
                if pos[i] < len(lst):
                    eng, fn, reads, writes, dc = lst[pos[i]]
                    pos[i] += 1
                    remaining -= 1
                    self.S.op(eng, fn, reads=reads, writes=writes, dma_class=dc)

    def sb(self, name, shape, dt=F32):
        size = int(np.prod(shape[1:])) * _dtsize(dt)
        size = (size + 31) // 32 * 32
        off = self.top
        self.top += size
        assert self.top <= SB_HI, "SBUF overflow at %s: %d" % (name, self.top)
        self.uid += 1
        return self.nc.alloc_sbuf_tensor_at("%s_%d" % (name, self.uid), list(shape), dt, offset=off)

    def mark(self):
        return self.top

    def release(self, m):
        self.S.barrier()
        self.top = m

    @contextlib.contextmanager
    def bank(self):
        if self.chain is not None:
            c = self.chain_bank_cnt.get(self.chain, 0)
            self.chain_bank_cnt[self.chain] = c + 1
            bl = CHAIN_BANKS[self.chain]
            yield self.banks[bl[c % len(bl)]]
            return
        idx = self.free_banks.pop(0)
        try:
            yield self.banks[idx]
        finally:
            self.free_banks.append(idx)

    def mm(self, out, lhsT, rhs, start=True, stop=True):
        self.emit('pe', lambda e: e.matmul(out, lhsT=lhsT, rhs=rhs, start=start, stop=stop),
                  reads=[lhsT, rhs], writes=[out])

    def tr(self, out, in_, ident):
        self.emit('pe', lambda e: e.transpose(out, in_, ident), reads=[in_, ident], writes=[out])

    def act(self, out, in_, func, bias=None, scale=None):
        reads = [in_]
        kw = {}
        if bias is not None:
            kw['bias'] = bias
            if not isinstance(bias, (int, float)):
                reads.append(bias)
        if scale is not None:
            kw['scale'] = scale
            if not isinstance(scale, (int, float)):
                reads.append(scale)
        self.emit('act', lambda e: e.activation(out, in_, func, **kw), reads=reads, writes=[out])

    def cp(self, out, in_, eng='dve'):
        if eng == 'act':
            self.emit('act', lambda e: e.copy(out, in_), reads=[in_], writes=[out])
        else:
            self.emit(eng, lambda e: e.tensor_copy(out, in_), reads=[in_], writes=[out])

    def tt(self, out, a, b, op, eng='dve'):
        self.emit(eng, lambda e: e.tensor_tensor(out, a, b, op), reads=[a, b], writes=[out])

    def ts(self, out, a, s1, op0, s2=None, op1=None, eng='dve'):
        reads = [a] + [s for s in (s1, s2) if s is not None and not isinstance(s, (int, float))]
        if op1 is None:
            self.emit(eng, lambda e: e.tensor_scalar(out, a, s1, None, op0), reads=reads, writes=[out])
        else:
            self.emit(eng, lambda e: e.tensor_scalar(out, a, s1, s2, op0, op1), reads=reads, writes=[out])

    def stt(self, out, in0, scalar, in1, op0, op1):
        reads = [in0, in1] + ([] if isinstance(scalar, (int, float)) else [scalar])
        self.emit('dve', lambda e: e.scalar_tensor_tensor(out, in0, scalar, in1, op0, op1),
                  reads=reads, writes=[out])

    def scan(self, out, d0, d1):
        self.emit('dve', lambda e: e.tensor_tensor_scan(out, d0, d1, 0.0, ALU.mult, ALU.add),
                  reads=[d0, d1], writes=[out])

    def recip(self, out, in_):
        self.emit('dve', lambda e: e.reciprocal(out, in_), reads=[in_], writes=[out])

    def memset(self, out, val, eng='dve'):
        self.emit(eng, lambda e: e.memset(out, val), writes=[out])

    def dma(self, out, in_, eng='sp', cls='io', rd=None, wr=None):
        self.emit(eng, lambda e: e.dma_start(out=out, in_=in_),
                  reads=[in_] if rd is None else rd, writes=[out] if wr is None else wr, dma_class=cls)


def bc(ap, shape):
    return ap.to_broadcast(list(shape))


def _fm(v, nch):
    return np.ascontiguousarray(np.asarray(v, np.float32).reshape(nch, 128).T)


PV_LAYER = [('nmw', 8), ('nfw', 8), ('bada', 48), ('mu', 10), ('w0', 3), ('a0', 3), ('kk', 3), ('ka', 3),
            ('rk', 3), ('lnw', 3), ('lnb', 3), ('convw', 36), ('alog', 1), ('dtb', 1), ('gnw', 1),
            ('lbl', 2), ('hnw', 1), ('lora', 384)]
PV_OFF = {}
_o = 0
for _l in range(DEPTH):
    for _n, _w in PV_LAYER:
        PV_OFF[(_n, _l)] = (_o, _w)
        _o += _w
PV_OFF[('fnw', 0)] = (_o, 8)
_o += 8
NPV = _o


def pack_pvec(inp):
    pv = np.zeros((128, NPV), np.float32)

    def put(name, l, arr):
        o, w = PV_OFF[(name, l)]
        assert arr.shape == (128, w), (name, arr.shape, w)
        pv[:, o:o + w] = arr

    for l in range(DEPTH):
        put('nmw', l, _fm(inp['norm_mix_w'][l], 8))
        put('nfw', l, _fm(inp['norm_ffn_w'][l], 8))
        put('bada', l, _fm(inp['b_ada'][l], 48))
        put('mu', l, _fm(inp['rwkv_mu'][l], 10))
        put('w0', l, _fm(inp['rwkv_w0'][l], 3))
        put('a0', l, _fm(inp['rwkv_a0'][l], 3))
        put('kk', l, _fm(inp['rwkv_k_k'][l], 3))
        put('ka', l, _fm(inp['rwkv_k_a'][l], 3))
        put('rk', l, _fm(np.asarray(inp['rwkv_r_k'][l]).reshape(384), 3))
        put('lnw', l, _fm(inp['rwkv_ln_w'][l], 3))
        put('lnb', l, _fm(inp['rwkv_ln_b'][l], 3))
        cw = np.asarray(inp['gdn_conv_w'][l], np.float32).reshape(4, 9, 128)
        put('convw', l, np.ascontiguousarray(cw.transpose(2, 0, 1)).reshape(128, 36))
        t = np.zeros((128, 1), np.float32)
        t[0:6, 0] = inp['gdn_A_log'][l]
        put('alog', l, t)
        t = np.zeros((128, 1), np.float32)
        t[0:6, 0] = inp['gdn_dt_bias'][l]
        put('dtb', l, t)
        put('gnw', l, np.tile(np.asarray(inp['gdn_norm_w'][l], np.float32), 2).reshape(128, 1))
        put('lbl', l, _fm(inp['hgrn_lb_logits'][l], 2))
        put('hnw', l, np.tile(np.asarray(inp['hgrn_norm_w'][l], np.float32), 2).reshape(128, 1))
        lo = np.zeros((128, 384), np.float32)
        lo[0:32] = inp['rwkv_w2'][l]
        lo[32:64] = inp['rwkv_a2'][l]
        lo[64:128] = inp['rwkv_g2'][l]
        put('lora', l, lo)
    put('fnw', 0, _fm(inp['final_norm_w'], 8))
    return pv


CST_ITEMS = [('ident', 128), ('bones', 128), ('ones', 128), ('ident2', 64),
             ('mask5_p', 320), ('IU_p', 64), ('BSL_p', 64), ('BIU_p', 64), ('BS_p', 64),
             ('mask5_s', 320), ('IU_s', 64), ('BSL_s', 64), ('BIU_s', 64), ('BS_s', 64),
             ('ind', 16), ('osel', 6 * 64), ('selb', 3 * 128), ('rm_p', 384), ('rm_s', 192)]
CST_OFF = {}
_o = 0
for _n, _w in CST_ITEMS:
    CST_OFF[_n] = (_o, _w)
    _o += _w
NCST = _o


def make_consts():
    c = np.zeros((128, NCST), np.float32)

    def put(name, arr):
        o, w = CST_OFF[name]
        assert arr.shape[1] == w, (name, arr.shape, w)
        c[0:arr.shape[0], o:o + w] = arr

    def rep(a):
        return np.concatenate([a, a], axis=0)

    put('ident', np.eye(128, dtype=np.float32))
    p = np.arange(128)
    put('bones', (p[:, None] // 64 == p[None, :] // 64).astype(np.float32))
    put('ones', np.ones((128, 128), np.float32))
    put('ident2', rep(np.eye(64, dtype=np.float32)))
    i = np.arange(64)
    for v, seg in (('p', 64), ('s', 4)):
        same = (i[:, None] // seg == i[None, :] // seg)
        SU = ((i[:, None] < i[None, :]) & same).astype(np.float32)
        IU = ((i[:, None] <= i[None, :]) & same).astype(np.float32)
        SL = ((i[:, None] > i[None, :]) & same).astype(np.float32)
        put('mask5_' + v, rep(np.concatenate([SU, SU, IU, IU, SL], axis=1)))
        put('IU_' + v, rep(IU))
        put('BSL_' + v, rep(BIG * (1.0 - SL)))
        put('BIU_' + v, rep(BIG * (1.0 - IU)))
        put('BS_' + v, rep(same.astype(np.float32)))
    put('ind', rep((i[:, None] // 4 == np.arange(16)[None, :]).astype(np.float32)))
    osel = np.zeros((6, 6, 64), np.float32)
    for hh in range(6):
        osel[hh, hh, :] = 1.0
    put('osel', osel.reshape(6, 6 * 64))
    selb = np.zeros((6, 3, 128), np.float32)
    for hp in range(3):
        for m in range(128):
            selb[2 * hp + m // 64, hp, m] = 1.0
    put('selb', selb.reshape(6, 3 * 128))
    col = np.arange(384)
    put('rm_p', np.tile((col % 64 != 0).astype(np.float32)[None, :], (128, 1)))
    col = np.arange(192)
    put('rm_s', np.tile((col % 4 != 0).astype(np.float32)[None, :], (128, 1)))
    return c


class Blk:
    def __init__(self, kind, idx, n=None):
        self.kind = kind
        self.idx = idx
        if kind == 'p':
            self.n = BLK if n is None else n
            self.nsq = 1
            self.tb = self.n
            self.ntile = self.n // 64
            self.nseg = 1
            self.tok0 = idx * self.n
            self.w = 3 + self.n
            self.v = 'p'
            self.levels = 5
        else:
            self.n = TOKS
            self.nsq = NSEQ
            self.tb = TSS
            self.ntile = 1
            self.nseg = NSEQ
            self.tok0 = TOKP
            self.w = NSEQ * 7
            self.v = 's'
            self.levels = 1
        self.seglen = 64 // self.nseg


def build_program(dbg_names=(), stop_after=None):
    nc = bass.Bass("TRN2", target_bir_lowering=False)
    dr = {}

    def din(name, shape):
        dr[name] = nc.dram_tensor(name, list(shape), F32, kind="ExternalInput").ap()

    def dout(name, shape):
        dr[name] = nc.dram_tensor(name, list(shape), F32, kind="ExternalOutput").ap()

    din('xp', [TOKP, D]); din('xs', [TOKS, D]); din('c17', [17, D])
    din('shift_s', [DEPTH, NSEQ, D]); din('srw', [DEPTH, NSEQ, 6, 64, 64]); din('sconv', [DEPTH, NSEQ, 3, 1152])
    din('sgdn', [DEPTH, NSEQ, 6, 64, 64]); din('shg', [DEPTH, NSEQ, 4, 64, 64])
    din('w_ada', [DEPTH, D, 6 * D]); din('w_in', [DEPTH, D, IN_COLS]); din('w_out', [DEPTH, D, D])
    din('w_fi', [DEPTH, D, 2 * FF]); din('w_fo', [DEPTH, FF, D])
    din('pvec', [128, NPV]); din('cst', [128, NCST])
    dout('yp', [TOKP, D]); dout('ys', [TOKS, D])
    dout('o_pshift', [DEPTH, 1, D]); dout('o_prw', [DEPTH, 1, 6, 64, 64]); dout('o_pconv', [DEPTH, 1, 3, 1152])
    dout('o_pgdn', [DEPTH, 1, 6, 64, 64]); dout('o_phg', [DEPTH, 1, 4, 64, 64])
    dout('o_sshift', [DEPTH, NSEQ, D]); dout('o_srw', [DEPTH, NSEQ, 6, 64, 64]); dout('o_sconv', [DEPTH, NSEQ, 3, 1152])
    dout('o_sgdn', [DEPTH, NSEQ, 6, 64, 64]); dout('o_shg', [DEPTH, NSEQ, 4, 64, 64])
    xscr = nc.dram_tensor("xscr", [128, KC, TOKP + TOKS], F32, kind="Internal").ap()
    dbg_list = []
    dbg_t = {}
    for dn, dshape in dbg_names:
        dbg_t[dn] = nc.dram_tensor("dbg_" + dn, list(dshape), F32, kind="ExternalOutput").ap()
    out_keys = []

    with contextlib.ExitStack() as stack:
        B = KB(nc, stack)
        S = B.S
        okc = [0]

        def odma(dst, src):
            okc[0] += 1
            key = "out#%d" % okc[0]
            B.dma(dst, src, cls='out', wr=[key])

        if stop_after is not None and stop_after.startswith('#'):
            S.max_ops = int(stop_after[1:])

        def stage(name):
            if stop_after is not None and stop_after == name:
                S.disabled = True

        def dbg(name, ap, shape=None):
            if name in dbg_t and name not in dbg_list:
                dbg_list.append(name)
                odma(dbg_t[name], ap)

        cst = B.sb("cst", [128, NCST])
        pvec = B.sb("pvec", [128, NPV])
        modT = B.sb("modT", [128, DEPTH, 48, 17])
        AMt = B.sb("AMt", [128, DEPTH, 8, 17])
        AFt = B.sb("AFt", [128, DEPTH, 8, 17])
        LBt = B.sb("LBt", [128, DEPTH, 2])
        OMLt = B.sb("OMLt", [128, DEPTH, 2])
        B.dma(cst[:], dr['cst'])
        B.dma(pvec[:], dr['pvec'])

        def C(name, rows=128, c0=0, c1=None):
            o, w = CST_OFF[name]
            c1 = w if c1 is None else c1
            return cst[0:rows, o + c0:o + c1]

        def PV(name, l, rows=128, c0=0, c1=None):
            o, w = PV_OFF[(name, l)]
            c1 = w if c1 is None else c1
            return pvec[0:rows, o + c0:o + c1]

        ident = C('ident')
        ones = C('ones')
        bones = C('bones')
        evac_rr = [0]

        def evac(out, in_):
            evac_rr[0] += 1
            B.cp(out, in_, eng='act')

        def rsqrt_ln(out, in_, scale=1.0, bias=0.0):
            B.act(out, in_, AF.Ln, bias=bias, scale=scale)
            B.act(out, out, AF.Exp, scale=-0.5)

        def sig_finish(t, eng='dve'):
            B.ts(t, t, 1.0, ALU.add, eng=eng)
            B.recip(t, t)

        def silu_exp(out, x, tmp):
            B.act(tmp, x, AF.Exp, scale=-1.0)
            sig_finish(tmp)
            B.tt(out, x, tmp, ALU.mult)

        m0 = B.mark()
        c17t = B.sb("c17t", [17, D])
        cact = B.sb("cact", [17, D])
        cT = B.sb("cT", [128, KC, 17])
        wada = [B.sb("wada%d" % i, [128, KC, 512]) for i in range(4)]
        msm = [B.sb("msm%d" % i, [17, 512]) for i in range(2)]
        B.dma(c17t[:], dr['c17'])
        B.act(cact[:], c17t[:], AF.Silu)
        with B.bank() as pb:
            for kc in range(KC):
                B.tr(pb[:, kc * 17:(kc + 1) * 17], cact[0:17, kc * 128:(kc + 1) * 128], ident[0:17, 0:17])
            B.cp(cT[:].rearrange("p k s -> p (k s)"), pb[:, 0:KC * 17])
        stage('p1')
        nblk = 0
        for l in range(DEPTH):
            wsrc = dr['w_ada'][l].rearrange("(kc p) n -> p kc n", p=128)
            for cb in range(12):
                wt = wada[nblk % 4]
                ms = msm[nblk % 2]
                B.dma(wt[:], wsrc[:, :, cb * 512:(cb + 1) * 512], cls='w')
                nblk += 1
                with B.bank() as pb:
                    for kc in range(KC):
                        B.mm(pb[0:17, :], cT[:, kc, :], wt[:, kc, :], start=(kc == 0), stop=(kc == KC - 1))
                    B.cp(ms[:], pb[0:17, :], eng='act')
                with B.bank() as pb:
                    for q in range(4):
                        B.tr(pb[:, q * 17:(q + 1) * 17], ms[0:17, q * 128:(q + 1) * 128], ident[0:17, 0:17])
                    B.tt(modT[:, l, 4 * cb:4 * cb + 4, :], pb[:, 0:68].rearrange("p (q s) -> p q s", s=17),
                         bc(PV('bada', l, c0=4 * cb, c1=4 * cb + 4).unsqueeze(2), [128, 4, 17]), ALU.add)
                stage('p2_%d_%d' % (l, cb))
            B.ts(AMt[:, l], modT[:, l, 8:16, :], 1.0, ALU.add)
            B.tt(AMt[:, l], AMt[:, l], bc(PV('nmw', l).unsqueeze(2), [128, 8, 17]), ALU.mult)
            B.ts(AFt[:, l], modT[:, l, 32:40, :], 1.0, ALU.add)
            B.tt(AFt[:, l], AFt[:, l], bc(PV('nfw', l).unsqueeze(2), [128, 8, 17]), ALU.mult)
        B.memset(LBt[:], 0.0)
        B.tt(LBt[:, 1, :], PV('lbl', 1), PV('lbl', 0), ALU.subtract)
        B.act(LBt[:, 1, :], LBt[:, 1, :], AF.Sigmoid)
        B.ts(OMLt[:], LBt[:], -1.0, ALU.mult, 1.0, ALU.add)
        dbg('modT', modT[:].rearrange("p l m s -> p (l m s)"), [128, DEPTH * 48 * 17])
        B.release(m0)
        stage('prologue')

        def MOD(l, which):
            if which == 'AM':
                return AMt[:, l]
            if which == 'AF':
                return AFt[:, l]
            lo = {'BM': 0, 'GM': 16, 'BF': 24, 'GF': 40}[which]
            return modT[:, l, lo:lo + 8, :]

        def seqcols(blk):
            return (0, 1) if blk.kind == 'p' else (1, 17)

        def v3(ap, blk):
            return ap.rearrange("p (s t) -> p s t", t=blk.tb)

        def rms_stats(xb, n, sq, rstd):
            with B.bank() as pb:
                for kc in range(KC):
                    B.act(sq[kc % 2][:, 0:n], xb[:, kc, 0:n], AF.Square)
                    B.mm(pb[:, 0:n], ones, sq[kc % 2][:, 0:n], start=(kc == 0), stop=(kc == KC - 1))
                rsqrt_ln(rstd[:, 0:n], pb[:, 0:n], scale=1.0 / D, bias=1e-6)

        def norm_mod(blk, xb, n, l, Aw, Bw, hdst_fn, sq, rstd, tmpn):
            rms_stats(xb, n, sq, rstd)
            s0, s1 = seqcols(blk)
            A = MOD(l, Aw)
            Bm = MOD(l, Bw)
            for kc in range(KC):
                tn = tmpn[kc % 2]
                B.tt(tn[:, 0:n], xb[:, kc, 0:n], rstd[:, 0:n], ALU.mult, eng='pool')
                if blk.kind == 'p':
                    B.act(hdst_fn(kc), tn[:, 0:n].rearrange("p (s t) -> p s t", s=1), AF.Identity,
                          bias=Bm[:, kc, s0:s1], scale=A[:, kc, s0:s1])
                else:
                    t3 = v3(tn[:, 0:n], blk)
                    B.tt(t3, t3, bc(A[:, kc, s0:s1].unsqueeze(2), [128, blk.nsq, blk.tb]), ALU.mult)
                    B.tt(hdst_fn(kc), t3, bc(Bm[:, kc, s0:s1].unsqueeze(2), [128, blk.nsq, blk.tb]), ALU.add)

        def gate_res(blk, xdst, pbsrc, l, Gw, kc, n, tmpn):
            s0, s1 = seqcols(blk)
            G = MOD(l, Gw)
            if blk.kind == 'p':
                B.stt(xdst, pbsrc, G[:, kc, s0:s1], xdst, ALU.mult, ALU.add)
            else:
                tn = tmpn[kc % 2]
                t3 = v3(tn[:, 0:n], blk)
                B.tt(t3, v3(pbsrc, blk), bc(G[:, kc, s0:s1].unsqueeze(2), [128, blk.nsq, blk.tb]), ALU.mult)
                B.tt(xdst, xdst, tn[:, 0:n], ALU.add)

        prompt_blocks = [Blk('p', i) for i in range(TOKP // BLK)]
        sample_block = Blk('s', 0)
        ffn_blocks = [Blk('p', i, n=FBLK) for i in range(TOKP // FBLK)] + [Blk('s', 0)]

        def xkey(phase, lay, tok0, n):
            return ["xscr#%d" % g for g in range(tok0 // 64, (tok0 + n + 63) // 64)]

        hsl = [slice(0, 64), slice(64, 128)]

        for l in range(DEPTH):
            mph = B.mark()
            WIN = [B.sb("w_inA", [128, KC, A_COLS], BF16), B.sb("w_inB", [128, KC, B_COLS], BF16),
                   B.sb("w_inC", [128, KC, C_COLS], BF16)]
            WIN_OFF = [0, A_COLS, A_COLS + B_COLS, IN_COLS]
            w_out_sb = B.sb("w_out", [128, KC, D], BF16)
            wsrc = dr['w_in'][l].rearrange("(kc p) n -> p kc n", p=128)
            for gi in range(3):
                for kc in range(0, KC, 4):
                    B.dma(WIN[gi][:, kc:kc + 4, :], wsrc[:, kc:kc + 4, WIN_OFF[gi]:WIN_OFF[gi + 1]], eng='pool', cls='w')

            def w_in_ap(kc, c0, m):
                gi = 0 if c0 < WIN_OFF[1] else (1 if c0 < WIN_OFF[2] else 2)
                assert c0 + m <= WIN_OFF[gi + 1]
                return WIN[gi][:, kc, c0 - WIN_OFF[gi]:c0 - WIN_OFF[gi] + m]
            wsrc = dr['w_out'][l].rearrange("(kc p) n -> p kc n", p=128)
            for kc in range(0, KC, 4):
                B.dma(w_out_sb[:, kc:kc + 4, :], wsrc[:, kc:kc + 4, :], eng='pool', cls='w')

            xb = B.sb("xb", [128, KC, BLK])
            STAGE = B.sb("STAGE", [128, 1152])
            xtm = STAGE[:, 0:D]
            sq = [B.sb("sq%d" % i, [128, BLK]) for i in range(2)]
            rstd = B.sb("rstd", [128, BLK])
            tmpn = [B.sb("tmpn%d" % i, [128, BLK]) for i in range(2)]
            hT_p = B.sb("hT_p", [128, KC, 8 + BLK], BF16)
            hT_s = B.sb("hT_s", [128, KC, NSEQ * 5], BF16)
            PB = B.sb("PB", [128, 10, 3 + BLK])
            ZB = B.sb("ZB", [128, 3, BLK])
            BETA = B.sb("BETA", [6, BLK])
            APRE = B.sb("APRE", [6, BLK])
            yT = B.sb("yT", [128, 8, BLK], BF16)
            SL3 = [B.sb("slot%d" % i, [128, 3, BLK]) for i in range(12)]
            XS = B.sb("XS", [128, 10, BLK])
            TL = B.sb("TL", [128, BLK])
            class _TS:
                pass
            NT = BLK // 64
            TSets = []
            for ci in range(3):
                t_ = _TS()
                t_.TM = B.sb("TM", [128, NT, 4, 64])
                t_.A4 = B.sb("A4", [128, NT, 4, 64])
                t_.NM = B.sb("NM", [128, NT, 64])
                t_.Pm = B.sb("Pm", [128, NT, 64])
                t_.NQ = [B.sb("NQ%d" % i, [128, NT, 2, 64]) for i in range(2)]
                t_.WTs = B.sb("WTs", [128, NT, 64])
                t_.Xs = B.sb("Xs", [128, NT, 64])
                t_.U0s = B.sb("U0s", [128, NT, 64])
                t_.U0Ts = t_.U0s
                t_.Us = B.sb("Us", [128, 64])
                t_.UTs = B.sb("UTs", [128, 64])
                t_.SCP = B.sb("SCP", [128, NT, 4])
                t_.SCD = B.sb("SCD", [128, NT, 4])
                t_.GSL = B.sb("GSL", [128, NT, 64])
                t_.GUI = B.sb("GUI", [128, NT, 64])
                t_.Qs = B.sb("Qs", [128, NT, 64])
                t_.KB3 = B.sb("KB3", [128, NT, 3, 64])
                t_.EBC = B.sb("EBC", [128, NT, 64])
                t_.QTg = B.sb("QTg", [128, NT, 64])
                TSets.append(t_)
            TB = _TS()

            def use_set(ci):
                TB.__dict__.update(TSets[ci].__dict__)
            use_set(0)
            SVM = B.sb("SVM", [128, 32, 64])
            UM = SVM[:, 0:16, :]
            VM = SVM[:, 16:32, :]
            Mp = {}
            for mx, npair in (('a', 3), ('b', 3), ('c', 2)):
                for hp in range(npair):
                    Mp[(mx, hp)] = B.sb("Mp_%s%d" % (mx, hp), [128, 1, 64])
            Ms = B.sb("Ms", [128, NSEQ, 64])
            STG = SVM[0:64, :, :].rearrange("p (s h) k -> p s h k", h=2)
            OSH = xtm[0:16, :]
            OCV = STAGE[0:48, :]
            G6 = B.sb("G6", [6, BLK])
            GC6 = B.sb("GC6", [6, BLK])
            BE6 = B.sb("BE6", [6, BLK])
            EA6 = B.sb("EA6", [6, 1])

            NEGB = B.sb("NEGB", [128, 6])
            B.ts(NEGB[:, 0:3], PV('w0', l), -1.0, ALU.mult)
            B.ts(NEGB[:, 3:6], PV('a0', l), -1.0, ALU.mult)
            B.memset(TL[:], 0.0, eng='pool')
            B.memset(hT_p[:], 0.0)
            for key in Mp:
                B.memset(Mp[key][:], 0.0, eng='pool')

            def load_x_block(blk):
                n = blk.n
                if l == 0:
                    if blk.kind == 'p':
                        for ti in range(n // 128):
                            B.dma(xtm[:], dr['xp'][blk.tok0 + ti * 128: blk.tok0 + (ti + 1) * 128, :])
                            for g in range(2):
                                with B.bank() as pb:
                                    for q in range(4):
                                        kc = 4 * g + q
                                        B.tr(pb[:, q * 128:(q + 1) * 128], xtm[:, kc * 128:(kc + 1) * 128], ident)
                                    B.cp(xb[:, 4 * g:4 * g + 4, ti * 128:(ti + 1) * 128],
                                         pb[:, 0:512].rearrange("p (q t) -> p q t", q=4), eng=('act' if g else 'dve'))
                    else:
                        B.dma(xtm[0:64, :], dr['xs'])
                        with B.bank() as pb:
                            for kc in range(KC):
                                B.tr(pb[:, kc * 64:(kc + 1) * 64], xtm[0:64, kc * 128:(kc + 1) * 128], ident[0:64, 0:64])
                            B.cp(xb[:, :, 0:64], pb[:, 0:512].rearrange("p (q t) -> p q t", q=8))
                else:
                    B.dma(xb[:, :, 0:n], xscr[:, :, blk.tok0:blk.tok0 + n], rd=xkey(0, l, blk.tok0, n))

            def PBv(blk, j0, j1, off):
                if blk.kind == 'p':
                    return PB[:, j0:j1, off:off + blk.tb].unsqueeze(2)
                return PB[:, j0:j1, 0:blk.w].rearrange("p j (s u) -> p j s u", u=7)[:, :, :, off:off + blk.tb]

            def c4(ap, blk):
                return ap.rearrange("p j (s t) -> p j s t", t=blk.tb)

            def proj(blk, cols, dst_fn):
                c0, m = cols
                with B.bank() as pb:
                    src = hT_p if blk.kind == 'p' else hT_s
                    N = 4 + blk.n if blk.kind == 'p' else NSEQ * 5
                    for kc in range(KC):
                        B.mm(pb[0:m, 0:N], w_in_ap(kc, c0, m), src[:, kc, 0:N],
                             start=(kc == 0), stop=(kc == KC - 1))
                    dst_fn(pb)

            def proj_to_PB(blk, cols, j, halo):
                def f(pb):
                    m = cols[1]
                    if blk.kind == 'p':
                        evac(PB[0:m, j, 3 - halo:3 + blk.n], pb[0:m, 3 - halo:3 + blk.n])
                    else:
                        h = 1 if halo == 1 else 0
                        src = pb[0:m, 0:NSEQ * 5].rearrange("p (s u) -> p s u", u=5)[:, :, 1 - h:5]
                        dst = PB[0:m, j, 0:blk.w].rearrange("p (s u) -> p s u", u=7)[:, :, 3 - h:7]
                        evac(dst, src)
                proj(blk, cols, f)

            def proj_to(blk, cols, dst):
                def f(pb):
                    m = cols[1]
                    if blk.kind == 'p':
                        evac(dst, pb[0:m, 3:3 + blk.n])
                    else:
                        src = pb[0:m, 0:NSEQ * 5].rearrange("p (s u) -> p s u", u=5)[:, :, 1:5]
                        evac(dst.rearrange("p (s t) -> p s t", t=4), src)
                proj(blk, cols, f)

            def wy_double(blk, N0, Q0):
                nt = blk.ntile
                B.tt(TB.Pm[:, 0:nt, :], Q0, bc(C('ident2').unsqueeze(1), [128, nt, 64]), ALU.add)
                Ncur = N0
                Qcur = Q0
                for lev in range(blk.levels):
                    last = (lev == blk.levels - 1)
                    nq = TB.NQ[lev % 2]
                    with B.bank() as pb:
                        pv = pb[:, 0:nt * 128].rearrange("p (n a t) -> p n a t", n=nt, a=2)
                        for ti in range(nt):
                            for h2 in range(2):
                                hs = hsl[h2]
                                B.mm(pv[hs, ti, 0, :], Qcur[hs, ti, :], Ncur[hs, ti, :])
                                if not last:
                                    B.mm(pv[hs, ti, 1, :], Ncur[hs, ti, :], Qcur[hs, ti, :])
                        if last:
                            evac(nq[:, 0:nt, 0, :], pv[:, :, 0, :])
                        else:
                            evac(nq[:, 0:nt], pv)
                    with B.bank() as pb:
                        pv = pb[:, 0:nt * 64].rearrange("p (n t) -> p n t", n=nt)
                        for ti in range(nt):
                            for h2 in range(2):
                                hs = hsl[h2]
                                B.mm(pv[hs, ti, :], nq[hs, ti, 0, :], TB.Pm[hs, ti, :])
                        B.tt(TB.Pm[:, 0:nt, :], TB.Pm[:, 0:nt, :], pv, ALU.add)
                    Ncur = nq[:, 0:nt, 0, :]
                    Qcur = nq[:, 0:nt, 1, :]

            def seq_step(blk, M, WT, U0, U0T, ArbT, y_extra, rt_fn, b_tm, upd_extra, DC_fn, mode, ydst,
                         Vmask=None):
                nseg = blk.nseg
                sl = blk.seglen
                if WT is not None:
                    if blk.kind == 'p':
                        with B.bank() as pb:
                            for h2 in range(2):
                                hs = hsl[h2]
                                B.mm(pb[hs, 0:64], WT[hs, :], M[hs, 0, :])
                            B.tt(TB.Us[:], pb[:, 0:64], U0[:], ALU.add)
                    else:
                        with B.bank() as pb:
                            for h2 in range(2):
                                hs = hsl[h2]
                                for sg in range(nseg):
                                    B.mm(pb[hs, sg * sl:(sg + 1) * sl], M[hs, sg, :], WT[hs, sg * sl:(sg + 1) * sl])
                            B.tt(TB.UTs[:], pb[:, 0:64], U0T[:], ALU.add)
                        with B.bank() as pb:
                            for h2 in range(2):
                                hs = hsl[h2]
                                B.mm(pb[hs, 0:64], TB.UTs[hs, :], ident[hs, hs])
                            evac(TB.Us[:], pb[:, 0:64])
                with B.bank() as pb:
                    for h2 in range(2):
                        hs = hsl[h2]
                        first = True
                        if WT is not None:
                            B.mm(pb[hs, 0:64], TB.Us[hs, :], ArbT[hs, :], start=True, stop=False)
                            first = False
                        if y_extra is not None:
                            B.mm(pb[hs, 0:64], y_extra[0][hs, :], y_extra[1][hs, :], start=first, stop=False)
                            first = False
                        for sg in range(nseg):
                            cc = slice(sg * sl, (sg + 1) * sl)
                            B.mm(pb[hs, cc], M[hs, sg, :], rt_fn(cc)[hs, :], start=False, stop=True)
                    evac(ydst, pb[:, 0:64])
                nb = (nseg * 64 + 511) // 512
                with B.bank() as pb0, B.bank() as pb1:
                    pbs = [pb0, pb1]
                    if blk.kind == 's' and WT is not None:
                        B.tt(UM[:], bc(TB.Us[:].unsqueeze(1), [128, 16, 64]),
                             bc(C('ind').unsqueeze(2), [128, 16, 64]), ALU.mult, eng='pool')
                    for h2 in range(2):
                        hs = hsl[h2]
                        for half in range(nb):
                            ncols = min(512, nseg * 64 - half * 512)
                            g0 = half * 8
                            first = True
                            if WT is not None:
                                if blk.kind == 'p':
                                    rhsU = TB.Us[hs, :]
                                else:
                                    rhsU = UM[hs, g0:g0 + 8, :].rearrange("p s v -> p (s v)")
                                B.mm(pbs[half][hs, 0:ncols], b_tm[hs, :], rhsU, start=True, stop=(upd_extra is None))
                                first = False
                            if upd_extra is not None:
                                lt, rv = upd_extra
                                if blk.kind == 'p':
                                    rhsV = rv[hs, :]
                                else:
                                    rhsV = Vmask[hs, g0:g0 + 8, :].rearrange("p s v -> p (s v)")
                                B.mm(pbs[half][hs, 0:ncols], lt[hs, :], rhsV, start=first, stop=True)
                    DC = DC_fn()
                    for half in range(nb):
                        nsg = min(8, nseg - half * 8)
                        g0 = half * 8
                        Mv = M[:, g0:g0 + nsg, :]
                        pv = pbs[half][:, 0:nsg * 64].rearrange("p (s v) -> p s v", v=64)
                        DCb = bc(DC[:, g0:g0 + nsg].unsqueeze(2), [128, nsg, 64])
                        if mode == 'rwkv':
                            B.tt(Mv, Mv, pv, ALU.add)
                            B.tt(Mv, Mv, DCb, ALU.mult, eng='pool')
                        elif blk.kind == 'p':
                            B.stt(M[:, 0, :], M[:, 0, :], DC[:, 0:1], pbs[half][:, 0:64], ALU.mult, ALU.add)
                        else:
                            B.tt(Mv, Mv, DCb, ALU.mult, eng='pool')
                            B.tt(Mv, Mv, pv, ALU.add)

            def segend(ap2d, blk, cs):
                t = ap2d[:, cs]
                return t.rearrange("p (s u) -> p s u", u=blk.seglen)[:, :, blk.seglen - 1]

            def vmask_build(v_tm):
                B.tt(VM[:], bc(v_tm.unsqueeze(1), [128, 16, 64]),
                     bc(C('ind').unsqueeze(2), [128, 16, 64]), ALU.mult, eng='pool')

            def state_io_in(mx, hp, src):
                if mx == 'a':
                    for h2 in range(2):
                        B.dma(STG[:, :, h2, :], src[:, 2 * hp + h2, :, :].rearrange("s v k -> v s k"))
                    for g in range(2):
                        with B.bank() as pb:
                            for q in range(8):
                                sg = 8 * g + q
                                B.tr(pb[:, q * 64:(q + 1) * 64], STG[:, sg].rearrange("p h k -> p (h k)"),
                                     ident[0:64, 0:64])
                            evac(Ms[:, 8 * g:8 * g + 8, :].rearrange("p s v -> p (s v)"), pb[:, 0:512])
                else:
                    for h2 in range(2):
                        B.dma(Ms[64 * h2:64 * h2 + 64, :, :], src[:, 2 * hp + h2, :, :].rearrange("s k v -> k s v"))

            def state_io_out(hp, M, nsq, dst):
                for h2 in range(2):
                    odma(dst[:, 2 * hp + h2, :, :].rearrange("s k v -> k s v"), M[64 * h2:64 * h2 + 64, 0:nsq, :])

            def state_out_rwkv(hp, M, nsq, dst):
                so = hp if nsq == 1 else 0
                for g in range((nsq + 3) // 4):
                    ns = min(4, nsq - 4 * g)
                    with B.bank() as pb:
                        for q in range(ns):
                            B.tr(pb[0:64, q * 128:(q + 1) * 128], M[:, 4 * g + q, :], ident)
                        evac(STG[:, so + 4 * g:so + 4 * g + ns].rearrange("p s h k -> p (s h k)"), pb[0:64, 0:ns * 128])
                for h2 in range(2):
                    odma(dst[:, 2 * hp + h2, :, :].rearrange("s v k -> v s k"), STG[:, so:so + nsq, h2, :])

            def run_pairs(blk, fn, npair):
                if blk.kind == 's':
                    for hp in range(npair):
                        use_set(0)
                        fn(hp)
                    return
                for hp in range(npair):
                    use_set(hp)
                    with B.in_chain(hp):
                        fn(hp)
                B.flush_chains()
                use_set(0)

            def process_block(blk, is_last):
                n = blk.n
                v = blk.v
                T = blk.tb
                nsq = blk.nsq
                load_x_block(blk)
                if blk.kind == 'p':
                    hdst = lambda kc: hT_p[:, kc, 3:3 + n].rearrange("p (s t) -> p s t", s=1)
                else:
                    B.dma(OSH[:], dr['shift_s'][l])
                    with B.bank() as pb:
                        for kc in range(KC):
                            B.tr(pb[:, kc * 16:(kc + 1) * 16], OSH[0:16, kc * 128:(kc + 1) * 128], ident[0:16, 0:16])
                        B.cp(hT_s[:].rearrange("p k (s u) -> p k s u", u=5)[:, :, :, 0],
                             pb[:, 0:128].rearrange("p (k s) -> p k s", s=16))
                    hdst = lambda kc: hT_s[:, kc, :].rearrange("p (s u) -> p s u", u=5)[:, :, 1:5]
                if (l, blk.kind, blk.idx) == DBG_AT:
                    dbg('xin', xb[:].rearrange("p c t -> p (c t)"))
                norm_mod(blk, xb, n, l, 'AM', 'BM', hdst, sq, rstd, tmpn)
                if (l, blk.kind, blk.idx) == DBG_AT:
                    dbg('rstd', rstd[:])
                stage('h_%s%d_%d' % (blk.kind, blk.idx, l))

                for j in range(10):
                    proj_to_PB(blk, (128 * j, 128), j, 1)
                (SG, AA, G_, KK, K2, BBt, CUM, EP, EM, EPV, T1) = SL3[0:11]
                KT, BT_, AT = K2, BBt, KK
                cur = PBv(blk, 0, 10, 3)
                prv = PBv(blk, 0, 10, 2)
                XS4 = c4(XS[:, :, 0:n], blk)
                mu4 = bc(PV('mu', l).unsqueeze(2).unsqueeze(3), [128, 10, nsq, T])
                B.tt(XS4, prv, cur, ALU.subtract, eng='pool')
                B.tt(XS4, XS4, mu4, ALU.mult)
                B.tt(XS4, XS4, cur, ALU.add, eng='pool')
                if blk.kind == 'p' and blk.idx == 0 and l == 0:
                    dbg('XS0', XS[:].rearrange("p j t -> p (j t)"), [128, 10 * BLK])
                if (l, blk.kind, blk.idx) == DBG_AT:
                    dbg('XSa', XS[:].rearrange("p j t -> p (j t)"))
                B.act(TL[0:32, 0:n], XS[0:32, 9, 0:n], AF.Exp, scale=2.0)
                B.act(TL[64:128, 0:n], XS[64:128, 9, 0:n], AF.Exp, scale=-1.0)
                B.ts(TL[:, 0:n], TL[:, 0:n], 1.0, ALU.add)
                B.recip(TL[0:32, 0:n], TL[0:32, 0:n])
                B.recip(TL[64:128, 0:n], TL[64:128, 0:n])
                B.ts(TL[0:32, 0:n], TL[0:32, 0:n], -2.0, ALU.mult, 1.0, ALU.add)
                lora = PV('lora', l)
                for c in range(3):
                    with B.bank() as pb:
                        B.mm(pb[:, 0:n], lora[0:32, c * 128:(c + 1) * 128], TL[0:32, 0:n])
                        B.act(SG[:, c, 0:n], pb[:, 0:n], AF.Exp, bias=NEGB[:, c:c + 1], scale=-1.0)
                    with B.bank() as pb:
                        B.mm(pb[:, 0:n], lora[32:64, c * 128:(c + 1) * 128], XS[32:64, 9, 0:n])
                        B.act(AA[:, c, 0:n], pb[:, 0:n], AF.Exp, bias=NEGB[:, 3 + c:4 + c], scale=-1.0)
                    with B.bank() as pb:
                        B.mm(pb[:, 0:n], lora[64:128, c * 128:(c + 1) * 128], TL[64:128, 0:n])
                        evac(G_[:, c, 0:n], pb[:, 0:n])
                sig_finish(SG[:, :, 0:n])
                sig_finish(AA[:, :, 0:n], eng='pool')
                for c in range(3):
                    B.scan(CUM[:, c, 0:n], C('rm_' + v, c1=n), SG[:, c, 0:n])
                B.act(EP[:, :, 0:n], CUM[:, :, 0:n], AF.Exp, scale=-C0)
                B.act(EM[:, :, 0:n], CUM[:, :, 0:n], AF.Exp, scale=C0)
                B.tt(EPV[:, :, 0:n], CUM[:, :, 0:n], SG[:, :, 0:n], ALU.subtract, eng='pool')
                B.act(EPV[:, :, 0:n], EPV[:, :, 0:n], AF.Exp, scale=-C0)
                r_ = XS[:, 0:3, 0:n]
                k_ = XS[:, 3:6, 0:n]

                def b3(name):
                    return bc(PV(name, l).unsqueeze(2), [128, 3, n])
                B.tt(KK[:, :, 0:n], k_, b3('kk'), ALU.mult)
                B.act(T1[:, :, 0:n], KK[:, :, 0:n], AF.Square)
                for c in range(3):
                    with B.bank() as pb:
                        B.mm(pb[:, 0:n], bones, T1[:, c, 0:n])
                        rsqrt_ln(T1[:, c, 0:n], pb[:, 0:n], bias=1e-6)
                B.tt(KK[:, :, 0:n], KK[:, :, 0:n], T1[:, :, 0:n], ALU.mult)
                B.tt(K2[:, :, 0:n], AA[:, :, 0:n], b3('ka'), ALU.mult, eng='pool')
                B.tt(K2[:, :, 0:n], K2[:, :, 0:n], b3('ka'), ALU.subtract, eng='pool')
                B.stt(K2[:, :, 0:n], K2[:, :, 0:n], 1.0, k_, ALU.add, ALU.mult)
                B.tt(BBt[:, :, 0:n], KK[:, :, 0:n], AA[:, :, 0:n], ALU.mult, eng='pool')
                BON = AA
                B.tt(T1[:, :, 0:n], r_, K2[:, :, 0:n], ALU.mult)
                B.tt(T1[:, :, 0:n], T1[:, :, 0:n], b3('rk'), ALU.mult, eng='pool')
                for c in range(3):
                    with B.bank() as pb:
                        B.mm(pb[:, 0:n], bones, T1[:, c, 0:n])
                        B.tt(BON[:, c, 0:n], pb[:, 0:n], XS[:, 6 + c, 0:n], ALU.mult)
                RT = XS
                B.tt(RT[:, 0:3, 0:n], r_, EP[:, :, 0:n], ALU.mult)
                B.tt(KT[:, :, 0:n], K2[:, :, 0:n], EM[:, :, 0:n], ALU.mult, eng='pool')
                B.tt(BT_[:, :, 0:n], BBt[:, :, 0:n], EM[:, :, 0:n], ALU.mult)
                B.stt(AT[:, :, 0:n], KK[:, :, 0:n], -1.0, EPV[:, :, 0:n], ALU.mult, ALU.mult)
                YR = SG
                stage('rwkvprep_%s%d_%d' % (blk.kind, blk.idx, l))
                def rwkv_pair(hp):
                    nt = blk.ntile
                    if blk.kind == 's':
                        state_io_in('a', hp, dr['srw'][l])
                        Mst = Ms
                    else:
                        Mst = Mp[('a', hp)]
                    css = [slice(64 * ti, 64 * ti + 64) for ti in range(nt)]
                    with B.bank() as pb:
                        pv = pb[:, 0:nt * 256].rearrange("p (n q c) -> p n q c", n=nt, q=4)
                        for ti in range(nt):
                            for h2 in range(2):
                                hs = hsl[h2]
                                for q, src in enumerate((AT, BT_, KT, None)):
                                    s_ap = XS[hs, 6 + hp, css[ti]] if src is None else src[hs, hp, css[ti]]
                                    B.mm(pv[hs, ti, q, :], s_ap, ident[hs, hs])
                        evac(TB.TM[:, 0:nt], pv)
                    with B.bank() as pb1, B.bank() as pb2:
                        p4 = pb1[:, 0:nt * 256].rearrange("p (n a t) -> p n a t", n=nt, a=4)
                        pn = pb2[:, 0:nt * 64].rearrange("p (n t) -> p n t", n=nt)
                        for ti in range(nt):
                            for h2 in range(2):
                                hs = hsl[h2]
                                at = AT[hs, hp, css[ti]]
                                bt = BT_[hs, hp, css[ti]]
                                kt = KT[hs, hp, css[ti]]
                                rt = RT[hs, hp, css[ti]]
                                B.mm(p4[hs, ti, 0, :], bt, at)
                                B.mm(p4[hs, ti, 1, :], kt, at)
                                B.mm(p4[hs, ti, 2, :], bt, rt)
                                B.mm(p4[hs, ti, 3, :], kt, rt)
                                B.mm(pn[hs, ti, :], at, bt)
                        m4 = C('mask5_' + v, c1=256).rearrange("p (a t) -> p a t", a=4)
                        B.tt(TB.A4[:, 0:nt], p4, bc(m4.unsqueeze(1), [128, nt, 4, 64]), ALU.mult)
                        B.tt(TB.NM[:, 0:nt], pn, bc(C('mask5_' + v, c0=256, c1=320).unsqueeze(1), [128, nt, 64]), ALU.mult)
                    wy_double(blk, TB.NM[:, 0:nt, :], TB.A4[:, 0:nt, 0, :])
                    with B.bank() as pb:
                        pv = pb[:, 0:nt * 64].rearrange("p (n t) -> p n t", n=nt)
                        for ti in range(nt):
                            for h2 in range(2):
                                hs = hsl[h2]
                                B.mm(pv[hs, ti, :], TB.TM[hs, ti, 0, :], TB.Pm[hs, ti, :])
                        evac(TB.WTs[:, 0:nt], pv)
                    with B.bank() as pb:
                        pv = pb[:, 0:nt * 64].rearrange("p (n t) -> p n t", n=nt)
                        for ti in range(nt):
                            for h2 in range(2):
                                hs = hsl[h2]
                                B.mm(pv[hs, ti, :], TB.A4[hs, ti, 1, :], TB.TM[hs, ti, 3, :])
                        evac(TB.Xs[:, 0:nt], pv)
                    with B.bank() as pb:
                        pv = pb[:, 0:nt * 64].rearrange("p (n t) -> p n t", n=nt)
                        for ti in range(nt):
                            for h2 in range(2):
                                hs = hsl[h2]
                                if blk.kind == 'p':
                                    B.mm(pv[hs, ti, :], TB.Pm[hs, ti, :], TB.Xs[hs, ti, :])
                                else:
                                    B.mm(pv[hs, ti, :], TB.Xs[hs, ti, :], TB.Pm[hs, ti, :])
                        evac(TB.U0s[:, 0:nt], pv)
                    if blk.kind == 's':
                        vmask_build(TB.TM[:, 0, 3, :])
                    for ti in range(nt):
                        cs = css[ti]
                        seq_step(blk, Mst, TB.WTs[:, ti, :], TB.U0s[:, ti, :], TB.U0Ts[:, ti, :],
                                 ArbT=TB.A4[:, ti, 2, :],
                                 y_extra=(TB.TM[:, ti, 3, :], TB.A4[:, ti, 3, :]),
                                 rt_fn=lambda cc, hp=hp, cs=cs: RT[:, hp, cs][:, cc],
                                 b_tm=TB.TM[:, ti, 1, :],
                                 upd_extra=(TB.TM[:, ti, 2, :], TB.TM[:, ti, 3, :]),
                                 DC_fn=lambda hp=hp, cs=cs: segend(EP[:, hp, :], blk, cs),
                                 mode='rwkv', ydst=YR[:, hp, cs], Vmask=VM)
                    if blk.kind == 's':
                        state_out_rwkv(hp, Ms, NSEQ, dr['o_srw'][l])
                    elif is_last:
                        state_out_rwkv(hp, Mst, 1, dr['o_prw'][l])

                bb0 = A_COLS
                for j in range(9):
                    proj_to_PB(blk, (bb0 + 128 * j, 128), j, 3)
                proj_to(blk, (bb0 + 1152, 6), BETA[0:6, 0:n])
                proj_to(blk, (bb0 + 1158, 6), APRE[0:6, 0:n])
                for j in range(3):
                    proj_to(blk, (bb0 + 1164 + 128 * j, 128), ZB[:, j, 0:n])
                run_pairs(blk, rwkv_pair, 3)
                stage('rwkvchain_%s%d_%d' % (blk.kind, blk.idx, l))
                for c in range(3):
                    with B.bank() as pb:
                        B.mm(pb[:, 0:n], bones, YR[:, c, 0:n])
                        B.act(CUM[:, c, 0:n], pb[:, 0:n], AF.Copy, scale=1.0 / 64)
                mean = CUM[:, :, 0:n]
                B.tt(YR[:, :, 0:n], YR[:, :, 0:n], mean, ALU.subtract, eng='pool')
                B.act(T1[:, :, 0:n], YR[:, :, 0:n], AF.Square)
                for c in range(3):
                    with B.bank() as pb:
                        B.mm(pb[:, 0:n], bones, T1[:, c, 0:n])
                        rsqrt_ln(T1[:, c, 0:n], pb[:, 0:n], scale=1.0 / 64, bias=64e-5)
                B.tt(YR[:, :, 0:n], YR[:, :, 0:n], T1[:, :, 0:n], ALU.mult)
                B.tt(YR[:, :, 0:n], YR[:, :, 0:n], b3('lnw'), ALU.mult, eng='pool')
                B.tt(YR[:, :, 0:n], YR[:, :, 0:n], b3('lnb'), ALU.add)
                B.tt(YR[:, :, 0:n], YR[:, :, 0:n], BON[:, :, 0:n], ALU.add, eng='pool')
                if (l, blk.kind, blk.idx) == DBG_AT:
                    dbg('YRa', YR[:].rearrange("p c t -> p (c t)"))
                    dbg('T1a', T1[:].rearrange("p c t -> p (c t)"))
                    dbg('Ga', G_[:].rearrange("p c t -> p (c t)"))
                B.tt(yT[:, 0:3, 0:n], YR[:, :, 0:n], G_[:, :, 0:n], ALU.mult)

                stage('rwkv_%s%d_%d' % (blk.kind, blk.idx, l))
                if blk.kind == 's':
                    B.dma(OCV[:], dr['sconv'][l].rearrange("s j c -> (s j) c"))
                    for g in range(3):
                        with B.bank() as pb:
                            for q in range(3):
                                j = 3 * g + q
                                B.tr(pb[:, q * 48:(q + 1) * 48], OCV[0:48, j * 128:(j + 1) * 128], ident[0:48, 0:48])
                            dst = PB[:, 3 * g:3 * g + 3, 0:blk.w].rearrange("p j (s u) -> p j s u", u=7)[:, :, :, 0:3]
                            evac(dst, pb[:, 0:144].rearrange("p (j s u) -> p j s u", j=3, u=3))
                if blk.kind == 's' or is_last:
                    nr = 3 * nsq
                    for g in range(3):
                        with B.bank() as pb:
                            for q in range(3):
                                j = 3 * g + q
                                if blk.kind == 'p':
                                    src = PB[:, j, n:n + 3]
                                else:
                                    src = TL[:, 48 * (j % 2):48 * (j % 2) + 48]
                                    B.cp(src.rearrange("p (s u) -> p s u", u=3),
                                         PB[:, j, 0:blk.w].rearrange("p (s u) -> p s u", u=7)[:, :, 4:7], eng='pool')
                                B.tr(pb[0:nr, q * 128:(q + 1) * 128], src, ident)
                            evac(OCV[0:nr, 384 * g:384 * g + 384], pb[0:nr, 0:384])
                    dst = dr['o_sconv'][l] if blk.kind == 's' else dr['o_pconv'][l]
                    odma(dst.rearrange("s j c -> (s j) c"), OCV[0:nr, :])
                (QKV0, QKV1, QKV2, CA0, CA1, CA2, SQ0, SQ1) = SL3[0:8]
                cw = PV('convw', l).rearrange("p (j c) -> p j c", j=4)
                ACC = [QKV0, QKV1, QKV2]
                TMPc = [CA0, CA1, CA2]
                for g in range(3):
                    acc4 = c4(ACC[g][:, :, 0:n], blk)
                    tmp4 = c4(TMPc[g][:, :, 0:n], blk)
                    for j in range(4):
                        src = PBv(blk, 3 * g, 3 * g + 3, j)
                        wj = bc(cw[:, j, 3 * g:3 * g + 3].unsqueeze(2).unsqueeze(3), [128, 3, nsq, T])
                        if j == 0:
                            B.tt(acc4, src, wj, ALU.mult, eng=('pool' if g == 1 else 'dve'))
                        else:
                            B.tt(tmp4, src, wj, ALU.mult, eng='pool')
                            B.tt(acc4, acc4, tmp4, ALU.add)
                    silu_exp(ACC[g][:, :, 0:n], ACC[g][:, :, 0:n], TMPc[g][:, :, 0:n])
                Qg, Kg, Vg = QKV0, QKV1, QKV2
                for arr, sc_, bi_ in ((Qg, 64.0, 64e-6), (Kg, 1.0, 1e-6)):
                    B.act(SQ0[:, :, 0:n], arr[:, :, 0:n], AF.Square)
                    for c in range(3):
                        with B.bank() as pb:
                            B.mm(pb[:, 0:n], bones, SQ0[:, c, 0:n])
                            rsqrt_ln(SQ1[:, c, 0:n], pb[:, 0:n], scale=sc_, bias=bi_)
                    B.tt(arr[:, :, 0:n], arr[:, :, 0:n], SQ1[:, :, 0:n], ALU.mult)
                B.act(BE6[:, 0:n], BETA[0:6, 0:n], AF.Exp, scale=-1.0)
                sig_finish(BE6[:, 0:n])
                B.act(G6[:, 0:n], APRE[0:6, 0:n], AF.Exp, bias=PV('dtb', l, rows=6))
                B.act(G6[:, 0:n], G6[:, 0:n], AF.Ln, bias=1.0)
                B.act(EA6[:], PV('alog', l, rows=6), AF.Exp)
                B.ts(G6[:, 0:n], G6[:, 0:n], EA6[:, 0:1], ALU.mult, -1.0, ALU.mult)
                B.scan(GC6[:, 0:n], C('rm_' + v, rows=6, c1=n), G6[:, 0:n])
                silu_exp(ZB[:, :, 0:n], ZB[:, :, 0:n], CA0[:, :, 0:n])
                OR = SL3[8]
                stage('gdnprep_%s%d_%d' % (blk.kind, blk.idx, l))
                def gdn_pair(hp):
                    nt = blk.ntile
                    Mst = Ms if blk.kind == 's' else Mp[('b', hp)]
                    if blk.kind == 's':
                        state_io_in('b', hp, dr['sgdn'][l])
                    css = [slice(64 * ti, 64 * ti + 64) for ti in range(nt)]
                    SCP = TB.SCP[:, 0:nt, :]
                    SCD = TB.SCD[:, 0:nt, :]
                    with B.bank() as pb:
                        pv = pb[:, 0:nt * 4].rearrange("p (n q) -> p n q", n=nt)
                        for ti in range(nt):
                            for h2 in range(2):
                                hs = hsl[h2]
                                hh = 2 * hp + h2
                                for q, src in enumerate((BE6, GC6, G6)):
                                    B.mm(pv[hs, ti, q:q + 1], src[0:6, css[ti]], ident[0:6, hh:hh + 1])
                        evac(SCP[:, :, 0:3], pv[:, :, 0:3])
                    with B.bank() as pb:
                        pv = pb[:, 0:nt * 4].rearrange("p (n q) -> p n q", n=nt)
                        for ti in range(nt):
                            for h2 in range(2):
                                hs = hsl[h2]
                                B.mm(pv[hs, ti, 0:1], C('BS_' + v)[hs, :], TB.SCP[hs, ti, 2:3])
                        B.tt(SCD[:, :, 2:3], pv[:, :, 0:1], SCP[:, :, 1:2], ALU.subtract)
                    B.act(SCD[:, :, 2:3], SCD[:, :, 2:3], AF.Exp)
                    B.act(SCD[:, :, 3:4], SCP[:, :, 1:2], AF.Exp)
                    B.stt(SCD[:, :, 0:1], SCP[:, :, 0:1], -1.0, SCD[:, :, 3:4], ALU.mult, ALU.mult)
                    B.ts(SCD[:, :, 1:2], SCP[:, :, 0:1], -1.0, ALU.mult)
                    GSL = TB.GSL[:, 0:nt, :]
                    GUI = TB.GUI[:, 0:nt, :]
                    NMv = TB.NM[:, 0:nt, :]

                    def b64(ap):
                        return bc(ap, [128, nt, 64])
                    with B.bank() as pb:
                        p3 = pb[:, 0:nt * 192].rearrange("p (n a t) -> p n a t", n=nt, a=3)
                        for ti in range(nt):
                            for h2 in range(2):
                                hs = hsl[h2]
                                hh = 2 * hp + h2
                                B.mm(p3[hs, ti, 0, :], C('osel', rows=6, c0=hh * 64, c1=hh * 64 + 64), GC6[0:6, css[ti]])
                                B.mm(p3[hs, ti, 1, :], Kg[hs, hp, css[ti]], Kg[hs, hp, css[ti]])
                                B.mm(p3[hs, ti, 2, :], Kg[hs, hp, css[ti]], Qg[hs, hp, css[ti]])
                        B.tt(GSL, p3[:, :, 0, :], b64(SCP[:, :, 1:2]), ALU.subtract)
                        B.tt(GUI, GSL, bc(C('BIU_' + v).unsqueeze(1), [128, nt, 64]), ALU.subtract)
                        B.tt(GSL, GSL, bc(C('BSL_' + v).unsqueeze(1), [128, nt, 64]), ALU.add, eng='pool')
                        B.act(GSL, GSL, AF.Exp, scale=-1.0)
                        B.act(GUI, GUI, AF.Exp)
                        B.tt(NMv, p3[:, :, 1, :], b64(SCD[:, :, 1:2]), ALU.mult)
                        B.tt(NMv, NMv, GSL, ALU.mult)
                        B.tt(GUI, p3[:, :, 2, :], GUI, ALU.mult)
                    with B.bank() as pb:
                        pv = pb[:, 0:nt * 64].rearrange("p (n t) -> p n t", n=nt)
                        for ti in range(nt):
                            for h2 in range(2):
                                hs = hsl[h2]
                                B.mm(pv[hs, ti, :], TB.NM[hs, ti, :], ident[hs, hs])
                        evac(TB.Qs[:, 0:nt], pv)
                    wy_double(blk, NMv, TB.Qs[:, 0:nt, :])
                    with B.bank() as pb:
                        pv = pb[:, 0:nt * 128].rearrange("p (n q c) -> p n q c", n=nt, q=2)
                        for ti in range(nt):
                            for h2 in range(2):
                                hs = hsl[h2]
                                B.mm(pv[hs, ti, 0, :], Kg[hs, hp, css[ti]], ident[hs, hs])
                                B.mm(pv[hs, ti, 1, :], Vg[hs, hp, css[ti]], ident[hs, hs])
                        evac(TB.TM[:, 0:nt, 0:2, :], pv)
                    B.tt(TB.KB3[:, 0:nt, 0, :], TB.TM[:, 0:nt, 0, :], b64(SCD[:, :, 0:1]), ALU.mult, eng='pool')
                    B.tt(TB.KB3[:, 0:nt, 1, :], TB.TM[:, 0:nt, 1, :], b64(SCP[:, :, 0:1]), ALU.mult)
                    B.tt(TB.KB3[:, 0:nt, 2, :], TB.TM[:, 0:nt, 0, :], b64(SCD[:, :, 2:3]), ALU.mult, eng='pool')
                    with B.bank() as pb:
                        pv = pb[:, 0:nt * 64].rearrange("p (n t) -> p n t", n=nt)
                        for ti in range(nt):
                            for h2 in range(2):
                                hs = hsl[h2]
                                B.mm(pv[hs, ti, :], TB.KB3[hs, ti, 0, :], TB.Pm[hs, ti, :])
                        evac(TB.WTs[:, 0:nt], pv)
                    with B.bank() as pb:
                        pv = pb[:, 0:nt * 64].rearrange("p (n t) -> p n t", n=nt)
                        for ti in range(nt):
                            for h2 in range(2):
                                hs = hsl[h2]
                                if blk.kind == 'p':
                                    B.mm(pv[hs, ti, :], TB.Pm[hs, ti, :], TB.KB3[hs, ti, 1, :])
                                else:
                                    B.mm(pv[hs, ti, :], TB.KB3[hs, ti, 1, :], TB.Pm[hs, ti, :])
                        evac(TB.U0s[:, 0:nt], pv)
                    with B.bank() as pb:
                        pv = pb[:, 0:nt * 64].rearrange("p (n t) -> p n t", n=nt)
                        for ti in range(nt):
                            B.mm(pv[:, ti, :], C('selb', rows=6, c0=hp * 128, c1=hp * 128 + 128), GC6[0:6, css[ti]])
                        B.act(TB.EBC[:, 0:nt], pv, AF.Exp)
                    B.tt(TB.QTg[:, 0:nt], Qg[:, hp, 0:nt * 64].rearrange("p (n t) -> p n t", n=nt), TB.EBC[:, 0:nt], ALU.mult)
                    for ti in range(nt):
                        cs = css[ti]
                        seq_step(blk, Mst, TB.WTs[:, ti, :], TB.U0s[:, ti, :], TB.U0Ts[:, ti, :],
                                 ArbT=TB.GUI[:, ti, :],
                                 y_extra=None,
                                 rt_fn=lambda cc, ti=ti: TB.QTg[:, ti, cc],
                                 b_tm=TB.KB3[:, ti, 2, :],
                                 upd_extra=None,
                                 DC_fn=lambda ti=ti: TB.EBC[:, ti, :].rearrange("p (s u) -> p s u", u=blk.seglen)[:, :, blk.seglen - 1],
                                 mode='gdn', ydst=OR[:, hp, cs])
                    if blk.kind == 's':
                        state_io_out(hp, Ms, NSEQ, dr['o_sgdn'][l])
                    elif is_last:
                        state_io_out(hp, Mst, 1, dr['o_pgdn'][l])

                cc0 = A_COLS + B_COLS
                for j in range(8):
                    proj_to(blk, (cc0 + 128 * j, 128), PB[:, j, 0:n])
                run_pairs(blk, gdn_pair, 3)
                stage('gdnchain_%s%d_%d' % (blk.kind, blk.idx, l))
                B.act(SQ0[:, :, 0:n], OR[:, :, 0:n], AF.Square)
                for c in range(3):
                    with B.bank() as pb:
                        B.mm(pb[:, 0:n], bones, SQ0[:, c, 0:n])
                        rsqrt_ln(SQ1[:, c, 0:n], pb[:, 0:n], scale=1.0 / 64, bias=1e-6)
                B.tt(OR[:, :, 0:n], OR[:, :, 0:n], SQ1[:, :, 0:n], ALU.mult)
                B.ts(OR[:, :, 0:n], OR[:, :, 0:n], PV('gnw', l), ALU.mult, eng='pool')
                if (l, blk.kind, blk.idx) == DBG_AT:
                    dbg('ORa', OR[:].rearrange("p c t -> p (c t)"))
                    dbg('ZBa', ZB[:].rearrange("p c t -> p (c t)"))
                B.tt(yT[:, 3:6, 0:n], OR[:, :, 0:n], ZB[:, :, 0:n], ALU.mult)

                stage('gdn_%s%d_%d' % (blk.kind, blk.idx, l))
                (SF, LF, KH_, QS, BC_, QTL, KTL, QH) = [t[:, 0:2, 0:n] for t in SL3[0:8]]
                (KHAT, OH, HT1) = [t[:, 0:2, 0:n] for t in SL3[8:11]]
                lb2 = bc(LBt[:, l, :].unsqueeze(2), [128, 2, n])
                oml2 = bc(OMLt[:, l, :].unsqueeze(2), [128, 2, n])
                B.act(SF, PB[:, 2:4, 0:n], AF.Exp, scale=-1.0)
                sig_finish(SF)
                B.tt(LF, SF, oml2, ALU.mult, eng='pool')
                B.tt(LF, LF, lb2, ALU.add)
                B.act(LF, LF, AF.Ln)
                B.ts(KH_, SF, -1.0, ALU.mult, 1.0, ALU.add, eng='pool')
                B.tt(KH_, KH_, oml2, ALU.mult)
                silu_exp(QS, PB[:, 0:2, 0:n], QS)
                for c in range(2):
                    B.scan(BC_[:, c, :], C('rm_' + v, c1=n), LF[:, c, :])
                sl = blk.seglen
                nsg_all = n // sl
                mid = sl // 2 - 1

                def sv(ap):
                    return ap.rearrange("p c (s u) -> p c s u", u=sl)
                B.tt(sv(HT1), sv(BC_), bc(sv(BC_)[:, :, :, mid:mid + 1], [128, 2, nsg_all, sl]), ALU.subtract, eng='pool')
                B.act(QTL, HT1, AF.Exp)
                B.tt(QTL, QTL, QS, ALU.mult)
                B.act(KTL, HT1, AF.Exp, scale=-1.0)
                B.tt(KTL, KTL, KH_, ALU.mult, eng='pool')
                EB = SL3[11][:, 0:2, 0:n]
                B.act(EB, BC_, AF.Exp)
                B.tt(QH, QS, EB, ALU.mult)
                B.tt(sv(HT1), bc(sv(BC_)[:, :, :, sl - 1:sl], [128, 2, nsg_all, sl]), sv(BC_), ALU.subtract, eng='pool')
                B.act(HT1, HT1, AF.Exp)
                B.tt(KHAT, KH_, HT1, ALU.mult)
                def hgrn_pair(hp):
                    nt = blk.ntile
                    if blk.kind == 's':
                        state_io_in('c', hp, dr['shg'][l])
                        Mst = Ms
                    else:
                        Mst = Mp[('c', hp)]
                    css = [slice(64 * ti, 64 * ti + 64) for ti in range(nt)]
                    with B.bank() as pb:
                        pv = pb[:, 0:nt * 128].rearrange("p (n q c) -> p n q c", n=nt, q=2)
                        for ti in range(nt):
                            for h2 in range(2):
                                hs = hsl[h2]
                                B.mm(pv[hs, ti, 0, :], KHAT[hs, hp, css[ti]], ident[hs, hs])
                                B.mm(pv[hs, ti, 1, :], PB[hs, 4 + hp, css[ti]], ident[hs, hs])
                        evac(TB.TM[:, 0:nt, 0:2, :], pv)
                    with B.bank() as pb:
                        pv = pb[:, 0:nt * 64].rearrange("p (n t) -> p n t", n=nt)
                        for ti in range(nt):
                            for h2 in range(2):
                                hs = hsl[h2]
                                B.mm(pv[hs, ti, :], KTL[hs, hp, css[ti]], QTL[hs, hp, css[ti]])
                        B.ts(TB.GUI[:, 0:nt], pv, 3.0e38, ALU.min, -3.0e38, ALU.max)
                        B.tt(TB.GUI[:, 0:nt], TB.GUI[:, 0:nt], bc(C('IU_' + v).unsqueeze(1), [128, nt, 64]), ALU.mult)
                    if blk.kind == 's':
                        vmask_build(TB.TM[:, 0, 1, :])
                    for ti in range(nt):
                        cs = css[ti]
                        seq_step(blk, Mst, None, None, None,
                                 ArbT=None,
                                 y_extra=(TB.TM[:, ti, 1, :], TB.GUI[:, ti, :]),
                                 rt_fn=lambda cc, hp=hp, cs=cs: QH[:, hp, cs][:, cc],
                                 b_tm=None,
                                 upd_extra=(TB.TM[:, ti, 0, :], TB.TM[:, ti, 1, :]),
                                 DC_fn=lambda hp=hp, cs=cs: segend(EB[:, hp, :], blk, cs),
                                 mode='gdn', ydst=OH[:, hp, cs], Vmask=VM)
                    if blk.kind == 's':
                        state_io_out(hp, Ms, NSEQ, dr['o_shg'][l])
                    elif is_last:
                        state_io_out(hp, Mst, 1, dr['o_phg'][l])

                stage('hgrnprep_%s%d_%d' % (blk.kind, blk.idx, l))
                run_pairs(blk, hgrn_pair, 2)
                stage('hgrnchain_%s%d_%d' % (blk.kind, blk.idx, l))
                SQh = SL3[0][:, 0:2, 0:n]
                SQg = SL3[1][:, 0:2, 0:n]
                B.act(SQh, OH, AF.Square)
                for c in range(2):
                    with B.bank() as pb:
                        B.mm(pb[:, 0:n], bones, SQh[:, c, :])
                        rsqrt_ln(SQg[:, c, :], pb[:, 0:n], scale=1.0 / 64, bias=1e-6)
                B.tt(OH, OH, SQg, ALU.mult)
                B.ts(OH, OH, PV('hnw', l), ALU.mult, eng='pool')
                B.act(SQh, PB[:, 6:8, 0:n], AF.Exp, scale=-1.0)
                sig_finish(SQh)
                if (l, blk.kind, blk.idx) == DBG_AT:
                    dbg('OHa', SL3[9][:].rearrange("p c t -> p (c t)"))
                    dbg('xba', xb[:].rearrange("p c t -> p (c t)"))
                B.tt(yT[:, 6:8, 0:n], OH, SQh, ALU.mult)

                if blk.kind == 's' or is_last:
                    HL = TL[:, :]
                    for kc in range(KC):
                        if blk.kind == 'p':
                            srcl = hT_p[:, kc, 3 + n - 1:3 + n]
                        else:
                            srcl = hT_s[:, kc, :].rearrange("p (s u) -> p s u", u=5)[:, :, 4]
                        B.cp(HL[:, kc * 16:kc * 16 + nsq], srcl, eng='pool')
                    for half in range(2):
                        with B.bank() as pb:
                            for q in range(4):
                                kc = 4 * half + q
                                B.tr(pb[0:nsq, q * 128:(q + 1) * 128], HL[:, kc * 16:kc * 16 + nsq], ident)
                            evac(OSH[0:nsq, 512 * half:512 * half + 512], pb[0:nsq, 0:512])
                    dst = dr['o_sshift'][l] if blk.kind == 's' else dr['o_pshift'][l]
                    odma(dst, OSH[0:nsq, :])
                for m in range(KC):
                    with B.bank() as pb:
                        for kc in range(KC):
                            B.mm(pb[:, 0:n], w_out_sb[:, kc, m * 128:(m + 1) * 128], yT[:, kc, 0:n],
                                 start=(kc == 0), stop=(kc == KC - 1))
                        gate_res(blk, xb[:, m, 0:n], pb[:, 0:n], l, 'GM', m, n, tmpn)
                B.dma(xscr[:, :, blk.tok0:blk.tok0 + n], xb[:, :, 0:n], wr=xkey(0, l, blk.tok0, n))
                stage('blk_%s%d_%d' % (blk.kind, blk.idx, l))

            for bi, blk in enumerate(prompt_blocks):
                process_block(blk, bi == len(prompt_blocks) - 1)
                B.cp(hT_p[:, :, 0:3], hT_p[:, :, BLK:BLK + 3], eng='pool')
            process_block(sample_block, True)
            B.release(mph)
            stage('mix_%d' % l)

            fph = B.mark()
            JG = 6
            NG = (NJ + JG - 1) // JG
            WFI = [B.sb("w_fi%d" % g, [128, KC, 2, (min(NJ, g * JG + JG) - g * JG) * 128], BF16) for g in range(NG)]
            WFO = [B.sb("w_fo%d" % g, [128, min(NJ, g * JG + JG) - g * JG, D], BF16) for g in range(NG)]
            wsrc = dr['w_fi'][l].rearrange("(kc p) n -> p kc n", p=128)
            for g in range(NG):
                j0, j1 = g * JG, min(NJ, g * JG + JG)
                w = (j1 - j0) * 128
                for part in range(2):
                    B.dma(WFI[g][:, :, part, 0:w], wsrc[:, :, part * FF + j0 * 128:part * FF + j0 * 128 + w],
                          eng='pool', cls='w')
            wsrc = dr['w_fo'][l].rearrange("(j p) n -> p j n", p=128)
            for g in range(NG):
                j0, j1 = g * JG, min(NJ, g * JG + JG)
                B.dma(WFO[g][:, 0:j1 - j0, :], wsrc[:, j0:j1, :], eng='pool', cls='w')
            xb2 = B.sb("xb2", [128, KC, FBLK])
            h2T = B.sb("h2T", [128, KC, FBLK], BF16)
            actT = B.sb("actT", [128, NJ, FBLK], BF16)
            sq2 = [B.sb("sq2_%d" % i, [128, FBLK]) for i in range(2)]
            rstd2 = B.sb("rstd2", [128, FBLK])
            tmp2 = [B.sb("tmp2_%d" % i, [128, FBLK]) for i in range(2)]
            sgt = [B.sb("sgt%d" % i, [128, FBLK]) for i in range(2)]
            last_layer = (l == DEPTH - 1)
            if last_layer:
                yfin = B.sb("yfin", [128, KC, FBLK])
                ytm = B.sb("ytm", [128, D])
            for fb in ffn_blocks:
                n = fb.n
                B.dma(xb2[:, :, 0:n], xscr[:, :, fb.tok0:fb.tok0 + n], rd=xkey(1, l, fb.tok0, n))
                hdst = lambda kc, fb=fb, n=n: v3(h2T[:, kc, 0:n], fb)
                norm_mod(fb, xb2, n, l, 'AF', 'BF', hdst, sq2, rstd2, tmp2)
                for j in range(NJ):
                    with B.bank() as pg, B.bank() as pu:
                        for kc in range(KC):
                            B.mm(pg[:, 0:n], WFI[j // JG][:, kc, 0, (j % JG) * 128:(j % JG + 1) * 128], h2T[:, kc, 0:n],
                                 start=(kc == 0), stop=(kc == KC - 1))
                        for kc in range(KC):
                            B.mm(pu[:, 0:n], WFI[j // JG][:, kc, 1, (j % JG) * 128:(j % JG + 1) * 128], h2T[:, kc, 0:n],
                                 start=(kc == 0), stop=(kc == KC - 1))
                        B.act(sgt[j % 2][:, 0:n], pg[:, 0:n], AF.Silu)
                        B.tt(actT[:, j, 0:n], sgt[j % 2][:, 0:n], pu[:, 0:n], ALU.mult)
                for m in range(KC):
                    with B.bank() as pb:
                        for j in range(NJ):
                            B.mm(pb[:, 0:n], WFO[j // JG][:, j % JG, m * 128:(m + 1) * 128], actT[:, j, 0:n],
                                 start=(j == 0), stop=(j == NJ - 1))
                        gate_res(fb, xb2[:, m, 0:n], pb[:, 0:n], l, 'GF', m, n, tmp2)
                if not last_layer:
                    B.dma(xscr[:, :, fb.tok0:fb.tok0 + n], xb2[:, :, 0:n], wr=xkey(1, l, fb.tok0, n))
                else:
                    rms_stats(xb2, n, sq2, rstd2)
                    for kc in range(KC):
                        B.stt(yfin[:, kc, 0:n], xb2[:, kc, 0:n], PV('fnw', 0, c0=kc, c1=kc + 1), rstd2[:, 0:n],
                              ALU.mult, ALU.mult)
                    nt = (n + 127) // 128
                    for ti in range(nt):
                        r = min(128, n - 128 * ti)
                        for g in range(2):
                            with B.bank() as pb:
                                for q in range(4):
                                    kc = 4 * g + q
                                    B.tr(pb[0:r, q * 128:(q + 1) * 128], yfin[:, kc, 128 * ti:128 * ti + r], ident)
                                evac(ytm[0:r, 512 * g:512 * g + 512], pb[0:r, 0:512])
                        if fb.kind == 'p':
                            odma(dr['yp'][fb.tok0 + 128 * ti:fb.tok0 + 128 * ti + r, :], ytm[0:r, :])
                        else:
                            odma(dr['ys'], ytm[0:r, :])
            B.release(fph)
            stage('ffn_%d' % l)

        S.disabled = False
        S.barrier()
        with nc.Block() as block:
            S.emit(block)
    build_program.dbg_list = dbg_list
    build_program.n_ops = S.n_ops
    build_program.sbuf_top = B.top
    return nc


INPUT_ORDER = None


def make_in_maps(inp):
    pv = pack_pvec(inp)
    cst = make_consts()
    f = lambda a: np.ascontiguousarray(np.asarray(a, np.float32))
    shared = {
        'w_ada': f(inp['w_ada']), 'w_in': f(inp['w_in']), 'w_out': f(inp['w_out']),
        'w_fi': f(inp['w_ffn_in']), 'w_fo': f(inp['w_ffn_out']), 'pvec': pv, 'cst': cst,
    }
    maps = []
    for c in range(NCORES):
        s0, s1 = NSEQ * c, NSEQ * (c + 1)
        m = dict(shared)
        m['xp'] = f(inp['x_prompt'][c])
        m['xs'] = f(np.asarray(inp['x_sample'][s0:s1]).reshape(TOKS, D))
        m['c17'] = f(np.concatenate([np.asarray(inp['c_prompt'][c:c + 1]), np.asarray(inp['c_sample'][s0:s1])], axis=0))
        m['shift_s'] = f(np.asarray(inp['state_rwkv_shift'])[:, s0:s1])
        m['srw'] = f(np.asarray(inp['state_rwkv'])[:, s0:s1])
        m['sconv'] = f(np.asarray(inp['state_gdn_conv'])[:, s0:s1])
        m['sgdn'] = f(np.asarray(inp['state_gdn'])[:, s0:s1])
        m['shg'] = f(np.asarray(inp['state_hgrn'])[:, s0:s1])
        maps.append(m)
    return maps


_NC_CACHE = {}


def kernel(**inputs):
    inp = {k: np.asarray(v) for k, v in inputs.items()}
    maps = make_in_maps(inp)
    if 'nc' not in _NC_CACHE:
        _NC_CACHE['nc'] = build_program()
    nc = _NC_CACHE['nc']
    res = run_bass_kernel_spmd(nc, maps, core_ids=list(range(NCORES)))
    R = res.results

    def cat(name, axis):
        return np.concatenate([np.asarray(R[c][name], np.float32) for c in range(NCORES)], axis=axis)

    y_prompt = np.stack([np.asarray(R[c]['yp'], np.float32) for c in range(NCORES)], axis=0)
    y_sample = cat('ys', 0).reshape(NCORES * NSEQ, TSS, D)
    p_shift = cat('o_pshift', 1)
    p_rwkv = cat('o_prw', 1)
    p_conv = cat('o_pconv', 1)
    p_gdn = cat('o_pgdn', 1)
    p_hgrn = cat('o_phg', 1)
    s_shift = cat('o_sshift', 1)
    s_rwkv = cat('o_srw', 1)
    s_conv = cat('o_sconv', 1)
    s_gdn = cat('o_sgdn', 1)
    s_hgrn = cat('o_shg', 1)
    return (y_prompt, y_sample, p_shift, p_rwkv, p_conv, p_gdn, p_hgrn,
            s_shift, s_rwkv, s_conv, s_gdn, s_hgrn)
```

```python
import contextlib
import numpy as np
import concourse.bass as bass
import concourse.mybir as mybir
from concourse.bass_utils import run_bass_kernel_spmd

F32 = mybir.dt.float32
BF16 = mybir.dt.bfloat16
ALU = mybir.AluOpType
AF = mybir.ActivationFunctionType

NCORES = 8
D = 1024
KC = 8
TOKP = 2048
NSEQ = 16
TSS = 4
TOKS = 64
DEPTH = 2
A_COLS = 1280
B_COLS = 1548
C_COLS = 1024
IN_COLS = 3852
FF = 2816
NJ = 22
BLK = 128
FBLK = 256
C0 = 0.6065306597126334
BIG = 1.0e4
SEM_LIMIT = 12000
SB_LO = 16512
SB_HI = 229376

DBG_AT = None
TRACE_LINES = False
FUSE_WAIT = True
DMA_SLOTS = 8
CHAIN_BANKS = {0: [0, 1], 1: [2, 3], 2: [4, 5]}
OP_LINES = {}


class Sched:
    ENGS = ('pe', 'act', 'dve', 'pool', 'sp')

    def __init__(self, nc, stack):
        self.nc = nc
        self.stack = stack
        self.prog = {e: [] for e in self.ENGS}
        self.stream_sem = {}
        self.stream_cnt = {}
        self.last_w = {}
        self.reads = {}
        self.seen = {e: {} for e in self.ENGS}
        self.snap = {}
        self.dma_rr = {}
        self.nsem = 0
        self.n_ops = 0
        self.disabled = False
        self.max_ops = None
        self.trace = None

    def _new_sem(self, stream):
        s = self.stack.enter_context(self.nc.semaphore("s%d" % self.nsem))
        self.nsem += 1
        self.stream_sem.setdefault(stream, []).append(s)
        self.stream_cnt[stream] = 0

    def _event(self, stream, inc):
        if stream not in self.stream_sem or self.stream_cnt[stream] + inc > SEM_LIMIT:
            self._new_sem(stream)
        self.stream_cnt[stream] += inc
        return (stream, len(self.stream_sem[stream]) - 1, self.stream_cnt[stream])

    @staticmethod
    def _key(a):
        return a if isinstance(a, str) else a.name

    def op(self, eng, fn, reads=(), writes=(), dma_class=None):
        if self.disabled or (self.max_ops is not None and self.n_ops >= self.max_ops):
            return None
        if dma_class is not None:
            k = self.dma_rr.get((eng, dma_class), 0)
            self.dma_rr[(eng, dma_class)] = k + 1
            stream = (eng, dma_class, k % DMA_SLOTS)
            inc = 16
        else:
            stream = eng
            inc = 1
        rk = [self._key(r) for r in reads]
        wk = [self._key(w) for w in writes]
        deps = []
        if dma_class is not None and stream in self.stream_sem and self.stream_cnt[stream] > 0:
            deps.append(((stream, len(self.stream_sem[stream]) - 1, self.stream_cnt[stream]), 'slot'))
        for k in rk:
            if k in self.last_w:
                deps.append((self.last_w[k], 'raw'))
        for k in wk:
            if k in self.last_w:
                deps.append((self.last_w[k], 'waw'))
            for ev in self.reads.get(k, ()):
                deps.append((ev, 'war'))
        waits = []
        seen = self.seen[eng]
        for ev, kind in deps:
            st, ep, val = ev
            if st == eng and dma_class is None:
                if eng == 'pe' or kind == 'war':
                    continue
            sk = (st, ep)
            if seen.get(sk, 0) >= val:
                continue
            seen[sk] = val
            waits.append((self.stream_sem[st][ep], val))
            snap = self.snap.get(ev)
            if snap:
                for k2, v2 in snap.items():
                    if seen.get(k2, 0) < v2:
                        seen[k2] = v2
        ev = self._event(stream, inc)
        sem = self.stream_sem[stream][ev[1]]
        self.snap[ev] = dict(seen)
        if TRACE_LINES:
            OP_LINES.setdefault(eng, []).append(getattr(fn, '_ln', None))
        if self.trace is not None:
            import sys as _sys
            self.trace.append((self.n_ops, eng, _sys._getframe(2).f_lineno))
        self.prog[eng].append((waits, fn, sem, inc))
        for k in wk:
            self.last_w[k] = ev
            self.reads[k] = []
        for k in rk:
            if k not in wk:
                self.reads.setdefault(k, []).append(ev)
        self.n_ops += 1
        return ev

    def barrier(self):
        evs = []
        for stream, sems in self.stream_sem.items():
            ep = len(sems) - 1
            if self.stream_cnt[stream] > 0:
                evs.append((stream, ep, self.stream_cnt[stream]))
        for e in self.ENGS:
            waits = []
            for st, ep, val in evs:
                if st == e:
                    continue
                sk = (st, ep)
                if self.seen[e].get(sk, 0) >= val:
                    continue
                self.seen[e][sk] = val
                waits.append((self.stream_sem[st][ep], val))
            if waits:
                self.prog[e].append((waits, None, None, 0))

    def emit(self, block):
        prog = self.prog

        def run(e, engobj):
            for waits, fn, sem, inc in prog[e]:
                if fn is None:
                    for s, v in waits:
                        engobj.wait_ge(s, v)
                    continue
                fuse = FUSE_WAIT and inc == 1 and len(waits) > 0
                for s, v in (waits[:-1] if fuse else waits):
                    engobj.wait_ge(s, v)
                ins = fn(engobj)
                if fuse:
                    ins._wait_ge(waits[-1][0], waits[-1][1])
                ins.then_inc(sem, inc)

        @block.tensor
        def _(eng):
            run('pe', eng)

        @block.scalar
        def _(eng):
            run('act', eng)

        @block.vector
        def _(eng):
            run('dve', eng)

        @block.gpsimd
        def _(eng):
            run('pool', eng)

        @block.sync
        def _(eng):
            run('sp', eng)


def _dtsize(dt):
    return 2 if dt == BF16 else 4


class KB:
    def __init__(self, nc, stack):
        self.nc = nc
        self.S = Sched(nc, stack)
        self.top = SB_LO
        self.uid = 0
        self.banks = [stack.enter_context(nc.psum_tensor("pb%d" % i, [128, 512], F32)) for i in range(8)]
        self.free_banks = list(range(8))
        self.chain = None
        self.chains = {}
        self.chain_bank_cnt = {}

    def emit(self, eng, fn, reads=(), writes=(), dma_class=None):
        if TRACE_LINES:
            import sys as _sys
            fn._ln = _sys._getframe(2).f_lineno
        if self.chain is None:
            self.S.op(eng, fn, reads=reads, writes=writes, dma_class=dma_class)
        else:
            self.chains[self.chain].append((eng, fn, list(reads), list(writes), dma_class))

    @contextlib.contextmanager
    def in_chain(self, k):
        assert self.chain is None
        self.chain = k
        self.chains.setdefault(k, [])
        try:
            yield
        finally:
            self.chain = None

    def flush_chains(self):
        lists = [self.chains[k] for k in sorted(self.chains)]
        self.chains = {}
        pos = [0] * len(lists)
        remaining = sum(len(x) for x in lists)
        while remaining:
            for i, lst in enumerate(lists):
                if pos[i] < len(lst):
                    eng, fn, reads, writes, dc = lst[pos[i]]
                    pos[i] += 1
                    remaining -= 1
                    self.S.op(eng, fn, reads=reads, writes=writes, dma_class=dc)

    def sb(self, name, shape, dt=F32):
        size = int(np.prod(shape[1:])) * _dtsize(dt)
        size = (size + 31) // 32 * 32
        off = self.top
        self.top += size
        assert self.top <= SB_HI, "SBUF overflow at %s: %d" % (name, self.top)
        self.uid += 1
        return self.nc.alloc_sbuf_tensor_at("%s_%d" % (name, self.uid), list(shape), dt, offset=off)

    def mark(self):
        return self.top

    def release(self, m):
        self.S.barrier()
        self.top = m

    @contextlib.contextmanager
    def bank(self):
        if self.chain is not None:
            c = self.chain_bank_cnt.get(self.chain, 0)
            self.chain_bank_cnt[self.chain] = c + 1
            bl = CHAIN_BANKS[self.chain]
            yield self.banks[bl[c % len(bl)]]
            return
        idx = self.free_banks.pop(0)
        try:
            yield self.banks[idx]
        finally:
            self.free_banks.append(idx)

    def mm(self, out, lhsT, rhs, start=True, stop=True):
        self.emit('pe', lambda e: e.matmul(out, lhsT=lhsT, rhs=rhs, start=start, stop=stop),
                  reads=[lhsT, rhs], writes=[out])

    def tr(self, out, in_, ident):
        self.emit('pe', lambda e: e.transpose(out, in_, ident), reads=[in_, ident], writes=[out])

    def act(self, out, in_, func, bias=None, scale=None):
        reads = [in_]
        kw = {}
        if bias is not None:
            kw['bias'] = bias
            if not isinstance(bias, (int, float)):
                reads.append(bias)
        if scale is not None:
            kw['scale'] = scale
            if not isinstance(scale, (int, float)):
                reads.append(scale)
        self.emit('act', lambda e: e.activation(out, in_, func, **kw), reads=reads, writes=[out])

    def cp(self, out, in_, eng='dve'):
        if eng == 'act':
            self.emit('act', lambda e: e.copy(out, in_), reads=[in_], writes=[out])
        else:
            self.emit(eng, lambda e: e.tensor_copy(out, in_), reads=[in_], writes=[out])

    def tt(self, out, a, b, op, eng='dve'):
        self.emit(eng, lambda e: e.tensor_tensor(out, a, b, op), reads=[a, b], writes=[out])

    def ts(self, out, a, s1, op0, s2=None, op1=None, eng='dve'):
        reads = [a] + [s for s in (s1, s2) if s is not None and not isinstance(s, (int, float))]
        if op1 is None:
            self.emit(eng, lambda e: e.tensor_scalar(out, a, s1, None, op0), reads=reads, writes=[out])
        else:
            self.emit(eng, lambda e: e.tensor_scalar(out, a, s1, s2, op0, op1), reads=reads, writes=[out])

    def stt(self, out, in0, scalar, in1, op0, op1):
        reads = [in0, in1] + ([] if isinstance(scalar, (int, float)) else [scalar])
        self.emit('dve', lambda e: e.scalar_tensor_tensor(out, in0, scalar, in1, op0, op1),
                  reads=reads, writes=[out])

    def scan(self, out, d0, d1):
        self.emit('dve', lambda e: e.tensor_tensor_scan(out, d0, d1, 0.0, ALU.mult, ALU.add),
                  reads=[d0, d1], writes=[out])

    def recip(self, out, in_):
        self.emit('dve', lambda e: e.reciprocal(out, in_), reads=[in_], writes=[out])

    def memset(self, out, val, eng='dve'):
        self.emit(eng, lambda e: e.memset(out, val), writes=[out])

    def dma(self, out, in_, eng='sp', cls='io', rd=None, wr=None):
        self.emit(eng, lambda e: e.dma_start(out=out, in_=in_),
                  reads=[in_] if rd is None else rd, writes=[out] if wr is None else wr, dma_class=cls)


def bc(ap, shape):
    return ap.to_broadcast(list(shape))


def _fm(v, nch):
    return np.ascontiguousarray(np.asarray(v, np.float32).reshape(nch, 128).T)


PV_LAYER = [('nmw', 8), ('nfw', 8), ('bada', 48), ('mu', 10), ('w0', 3), ('a0', 3), ('kk', 3), ('ka', 3),
            ('rk', 3), ('lnw', 3), ('lnb', 3), ('convw', 36), ('alog', 1), ('dtb', 1), ('gnw', 1),
            ('lbl', 2), ('hnw', 1), ('lora', 384)]
PV_OFF = {}
_o = 0
for _l in range(DEPTH):
    for _n, _w in PV_LAYER:
        PV_OFF[(_n, _l)] = (_o, _w)
        _o += _w
PV_OFF[('fnw', 0)] = (_o, 8)
_o += 8
NPV = _o


def pack_pvec(inp):
    pv = np.zeros((128, NPV), np.float32)

    def put(name, l, arr):
        o, w = PV_OFF[(name, l)]
        assert arr.shape == (128, w), (name, arr.shape, w)
        pv[:, o:o + w] = arr

    for l in range(DEPTH):
        put('nmw', l, _fm(inp['norm_mix_w'][l], 8))
        put('nfw', l, _fm(inp['norm_ffn_w'][l], 8))
        put('bada', l, _fm(inp['b_ada'][l], 48))
        put('mu', l, _fm(inp['rwkv_mu'][l], 10))
        put('w0', l, _fm(inp['rwkv_w0'][l], 3))
        put('a0', l, _fm(inp['rwkv_a0'][l], 3))
        put('kk', l, _fm(inp['rwkv_k_k'][l], 3))
        put('ka', l, _fm(inp['rwkv_k_a'][l], 3))
        put('rk', l, _fm(np.asarray(inp['rwkv_r_k'][l]).reshape(384), 3))
        put('lnw', l, _fm(inp['rwkv_ln_w'][l], 3))
        put('lnb', l, _fm(inp['rwkv_ln_b'][l], 3))
        cw = np.asarray(inp['gdn_conv_w'][l], np.float32).reshape(4, 9, 128)
        put('convw', l, np.ascontiguousarray(cw.transpose(2, 0, 1)).reshape(128, 36))
        t = np.zeros((128, 1), np.float32)
        t[0:6, 0] = inp['gdn_A_log'][l]
        put('alog', l, t)
        t = np.zeros((128, 1), np.float32)
        t[0:6, 0] = inp['gdn_dt_bias'][l]
        put('dtb', l, t)
        put('gnw', l, np.tile(np.asarray(inp['gdn_norm_w'][l], np.float32), 2).reshape(128, 1))
        put('lbl', l, _fm(inp['hgrn_lb_logits'][l], 2))
        put('hnw', l, np.tile(np.asarray(inp['hgrn_norm_w'][l], np.float32), 2).reshape(128, 1))
        lo = np.zeros((128, 384), np.float32)
        lo[0:32] = inp['rwkv_w2'][l]
        lo[32:64] = inp['rwkv_a2'][l]
        lo[64:128] = inp['rwkv_g2'][l]
        put('lora', l, lo)
    put('fnw', 0, _fm(inp['final_norm_w'], 8))
    return pv


CST_ITEMS = [('ident', 128), ('bones', 128), ('ones', 128), ('ident2', 64),
             ('mask5_p', 320), ('IU_p', 64), ('BSL_p', 64), ('BIU_p', 64), ('BS_p', 64),
             ('mask5_s', 320), ('IU_s', 64), ('BSL_s', 64), ('BIU_s', 64), ('BS_s', 64),
             ('ind', 16), ('osel', 6 * 64), ('selb', 3 * 128), ('rm_p', 384), ('rm_s', 192)]
CST_OFF = {}
_o = 0
for _n, _w in CST_ITEMS:
    CST_OFF[_n] = (_o, _w)
    _o += _w
NCST = _o


def make_consts():
    c = np.zeros((128, NCST), np.float32)

    def put(name, arr):
        o, w = CST_OFF[name]
        assert arr.shape[1] == w, (name, arr.shape, w)
        c[0:arr.shape[0], o:o + w] = arr

    def rep(a):
        return np.concatenate([a, a], axis=0)

    put('ident', np.eye(128, dtype=np.float32))
    p = np.arange(128)
    put('bones', (p[:, None] // 64 == p[None, :] // 64).astype(np.float32))
    put('ones', np.ones((128, 128), np.float32))
    put('ident2', rep(np.eye(64, dtype=np.float32)))
    i = np.arange(64)
    for v, seg in (('p', 64), ('s', 4)):
        same = (i[:, None] // seg == i[None, :] // seg)
        SU = ((i[:, None] < i[None, :]) & same).astype(np.float32)
        IU = ((i[:, None] <= i[None, :]) & same).astype(np.float32)
        SL = ((i[:, None] > i[None, :]) & same).astype(np.float32)
        put('mask5_' + v, rep(np.concatenate([SU, SU, IU, IU, SL], axis=1)))
        put('IU_' + v, rep(IU))
        put('BSL_' + v, rep(BIG * (1.0 - SL)))
        put('BIU_' + v, rep(BIG * (1.0 - IU)))
        put('BS_' + v, rep(same.astype(np.float32)))
    put('ind', rep((i[:, None] // 4 == np.arange(16)[None, :]).astype(np.float32)))
    osel = np.zeros((6, 6, 64), np.float32)
    for hh in range(6):
        osel[hh, hh, :] = 1.0
    put('osel', osel.reshape(6, 6 * 64))
    selb = np.zeros((6, 3, 128), np.float32)
    for hp in range(3):
        for m in range(128):
            selb[2 * hp + m // 64, hp, m] = 1.0
    put('selb', selb.reshape(6, 3 * 128))
    col = np.arange(384)
    put('rm_p', np.tile((col % 64 != 0).astype(np.float32)[None, :], (128, 1)))
    col = np.arange(192)
    put('rm_s', np.tile((col % 4 != 0).astype(np.float32)[None, :], (128, 1)))
    return c


class Blk:
    def __init__(self, kind, idx, n=None):
        self.kind = kind
        self.idx = idx
        if kind == 'p':
            self.n = BLK if n is None else n
            self.nsq = 1
            self.tb = self.n
            self.ntile = self.n // 64
            self.nseg = 1
            self.tok0 = idx * self.n
            self.w = 3 + self.n
            self.v = 'p'
            self.levels = 5
        else:
            self.n = TOKS
            self.nsq = NSEQ
            self.tb = TSS
            self.ntile = 1
            self.nseg = NSEQ
            self.tok0 = TOKP
            self.w = NSEQ * 7
            self.v = 's'
            self.levels = 1
        self.seglen = 64 // self.nseg


def build_program(dbg_names=(), stop_after=None):
    nc = bass.Bass("TRN2", target_bir_lowering=False)
    dr = {}

    def din(name, shape):
        dr[name] = nc.dram_tensor(name, list(shape), F32, kind="ExternalInput").ap()

    def dout(name, shape):
        dr[name] = nc.dram_tensor(name, list(shape), F32, kind="ExternalOutput").ap()

    din('xp', [TOKP, D]); din('xs', [TOKS, D]); din('c17', [17, D])
    din('shift_s', [DEPTH, NSEQ, D]); din('srw', [DEPTH, NSEQ, 6, 64, 64]); din('sconv', [DEPTH, NSEQ, 3, 1152])
    din('sgdn', [DEPTH, NSEQ, 6, 64, 64]); din('shg', [DEPTH, NSEQ, 4, 64, 64])
    din('w_ada', [DEPTH, D, 6 * D]); din('w_in', [DEPTH, D, IN_COLS]); din('w_out', [DEPTH, D, D])
    din('w_fi', [DEPTH, D, 2 * FF]); din('w_fo', [DEPTH, FF, D])
    din('pvec', [128, NPV]); din('cst', [128, NCST])
    dout('yp', [TOKP, D]); dout('ys', [TOKS, D])
    dout('o_pshift', [DEPTH, 1, D]); dout('o_prw', [DEPTH, 1, 6, 64, 64]); dout('o_pconv', [DEPTH, 1, 3, 1152])
    dout('o_pgdn', [DEPTH, 1, 6, 64, 64]); dout('o_phg', [DEPTH, 1, 4, 64, 64])
    dout('o_sshift', [DEPTH, NSEQ, D]); dout('o_srw', [DEPTH, NSEQ, 6, 64, 64]); dout('o_sconv', [DEPTH, NSEQ, 3, 1152])
    dout('o_sgdn', [DEPTH, NSEQ, 6, 64, 64]); dout('o_shg', [DEPTH, NSEQ, 4, 64, 64])
    xscr = nc.dram_tensor("xscr", [128, KC, TOKP + TOKS], F32, kind="Internal").ap()
    dbg_list = []
    dbg_t = {}
    for dn, dshape in dbg_names:
        dbg_t[dn] = nc.dram_tensor("dbg_" + dn, list(dshape), F32, kind="ExternalOutput").ap()
    out_keys = []

    with contextlib.ExitStack() as stack:
        B = KB(nc, stack)
        S = B.S
        okc = [0]

        def odma(dst, src):
            okc[0] += 1
            key = "out#%d" % okc[0]
            B.dma(dst, src, cls='out', wr=[key])

        if stop_after is not None and stop_after.startswith('#'):
            S.max_ops = int(stop_after[1:])

        def stage(name):
            if stop_after is not None and stop_after == name:
                S.disabled = True

        def dbg(name, ap, shape=None):
            if name in dbg_t and name not in dbg_list:
                dbg_list.append(name)
                odma(dbg_t[name], ap)

        cst = B.sb("cst", [128, NCST])
        pvec = B.sb("pvec", [128, NPV])
        modT = B.sb("modT", [128, DEPTH, 48, 17])
        AMt = B.sb("AMt", [128, DEPTH, 8, 17])
        AFt = B.sb("AFt", [128, DEPTH, 8, 17])
        LBt = B.sb("LBt", [128, DEPTH, 2])
        OMLt = B.sb("OMLt", [128, DEPTH, 2])
        B.dma(cst[:], dr['cst'])
        B.dma(pvec[:], dr['pvec'])

        def C(name, rows=128, c0=0, c1=None):
            o, w = CST_OFF[name]
            c1 = w if c1 is None else c1
            return cst[0:rows, o + c0:o + c1]

        def PV(name, l, rows=128, c0=0, c1=None):
            o, w = PV_OFF[(name, l)]
            c1 = w if c1 is None else c1
            return pvec[0:rows, o + c0:o + c1]

        ident = C('ident')
        ones = C('ones')
        bones = C('bones')
        evac_rr = [0]

        def evac(out, in_):
            evac_rr[0] += 1
            B.cp(out, in_, eng='act')

        def rsqrt_ln(out, in_, scale=1.0, bias=0.0):
            B.act(out, in_, AF.Ln, bias=bias, scale=scale)
            B.act(out, out, AF.Exp, scale=-0.5)

        def sig_finish(t, eng='dve'):
            B.ts(t, t, 1.0, ALU.add, eng=eng)
            B.recip(t, t)

        def silu_exp(out, x, tmp):
            B.act(tmp, x, AF.Exp, scale=-1.0)
            sig_finish(tmp)
            B.tt(out, x, tmp, ALU.mult)

        m0 = B.mark()
        c17t = B.sb("c17t", [17, D])
        cact = B.sb("cact", [17, D])
        cT = B.sb("cT", [128, KC, 17])
        wada = [B.sb("wada%d" % i, [128, KC, 512]) for i in range(4)]
        msm = [B.sb("msm%d" % i, [17, 512]) for i in range(2)]
        B.dma(c17t[:], dr['c17'])
        B.act(cact[:], c17t[:], AF.Silu)
        with B.bank() as pb:
            for kc in range(KC):
                B.tr(pb[:, kc * 17:(kc + 1) * 17], cact[0:17, kc * 128:(kc + 1) * 128], ident[0:17, 0:17])
            B.cp(cT[:].rearrange("p k s -> p (k s)"), pb[:, 0:KC * 17])
        stage('p1')
        nblk = 0
        for l in range(DEPTH):
            wsrc = dr['w_ada'][l].rearrange("(kc p) n -> p kc n", p=128)
            for cb in range(12):
                wt = wada[nblk % 4]
                ms = msm[nblk % 2]
                B.dma(wt[:], wsrc[:, :, cb * 512:(cb + 1) * 512], cls='w')
                nblk += 1
                with B.bank() as pb:
                    for kc in range(KC):
                        B.mm(pb[0:17, :], cT[:, kc, :], wt[:, kc, :], start=(kc == 0), stop=(kc == KC - 1))
                    B.cp(ms[:], pb[0:17, :], eng='act')
                with B.bank() as pb:
                    for q in range(4):
                        B.tr(pb[:, q * 17:(q + 1) * 17], ms[0:17, q * 128:(q + 1) * 128], ident[0:17, 0:17])
                    B.tt(modT[:, l, 4 * cb:4 * cb + 4, :], pb[:, 0:68].rearrange("p (q s) -> p q s", s=17),
                         bc(PV('bada', l, c0=4 * cb, c1=4 * cb + 4).unsqueeze(2), [128, 4, 17]), ALU.add)
                stage('p2_%d_%d' % (l, cb))
            B.ts(AMt[:, l], modT[:, l, 8:16, :], 1.0, ALU.add)
            B.tt(AMt[:, l], AMt[:, l], bc(PV('nmw', l).unsqueeze(2), [128, 8, 17]), ALU.mult)
            B.ts(AFt[:, l], modT[:, l, 32:40, :], 1.0, ALU.add)
            B.tt(AFt[:, l], AFt[:, l], bc(PV('nfw', l).unsqueeze(2), [128, 8, 17]), ALU.mult)
        B.memset(LBt[:], 0.0)
        B.tt(LBt[:, 1, :], PV('lbl', 1), PV('lbl', 0), ALU.subtract)
        B.act(LBt[:, 1, :], LBt[:, 1, :], AF.Sigmoid)
        B.ts(OMLt[:], LBt[:], -1.0, ALU.mult, 1.0, ALU.add)
        dbg('modT', modT[:].rearrange("p l m s -> p (l m s)"), [128, DEPTH * 48 * 17])
        B.release(m0)
        stage('prologue')

        def MOD(l, which):
            if which == 'AM':
                return AMt[:, l]
            if which == 'AF':
                return AFt[:, l]
            lo = {'BM': 0, 'GM': 16, 'BF': 24, 'GF': 40}[which]
            return modT[:, l, lo:lo + 8, :]

        def seqcols(blk):
            return (0, 1) if blk.kind == 'p' else (1, 17)

        def v3(ap, blk):
            return ap.rearrange("p (s t) -> p s t", t=blk.tb)

        def rms_stats(xb, n, sq, rstd):
            with B.bank() as pb:
                for kc in range(KC):
                    B.act(sq[kc % 2][:, 0:n], xb[:, kc, 0:n], AF.Square)
                    B.mm(pb[:, 0:n], ones, sq[kc % 2][:, 0:n], start=(kc == 0), stop=(kc == KC - 1))
                rsqrt_ln(rstd[:, 0:n], pb[:, 0:n], scale=1.0 / D, bias=1e-6)

        def norm_mod(blk, xb, n, l, Aw, Bw, hdst_fn, sq, rstd, tmpn):
            rms_stats(xb, n, sq, rstd)
            s0, s1 = seqcols(blk)
            A = MOD(l, Aw)
            Bm = MOD(l, Bw)
            for kc in range(KC):
                tn = tmpn[kc % 2]
                B.tt(tn[:, 0:n], xb[:, kc, 0:n], rstd[:, 0:n], ALU.mult, eng='pool')
                if blk.kind == 'p':
                    B.act(hdst_fn(kc), tn[:, 0:n].rearrange("p (s t) -> p s t", s=1), AF.Identity,
                          bias=Bm[:, kc, s0:s1], scale=A[:, kc, s0:s1])
                else:
                    t3 = v3(tn[:, 0:n], blk)
                    B.tt(t3, t3, bc(A[:, kc, s0:s1].unsqueeze(2), [128, blk.nsq, blk.tb]), ALU.mult)
                    B.tt(hdst_fn(kc), t3, bc(Bm[:, kc, s0:s1].unsqueeze(2), [128, blk.nsq, blk.tb]), ALU.add)

        def gate_res(blk, xdst, pbsrc, l, Gw, kc, n, tmpn):
            s0, s1 = seqcols(blk)
            G = MOD(l, Gw)
            if blk.kind == 'p':
                B.stt(xdst, pbsrc, G[:, kc, s0:s1], xdst, ALU.mult, ALU.add)
            else:
                tn = tmpn[kc % 2]
                t3 = v3(tn[:, 0:n], blk)
                B.tt(t3, v3(pbsrc, blk), bc(G[:, kc, s0:s1].unsqueeze(2), [128, blk.nsq, blk.tb]), ALU.mult)
                B.tt(xdst, xdst, tn[:, 0:n], ALU.add)

        prompt_blocks = [Blk('p', i) for i in range(TOKP // BLK)]
        sample_block = Blk('s', 0)
        ffn_blocks = [Blk('p', i, n=FBLK) for i in range(TOKP // FBLK)] + [Blk('s', 0)]

        def xkey(phase, lay, tok0, n):
            return ["xscr#%d" % g for g in range(tok0 // 64, (tok0 + n + 63) // 64)]

        hsl = [slice(0, 64), slice(64, 128)]

        for l in range(DEPTH):
            mph = B.mark()
            WIN = [B.sb("w_inA", [128, KC, A_COLS], BF16), B.sb("w_inB", [128, KC, B_COLS], BF16),
                   B.sb("w_inC", [128, KC, C_COLS], BF16)]
            WIN_OFF = [0, A_COLS, A_COLS + B_COLS, IN_COLS]
            w_out_sb = B.sb("w_out", [128, KC, D], BF16)
            wsrc = dr['w_in'][l].rearrange("(kc p) n -> p kc n", p=128)
            for gi in range(3):
                for kc in range(0, KC, 4):
                    B.dma(WIN[gi][:, kc:kc + 4, :], wsrc[:, kc:kc + 4, WIN_OFF[gi]:WIN_OFF[gi + 1]], eng='pool', cls='w')

            def w_in_ap(kc, c0, m):
                gi = 0 if c0 < WIN_OFF[1] else (1 if c0 < WIN_OFF[2] else 2)
                assert c0 + m <= WIN_OFF[gi + 1]
                return WIN[gi][:, kc, c0 - WIN_OFF[gi]:c0 - WIN_OFF[gi] + m]
            wsrc = dr['w_out'][l].rearrange("(kc p) n -> p kc n", p=128)
            for kc in range(0, KC, 4):
                B.dma(w_out_sb[:, kc:kc + 4, :], wsrc[:, kc:kc + 4, :], eng='pool', cls='w')

            xb = B.sb("xb", [128, KC, BLK])
            STAGE = B.sb("STAGE", [128, 1152])
            xtm = STAGE[:, 0:D]
            sq = [B.sb("sq%d" % i, [128, BLK]) for i in range(2)]
            rstd = B.sb("rstd", [128, BLK])
            tmpn = [B.sb("tmpn%d" % i, [128, BLK]) for i in range(2)]
            hT_p = B.sb("hT_p", [128, KC, 8 + BLK], BF16)
            hT_s = B.sb("hT_s", [128, KC, NSEQ * 5], BF16)
            PB = B.sb("PB", [128, 10, 3 + BLK])
            ZB = B.sb("ZB", [128, 3, BLK])
            BETA = B.sb("BETA", [6, BLK])
            APRE = B.sb("APRE", [6, BLK])
            yT = B.sb("yT", [128, 8, BLK], BF16)
            SL3 = [B.sb("slot%d" % i, [128, 3, BLK]) for i in range(12)]
            XS = B.sb("XS", [128, 10, BLK])
            TL = B.sb("TL", [128, BLK])
            class _TS:
                pass
            NT = BLK // 64
            TSets = []
            for ci in range(3):
                t_ = _TS()
                t_.TM = B.sb("TM", [128, NT, 4, 64])
                t_.A4 = B.sb("A4", [128, NT, 4, 64])
                t_.NM = B.sb("NM", [128, NT, 64])
                t_.Pm = B.sb("Pm", [128, NT, 64])
                t_.NQ = [B.sb("NQ%d" % i, [128, NT, 2, 64]) for i in range(2)]
                t_.WTs = B.sb("WTs", [128, NT, 64])
                t_.Xs = B.sb("Xs", [128, NT, 64])
                t_.U0s = B.sb("U0s", [128, NT, 64])
                t_.U0Ts = t_.U0s
                t_.Us = B.sb("Us", [128, 64])
                t_.UTs = B.sb("UTs", [128, 64])
                t_.SCP = B.sb("SCP", [128, NT, 4])
                t_.SCD = B.sb("SCD", [128, NT, 4])
                t_.GSL = B.sb("GSL", [128, NT, 64])
                t_.GUI = B.sb("GUI", [128, NT, 64])
                t_.Qs = B.sb("Qs", [128, NT, 64])
                t_.KB3 = B.sb("KB3", [128, NT, 3, 64])
                t_.EBC = B.sb("EBC", [128, NT, 64])
                t_.QTg = B.sb("QTg", [128, NT, 64])
                TSets.append(t_)
            TB = _TS()

            def use_set(ci):
                TB.__dict__.update(TSets[ci].__dict__)
            use_set(0)
            SVM = B.sb("SVM", [128, 32, 64])
            UM = SVM[:, 0:16, :]
            VM = SVM[:, 16:32, :]
            Mp = {}
            for mx, npair in (('a', 3), ('b', 3), ('c', 2)):
                for hp in range(npair):
                    Mp[(mx, hp)] = B.sb("Mp_%s%d" % (mx, hp), [128, 1, 64])
            Ms = B.sb("Ms", [128, NSEQ, 64])
            STG = SVM[0:64, :, :].rearrange("p (s h) k -> p s h k", h=2)
            OSH = xtm[0:16, :]
            OCV = STAGE[0:48, :]
            G6 = B.sb("G6", [6, BLK])
            GC6 = B.sb("GC6", [6, BLK])
            BE6 = B.sb("BE6", [6, BLK])
            EA6 = B.sb("EA6", [6, 1])

            NEGB = B.sb("NEGB", [128, 6])
            B.ts(NEGB[:, 0:3], PV('w0', l), -1.0, ALU.mult)
            B.ts(NEGB[:, 3:6], PV('a0', l), -1.0, ALU.mult)
            B.memset(TL[:], 0.0, eng='pool')
            B.memset(hT_p[:], 0.0)
            for key in Mp:
                B.memset(Mp[key][:], 0.0, eng='pool')

            def load_x_block(blk):
                n = blk.n
                if l == 0:
                    if blk.kind == 'p':
                        for ti in range(n // 128):
                            B.dma(xtm[:], dr['xp'][blk.tok0 + ti * 128: blk.tok0 + (ti + 1) * 128, :])
                            for g in range(2):
                                with B.bank() as pb:
                                    for q in range(4):
                                        kc = 4 * g + q
                                        B.tr(pb[:, q * 128:(q + 1) * 128], xtm[:, kc * 128:(kc + 1) * 128], ident)
                                    B.cp(xb[:, 4 * g:4 * g + 4, ti * 128:(ti + 1) * 128],
                                         pb[:, 0:512].rearrange("p (q t) -> p q t", q=4), eng=('act' if g else 'dve'))
                    else:
                        B.dma(xtm[0:64, :], dr['xs'])
                        with B.bank() as pb:
                            for kc in range(KC):
                                B.tr(pb[:, kc * 64:(kc + 1) * 64], xtm[0:64, kc * 128:(kc + 1) * 128], ident[0:64, 0:64])
                            B.cp(xb[:, :, 0:64], pb[:, 0:512].rearrange("p (q t) -> p q t", q=8))
                else:
                    B.dma(xb[:, :, 0:n], xscr[:, :, blk.tok0:blk.tok0 + n], rd=xkey(0, l, blk.tok0, n))

            def PBv(blk, j0, j1, off):
                if blk.kind == 'p':
                    return PB[:, j0:j1, off:off + blk.tb].unsqueeze(2)
                return PB[:, j0:j1, 0:blk.w].rearrange("p j (s u) -> p j s u", u=7)[:, :, :, off:off + blk.tb]

            def c4(ap, blk):
                return ap.rearrange("p j (s t) -> p j s t", t=blk.tb)

            def proj(blk, cols, dst_fn):
                c0, m = cols
                with B.bank() as pb:
                    src = hT_p if blk.kind == 'p' else hT_s
                    N = 4 + blk.n if blk.kind == 'p' else NSEQ * 5
                    for kc in range(KC):
                        B.mm(pb[0:m, 0:N], w_in_ap(kc, c0, m), src[:, kc, 0:N],
                             start=(kc == 0), stop=(kc == KC - 1))
                    dst_fn(pb)

            def proj_to_PB(blk, cols, j, halo):
                def f(pb):
                    m = cols[1]
                    if blk.kind == 'p':
                        evac(PB[0:m, j, 3 - halo:3 + blk.n], pb[0:m, 3 - halo:3 + blk.n])
                    else:
                        h = 1 if halo == 1 else 0
                        src = pb[0:m, 0:NSEQ * 5].rearrange("p (s u) -> p s u", u=5)[:, :, 1 - h:5]
                        dst = PB[0:m, j, 0:blk.w].rearrange("p (s u) -> p s u", u=7)[:, :, 3 - h:7]
                        evac(dst, src)
                proj(blk, cols, f)

            def proj_to(blk, cols, dst):
                def f(pb):
                    m = cols[1]
                    if blk.kind == 'p':
                        evac(dst, pb[0:m, 3:3 + blk.n])
                    else:
                        src = pb[0:m, 0:NSEQ * 5].rearrange("p (s u) -> p s u", u=5)[:, :, 1:5]
                        evac(dst.rearrange("p (s t) -> p s t", t=4), src)
                proj(blk, cols, f)

            def wy_double(blk, N0, Q0):
                nt = blk.ntile
                B.tt(TB.Pm[:, 0:nt, :], Q0, bc(C('ident2').unsqueeze(1), [128, nt, 64]), ALU.add)
                Ncur = N0
                Qcur = Q0
                for lev in range(blk.levels):
                    last = (lev == blk.levels - 1)
                    nq = TB.NQ[lev % 2]
                    with B.bank() as pb:
                        pv = pb[:, 0:nt * 128].rearrange("p (n a t) -> p n a t", n=nt, a=2)
                        for ti in range(nt):
                            for h2 in range(2):
                                hs = hsl[h2]
                                B.mm(pv[hs, ti, 0, :], Qcur[hs, ti, :], Ncur[hs, ti, :])
                                if not last:
                                    B.mm(pv[hs, ti, 1, :], Ncur[hs, ti, :], Qcur[hs, ti, :])
                        if last:
                            evac(nq[:, 0:nt, 0, :], pv[:, :, 0, :])
                        else:
                            evac(nq[:, 0:nt], pv)
                    with B.bank() as pb:
                        pv = pb[:, 0:nt * 64].rearrange("p (n t) -> p n t", n=nt)
                        for ti in range(nt):
                            for h2 in range(2):
                                hs = hsl[h2]
                                B.mm(pv[hs, ti, :], nq[hs, ti, 0, :], TB.Pm[hs, ti, :])
                        B.tt(TB.Pm[:, 0:nt, :], TB.Pm[:, 0:nt, :], pv, ALU.add)
                    Ncur = nq[:, 0:nt, 0, :]
                    Qcur = nq[:, 0:nt, 1, :]

            def seq_step(blk, M, WT, U0, U0T, ArbT, y_extra, rt_fn, b_tm, upd_extra, DC_fn, mode, ydst,
                         Vmask=None):
                nseg = blk.nseg
                sl = blk.seglen
                if WT is not None:
                    if blk.kind == 'p':
                        with B.bank() as pb:
                            for h2 in range(2):
                                hs = hsl[h2]
                                B.mm(pb[hs, 0:64], WT[hs, :], M[hs, 0, :])
                            B.tt(TB.Us[:], pb[:, 0:64], U0[:], ALU.add)
                    else:
                        with B.bank() as pb:
                            for h2 in range(2):
                                hs = hsl[h2]
                                for sg in range(nseg):
                                    B.mm(pb[hs, sg * sl:(sg + 1) * sl], M[hs, sg, :], WT[hs, sg * sl:(sg + 1) * sl])
                            B.tt(TB.UTs[:], pb[:, 0:64], U0T[:], ALU.add)
                        with B.bank() as pb:
                            for h2 in range(2):
                                hs = hsl[h2]
                                B.mm(pb[hs, 0:64], TB.UTs[hs, :], ident[hs, hs])
                            evac(TB.Us[:], pb[:, 0:64])
                with B.bank() as pb:
                    for h2 in range(2):
                        hs = hsl[h2]
                        first = True
                        if WT is not None:
                            B.mm(pb[hs, 0:64], TB.Us[hs, :], ArbT[hs, :], start=True, stop=False)
                            first = False
                        if y_extra is not None:
                            B.mm(pb[hs, 0:64], y_extra[0][hs, :], y_extra[1][hs, :], start=first, stop=False)
                            first = False
                        for sg in range(nseg):
                            cc = slice(sg * sl, (sg + 1) * sl)
                            B.mm(pb[hs, cc], M[hs, sg, :], rt_fn(cc)[hs, :], start=False, stop=True)
                    evac(ydst, pb[:, 0:64])
                nb = (nseg * 64 + 511) // 512
                with B.bank() as pb0, B.bank() as pb1:
                    pbs = [pb0, pb1]
                    if blk.kind == 's' and WT is not None:
                        B.tt(UM[:], bc(TB.Us[:].unsqueeze(1), [128, 16, 64]),
                             bc(C('ind').unsqueeze(2), [128, 16, 64]), ALU.mult, eng='pool')
                    for h2 in range(2):
                        hs = hsl[h2]
                        for half in range(nb):
                            ncols = min(512, nseg * 64 - half * 512)
                            g0 = half * 8
                            first = True
                            if WT is not None:
                                if blk.kind == 'p':
                                    rhsU = TB.Us[hs, :]
                                else:
                                    rhsU = UM[hs, g0:g0 + 8, :].rearrange("p s v -> p (s v)")
                                B.mm(pbs[half][hs, 0:ncols], b_tm[hs, :], rhsU, start=True, stop=(upd_extra is None))
                                first = False
                            if upd_extra is not None:
                                lt, rv = upd_extra
                                if blk.kind == 'p':
                                    rhsV = rv[hs, :]
                                else:
                                    rhsV = Vmask[hs, g0:g0 + 8, :].rearrange("p s v -> p (s v)")
                                B.mm(pbs[half][hs, 0:ncols], lt[hs, :], rhsV, start=first, stop=True)
                    DC = DC_fn()
                    for half in range(nb):
                        nsg = min(8, nseg - half * 8)
                        g0 = half * 8
                        Mv = M[:, g0:g0 + nsg, :]
                        pv = pbs[half][:, 0:nsg * 64].rearrange("p (s v) -> p s v", v=64)
                        DCb = bc(DC[:, g0:g0 + nsg].unsqueeze(2), [128, nsg, 64])
                        if mode == 'rwkv':
                            B.tt(Mv, Mv, pv, ALU.add)
                            B.tt(Mv, Mv, DCb, ALU.mult, eng='pool')
                        elif blk.kind == 'p':
                            B.stt(M[:, 0, :], M[:, 0, :], DC[:, 0:1], pbs[half][:, 0:64], ALU.mult, ALU.add)
                        else:
                            B.tt(Mv, Mv, DCb, ALU.mult, eng='pool')
                            B.tt(Mv, Mv, pv, ALU.add)

            def segend(ap2d, blk, cs):
                t = ap2d[:, cs]
                return t.rearrange("p (s u) -> p s u", u=blk.seglen)[:, :, blk.seglen - 1]

            def vmask_build(v_tm):
                B.tt(VM[:], bc(v_tm.unsqueeze(1), [128, 16, 64]),
                     bc(C('ind').unsqueeze(2), [128, 16, 64]), ALU.mult, eng='pool')

            def state_io_in(mx, hp, src):
                if mx == 'a':
                    for h2 in range(2):
                        B.dma(STG[:, :, h2, :], src[:, 2 * hp + h2, :, :].rearrange("s v k -> v s k"))
                    for g in range(2):
                        with B.bank() as pb:
                            for q in range(8):
                                sg = 8 * g + q
                                B.tr(pb[:, q * 64:(q + 1) * 64], STG[:, sg].rearrange("p h k -> p (h k)"),
                                     ident[0:64, 0:64])
                            evac(Ms[:, 8 * g:8 * g + 8, :].rearrange("p s v -> p (s v)"), pb[:, 0:512])
                else:
                    for h2 in range(2):
                        B.dma(Ms[64 * h2:64 * h2 + 64, :, :], src[:, 2 * hp + h2, :, :].rearrange("s k v -> k s v"))

            def state_io_out(hp, M, nsq, dst):
                for h2 in range(2):
                    odma(dst[:, 2 * hp + h2, :, :].rearrange("s k v -> k s v"), M[64 * h2:64 * h2 + 64, 0:nsq, :])

            def state_out_rwkv(hp, M, nsq, dst):
                so = hp if nsq == 1 else 0
                for g in range((nsq + 3) // 4):
                    ns = min(4, nsq - 4 * g)
                    with B.bank() as pb:
                        for q in range(ns):
                            B.tr(pb[0:64, q * 128:(q + 1) * 128], M[:, 4 * g + q, :], ident)
                        evac(STG[:, so + 4 * g:so + 4 * g + ns].rearrange("p s h k -> p (s h k)"), pb[0:64, 0:ns * 128])
                for h2 in range(2):
                    odma(dst[:, 2 * hp + h2, :, :].rearrange("s v k -> v s k"), STG[:, so:so + nsq, h2, :])

            def run_pairs(blk, fn, npair):
                if blk.kind == 's':
                    for hp in range(npair):
                        use_set(0)
                        fn(hp)
                    return
                for hp in range(npair):
                    use_set(hp)
                    with B.in_chain(hp):
                        fn(hp)
                B.flush_chains()
                use_set(0)

            def process_block(blk, is_last):
                n = blk.n
                v = blk.v
                T = blk.tb
                nsq = blk.nsq
                load_x_block(blk)
                if blk.kind == 'p':
                    hdst = lambda kc: hT_p[:, kc, 3:3 + n].rearrange("p (s t) -> p s t", s=1)
                else:
                    B.dma(OSH[:], dr['shift_s'][l])
                    with B.bank() as pb:
                        for kc in range(KC):
                            B.tr(pb[:, kc * 16:(kc + 1) * 16], OSH[0:16, kc * 128:(kc + 1) * 128], ident[0:16, 0:16])
                        B.cp(hT_s[:].rearrange("p k (s u) -> p k s u", u=5)[:, :, :, 0],
                             pb[:, 0:128].rearrange("p (k s) -> p k s", s=16))
                    hdst = lambda kc: hT_s[:, kc, :].rearrange("p (s u) -> p s u", u=5)[:, :, 1:5]
                if (l, blk.kind, blk.idx) == DBG_AT:
                    dbg('xin', xb[:].rearrange("p c t -> p (c t)"))
                norm_mod(blk, xb, n, l, 'AM', 'BM', hdst, sq, rstd, tmpn)
                if (l, blk.kind, blk.idx) == DBG_AT:
                    dbg('rstd', rstd[:])
                stage('h_%s%d_%d' % (blk.kind, blk.idx, l))

                for j in range(10):
                    proj_to_PB(blk, (128 * j, 128), j, 1)
                (SG, AA, G_, KK, K2, BBt, CUM, EP, EM, EPV, T1) = SL3[0:11]
                KT, BT_, AT = K2, BBt, KK
                cur = PBv(blk, 0, 10, 3)
                prv = PBv(blk, 0, 10, 2)
                XS4 = c4(XS[:, :, 0:n], blk)
                mu4 = bc(PV('mu', l).unsqueeze(2).unsqueeze(3), [128, 10, nsq, T])
                B.tt(XS4, prv, cur, ALU.subtract, eng='pool')
                B.tt(XS4, XS4, mu4, ALU.mult)
                B.tt(XS4, XS4, cur, ALU.add, eng='pool')
                if blk.kind == 'p' and blk.idx == 0 and l == 0:
                    dbg('XS0', XS[:].rearrange("p j t -> p (j t)"), [128, 10 * BLK])
                if (l, blk.kind, blk.idx) == DBG_AT:
                    dbg('XSa', XS[:].rearrange("p j t -> p (j t)"))
                B.act(TL[0:32, 0:n], XS[0:32, 9, 0:n], AF.Exp, scale=2.0)
                B.act(TL[64:128, 0:n], XS[64:128, 9, 0:n], AF.Exp, scale=-1.0)
                B.ts(TL[:, 0:n], TL[:, 0:n], 1.0, ALU.add)
                B.recip(TL[0:32, 0:n], TL[0:32, 0:n])
                B.recip(TL[64:128, 0:n], TL[64:128, 0:n])
                B.ts(TL[0:32, 0:n], TL[0:32, 0:n], -2.0, ALU.mult, 1.0, ALU.add)
                lora = PV('lora', l)
                for c in range(3):
                    with B.bank() as pb:
                        B.mm(pb[:, 0:n], lora[0:32, c * 128:(c + 1) * 128], TL[0:32, 0:n])
                        B.act(SG[:, c, 0:n], pb[:, 0:n], AF.Exp, bias=NEGB[:, c:c + 1], scale=-1.0)
                    with B.bank() as pb:
                        B.mm(pb[:, 0:n], lora[32:64, c * 128:(c + 1) * 128], XS[32:64, 9, 0:n])
                        B.act(AA[:, c, 0:n], pb[:, 0:n], AF.Exp, bias=NEGB[:, 3 + c:4 + c], scale=-1.0)
                    with B.bank() as pb:
                        B.mm(pb[:, 0:n], lora[64:128, c * 128:(c + 1) * 128], TL[64:128, 0:n])
                        evac(G_[:, c, 0:n], pb[:, 0:n])
                sig_finish(SG[:, :, 0:n])
                sig_finish(AA[:, :, 0:n], eng='pool')
                for c in range(3):
                    B.scan(CUM[:, c, 0:n], C('rm_' + v, c1=n), SG[:, c, 0:n])
                B.act(EP[:, :, 0:n], CUM[:, :, 0:n], AF.Exp, scale=-C0)
                B.act(EM[:, :, 0:n], CUM[:, :, 0:n], AF.Exp, scale=C0)
                B.tt(EPV[:, :, 0:n], CUM[:, :, 0:n], SG[:, :, 0:n], ALU.subtract, eng='pool')
                B.act(EPV[:, :, 0:n], EPV[:, :, 0:n], AF.Exp, scale=-C0)
                r_ = XS[:, 0:3, 0:n]
                k_ = XS[:, 3:6, 0:n]

                def b3(name):
                    return bc(PV(name, l).unsqueeze(2), [128, 3, n])
                B.tt(KK[:, :, 0:n], k_, b3('kk'), ALU.mult)
                B.act(T1[:, :, 0:n], KK[:, :, 0:n], AF.Square)
                for c in range(3):
                    with B.bank() as pb:
                        B.mm(pb[:, 0:n], bones, T1[:, c, 0:n])
                        rsqrt_ln(T1[:, c, 0:n], pb[:, 0:n], bias=1e-6)
                B.tt(KK[:, :, 0:n], KK[:, :, 0:n], T1[:, :, 0:n], ALU.mult)
                B.tt(K2[:, :, 0:n], AA[:, :, 0:n], b3('ka'), ALU.mult, eng='pool')
                B.tt(K2[:, :, 0:n], K2[:, :, 0:n], b3('ka'), ALU.subtract, eng='pool')
                B.stt(K2[:, :, 0:n], K2[:, :, 0:n], 1.0, k_, ALU.add, ALU.mult)
                B.tt(BBt[:, :, 0:n], KK[:, :, 0:n], AA[:, :, 0:n], ALU.mult, eng='pool')
                BON = AA
                B.tt(T1[:, :, 0:n], r_, K2[:, :, 0:n], ALU.mult)
                B.tt(T1[:, :, 0:n], T1[:, :, 0:n], b3('rk'), ALU.mult, eng='pool')
                for c in range(3):
                    with B.bank() as pb:
                        B.mm(pb[:, 0:n], bones, T1[:, c, 0:n])
                        B.tt(BON[:, c, 0:n], pb[:, 0:n], XS[:, 6 + c, 0:n], ALU.mult)
                RT = XS
                B.tt(RT[:, 0:3, 0:n], r_, EP[:, :, 0:n], ALU.mult)
                B.tt(KT[:, :, 0:n], K2[:, :, 0:n], EM[:, :, 0:n], ALU.mult, eng='pool')
                B.tt(BT_[:, :, 0:n], BBt[:, :, 0:n], EM[:, :, 0:n], ALU.mult)
                B.stt(AT[:, :, 0:n], KK[:, :, 0:n], -1.0, EPV[:, :, 0:n], ALU.mult, ALU.mult)
                YR = SG
                stage('rwkvprep_%s%d_%d' % (blk.kind, blk.idx, l))
                def rwkv_pair(hp):
                    nt = blk.ntile
                    if blk.kind == 's':
                        state_io_in('a', hp, dr['srw'][l])
                        Mst = Ms
                    else:
                        Mst = Mp[('a', hp)]
                    css = [slice(64 * ti, 64 * ti + 64) for ti in range(nt)]
                    with B.bank() as pb:
                        pv = pb[:, 0:nt * 256].rearrange("p (n q c) -> p n q c", n=nt, q=4)
                        for ti in range(nt):
                            for h2 in range(2):
                                hs = hsl[h2]
                                for q, src in enumerate((AT, BT_, KT, None)):
                                    s_ap = XS[hs, 6 + hp, css[ti]] if src is None else src[hs, hp, css[ti]]
                                    B.mm(pv[hs, ti, q, :], s_ap, ident[hs, hs])
                        evac(TB.TM[:, 0:nt], pv)
                    with B.bank() as pb1, B.bank() as pb2:
                        p4 = pb1[:, 0:nt * 256].rearrange("p (n a t) -> p n a t", n=nt, a=4)
                        pn = pb2[:, 0:nt * 64].rearrange("p (n t) -> p n t", n=nt)
                        for ti in range(nt):
                            for h2 in range(2):
                                hs = hsl[h2]
                                at = AT[hs, hp, css[ti]]
                                bt = BT_[hs, hp, css[ti]]
                                kt = KT[hs, hp, css[ti]]
                                rt = RT[hs, hp, css[ti]]
                                B.mm(p4[hs, ti, 0, :], bt, at)
                                B.mm(p4[hs, ti, 1, :], kt, at)
                                B.mm(p4[hs, ti, 2, :], bt, rt)
                                B.mm(p4[hs, ti, 3, :], kt, rt)
                                B.mm(pn[hs, ti, :], at, bt)
                        m4 = C('mask5_' + v, c1=256).rearrange("p (a t) -> p a t", a=4)
                        B.tt(TB.A4[:, 0:nt], p4, bc(m4.unsqueeze(1), [128, nt, 4, 64]), ALU.mult)
                        B.tt(TB.NM[:, 0:nt], pn, bc(C('mask5_' + v, c0=256, c1=320).unsqueeze(1), [128, nt, 64]), ALU.mult)
                    wy_double(blk, TB.NM[:, 0:nt, :], TB.A4[:, 0:nt, 0, :])
                    with B.bank() as pb:
                        pv = pb[:, 0:nt * 64].rearrange("p (n t) -> p n t", n=nt)
                        for ti in range(nt):
                            for h2 in range(2):
                                hs = hsl[h2]
                                B.mm(pv[hs, ti, :], TB.TM[hs, ti, 0, :], TB.Pm[hs, ti, :])
                        evac(TB.WTs[:, 0:nt], pv)
                    with B.bank() as pb:
                        pv = pb[:, 0:nt * 64].rearrange("p (n t) -> p n t", n=nt)
                        for ti in range(nt):
                            for h2 in range(2):
                                hs = hsl[h2]
                                B.mm(pv[hs, ti, :], TB.A4[hs, ti, 1, :], TB.TM[hs, ti, 3, :])
                        evac(TB.Xs[:, 0:nt], pv)
                    with B.bank() as pb:
                        pv = pb[:, 0:nt * 64].rearrange("p (n t) -> p n t", n=nt)
                        for ti in range(nt):
                            for h2 in range(2):
                                hs = hsl[h2]
                                if blk.kind == 'p':
                                    B.mm(pv[hs, ti, :], TB.Pm[hs, ti, :], TB.Xs[hs, ti, :])
                                else:
                                    B.mm(pv[hs, ti, :], TB.Xs[hs, ti, :], TB.Pm[hs, ti, :])
                        evac(TB.U0s[:, 0:nt], pv)
                    if blk.kind == 's':
                        vmask_build(TB.TM[:, 0, 3, :])
                    for ti in range(nt):
                        cs = css[ti]
                        seq_step(blk, Mst, TB.WTs[:, ti, :], TB.U0s[:, ti, :], TB.U0Ts[:, ti, :],
                                 ArbT=TB.A4[:, ti, 2, :],
                                 y_extra=(TB.TM[:, ti, 3, :], TB.A4[:, ti, 3, :]),
                                 rt_fn=lambda cc, hp=hp, cs=cs: RT[:, hp, cs][:, cc],
                                 b_tm=TB.TM[:, ti, 1, :],
                                 upd_extra=(TB.TM[:, ti, 2, :], TB.TM[:, ti, 3, :]),
                                 DC_fn=lambda hp=hp, cs=cs: segend(EP[:, hp, :], blk, cs),
                                 mode='rwkv', ydst=YR[:, hp, cs], Vmask=VM)
                    if blk.kind == 's':
                        state_out_rwkv(hp, Ms, NSEQ, dr['o_srw'][l])
                    elif is_last:
                        state_out_rwkv(hp, Mst, 1, dr['o_prw'][l])

                bb0 = A_COLS
                for j in range(9):
                    proj_to_PB(blk, (bb0 + 128 * j, 128), j, 3)
                proj_to(blk, (bb0 + 1152, 6), BETA[0:6, 0:n])
                proj_to(blk, (bb0 + 1158, 6), APRE[0:6, 0:n])
                for j in range(3):
                    proj_to(blk, (bb0 + 1164 + 128 * j, 128), ZB[:, j, 0:n])
                run_pairs(blk, rwkv_pair, 3)
                stage('rwkvchain_%s%d_%d' % (blk.kind, blk.idx, l))
                for c in range(3):
                    with B.bank() as pb:
                        B.mm(pb[:, 0:n], bones, YR[:, c, 0:n])
                        B.act(CUM[:, c, 0:n], pb[:, 0:n], AF.Copy, scale=1.0 / 64)
                mean = CUM[:, :, 0:n]
                B.tt(YR[:, :, 0:n], YR[:, :, 0:n], mean, ALU.subtract, eng='pool')
                B.act(T1[:, :, 0:n], YR[:, :, 0:n], AF.Square)
                for c in range(3):
                    with B.bank() as pb:
                        B.mm(pb[:, 0:n], bones, T1[:, c, 0:n])
                        rsqrt_ln(T1[:, c, 0:n], pb[:, 0:n], scale=1.0 / 64, bias=64e-5)
                B.tt(YR[:, :, 0:n], YR[:, :, 0:n], T1[:, :, 0:n], ALU.mult)
                B.tt(YR[:, :, 0:n], YR[:, :, 0:n], b3('lnw'), ALU.mult, eng='pool')
                B.tt(YR[:, :, 0:n], YR[:, :, 0:n], b3('lnb'), ALU.add)
                B.tt(YR[:, :, 0:n], YR[:, :, 0:n], BON[:, :, 0:n], ALU.add, eng='pool')
                if (l, blk.kind, blk.idx) == DBG_AT:
                    dbg('YRa', YR[:].rearrange("p c t -> p (c t)"))
                    dbg('T1a', T1[:].rearrange("p c t -> p (c t)"))
                    dbg('Ga', G_[:].rearrange("p c t -> p (c t)"))
                B.tt(yT[:, 0:3, 0:n], YR[:, :, 0:n], G_[:, :, 0:n], ALU.mult)

                stage('rwkv_%s%d_%d' % (blk.kind, blk.idx, l))
                if blk.kind == 's':
                    B.dma(OCV[:], dr['sconv'][l].rearrange("s j c -> (s j) c"))
                    for g in range(3):
                        with B.bank() as pb:
                            for q in range(3):
                                j = 3 * g + q
                                B.tr(pb[:, q * 48:(q + 1) * 48], OCV[0:48, j * 128:(j + 1) * 128], ident[0:48, 0:48])
                            dst = PB[:, 3 * g:3 * g + 3, 0:blk.w].rearrange("p j (s u) -> p j s u", u=7)[:, :, :, 0:3]
                            evac(dst, pb[:, 0:144].rearrange("p (j s u) -> p j s u", j=3, u=3))
                if blk.kind == 's' or is_last:
                    nr = 3 * nsq
                    for g in range(3):
                        with B.bank() as pb:
                            for q in range(3):
                                j = 3 * g + q
                                if blk.kind == 'p':
                                    src = PB[:, j, n:n + 3]
                                else:
                                    src = TL[:, 48 * (j % 2):48 * (j % 2) + 48]
                                    B.cp(src.rearrange("p (s u) -> p s u", u=3),
                                         PB[:, j, 0:blk.w].rearrange("p (s u) -> p s u", u=7)[:, :, 4:7], eng='pool')
                                B.tr(pb[0:nr, q * 128:(q + 1) * 128], src, ident)
                            evac(OCV[0:nr, 384 * g:384 * g + 384], pb[0:nr, 0:384])
                    dst = dr['o_sconv'][l] if blk.kind == 's' else dr['o_pconv'][l]
                    odma(dst.rearrange("s j c -> (s j) c"), OCV[0:nr, :])
                (QKV0, QKV1, QKV2, CA0, CA1, CA2, SQ0, SQ1) = SL3[0:8]
                cw = PV('convw', l).rearrange("p (j c) -> p j c", j=4)
                ACC = [QKV0, QKV1, QKV2]
                TMPc = [CA0, CA1, CA2]
                for g in range(3):
                    acc4 = c4(ACC[g][:, :, 0:n], blk)
                    tmp4 = c4(TMPc[g][:, :, 0:n], blk)
                    for j in range(4):
                        src = PBv(blk, 3 * g, 3 * g + 3, j)
                        wj = bc(cw[:, j, 3 * g:3 * g + 3].unsqueeze(2).unsqueeze(3), [128, 3, nsq, T])
                        if j == 0:
                            B.tt(acc4, src, wj, ALU.mult, eng=('pool' if g == 1 else 'dve'))
                        else:
                            B.tt(tmp4, src, wj, ALU.mult, eng='pool')
                            B.tt(acc4, acc4, tmp4, ALU.add)
                    silu_exp(ACC[g][:, :, 0:n], ACC[g][:, :, 0:n], TMPc[g][:, :, 0:n])
                Qg, Kg, Vg = QKV0, QKV1, QKV2
                for arr, sc_, bi_ in ((Qg, 64.0, 64e-6), (Kg, 1.0, 1e-6)):
                    B.act(SQ0[:, :, 0:n], arr[:, :, 0:n], AF.Square)
                    for c in range(3):
                        with B.bank() as pb:
                            B.mm(pb[:, 0:n], bones, SQ0[:, c, 0:n])
                            rsqrt_ln(SQ1[:, c, 0:n], pb[:, 0:n], scale=sc_, bias=bi_)
                    B.tt(arr[:, :, 0:n], arr[:, :, 0:n], SQ1[:, :, 0:n], ALU.mult)
                B.act(BE6[:, 0:n], BETA[0:6, 0:n], AF.Exp, scale=-1.0)
                sig_finish(BE6[:, 0:n])
                B.act(G6[:, 0:n], APRE[0:6, 0:n], AF.Exp, bias=PV('dtb', l, rows=6))
                B.act(G6[:, 0:n], G6[:, 0:n], AF.Ln, bias=1.0)
                B.act(EA6[:], PV('alog', l, rows=6), AF.Exp)
                B.ts(G6[:, 0:n], G6[:, 0:n], EA6[:, 0:1], ALU.mult, -1.0, ALU.mult)
                B.scan(GC6[:, 0:n], C('rm_' + v, rows=6, c1=n), G6[:, 0:n])
                silu_exp(ZB[:, :, 0:n], ZB[:, :, 0:n], CA0[:, :, 0:n])
                OR = SL3[8]
                stage('gdnprep_%s%d_%d' % (blk.kind, blk.idx, l))
                def gdn_pair(hp):
                    nt = blk.ntile
                    Mst = Ms if blk.kind == 's' else Mp[('b', hp)]
                    if blk.kind == 's':
                        state_io_in('b', hp, dr['sgdn'][l])
                    css = [slice(64 * ti, 64 * ti + 64) for ti in range(nt)]
                    SCP = TB.SCP[:, 0:nt, :]
                    SCD = TB.SCD[:, 0:nt, :]
                    with B.bank() as pb:
                        pv = pb[:, 0:nt * 4].rearrange("p (n q) -> p n q", n=nt)
                        for ti in range(nt):
                            for h2 in range(2):
                                hs = hsl[h2]
                                hh = 2 * hp + h2
                                for q, src in enumerate((BE6, GC6, G6)):
                                    B.mm(pv[hs, ti, q:q + 1], src[0:6, css[ti]], ident[0:6, hh:hh + 1])
                        evac(SCP[:, :, 0:3], pv[:, :, 0:3])
                    with B.bank() as pb:
                        pv = pb[:, 0:nt * 4].rearrange("p (n q) -> p n q", n=nt)
                        for ti in range(nt):
                            for h2 in range(2):
                                hs = hsl[h2]
                                B.mm(pv[hs, ti, 0:1], C('BS_' + v)[hs, :], TB.SCP[hs, ti, 2:3])
                        B.tt(SCD[:, :, 2:3], pv[:, :, 0:1], SCP[:, :, 1:2], ALU.subtract)
                    B.act(SCD[:, :, 2:3], SCD[:, :, 2:3], AF.Exp)
                    B.act(SCD[:, :, 3:4], SCP[:, :, 1:2], AF.Exp)
                    B.stt(SCD[:, :, 0:1], SCP[:, :, 0:1], -1.0, SCD[:, :, 3:4], ALU.mult, ALU.mult)
                    B.ts(SCD[:, :, 1:2], SCP[:, :, 0:1], -1.0, ALU.mult)
                    GSL = TB.GSL[:, 0:nt, :]
                    GUI = TB.GUI[:, 0:nt, :]
                    NMv = TB.NM[:, 0:nt, :]

                    def b64(ap):
                        return bc(ap, [128, nt, 64])
                    with B.bank() as pb:
                        p3 = pb[:, 0:nt * 192].rearrange("p (n a t) -> p n a t", n=nt, a=3)
                        for ti in range(nt):
                            for h2 in range(2):
                                hs = hsl[h2]
                                hh = 2 * hp + h2
                                B.mm(p3[hs, ti, 0, :], C('osel', rows=6, c0=hh * 64, c1=hh * 64 + 64), GC6[0:6, css[ti]])
                                B.mm(p3[hs, ti, 1, :], Kg[hs, hp, css[ti]], Kg[hs, hp, css[ti]])
                                B.mm(p3[hs, ti, 2, :], Kg[hs, hp, css[ti]], Qg[hs, hp, css[ti]])
                        B.tt(GSL, p3[:, :, 0, :], b64(SCP[:, :, 1:2]), ALU.subtract)
                        B.tt(GUI, GSL, bc(C('BIU_' + v).unsqueeze(1), [128, nt, 64]), ALU.subtract)
                        B.tt(GSL, GSL, bc(C('BSL_' + v).unsqueeze(1), [128, nt, 64]), ALU.add)
                        B.act(GSL, GSL, AF.Exp, scale=-1.0)
                        B.act(GUI, GUI, AF.Exp)
                        B.tt(NMv, p3[:, :, 1, :], b64(SCD[:, :, 1:2]), ALU.mult)
                        B.tt(NMv, NMv, GSL, ALU.mult)
                        B.tt(GUI, p3[:, :, 2, :], GUI, ALU.mult)
                    with B.bank() as pb:
                        pv = pb[:, 0:nt * 64].rearrange("p (n t) -> p n t", n=nt)
                        for ti in range(nt):
                            for h2 in range(2):
                                hs = hsl[h2]
                                B.mm(pv[hs, ti, :], TB.NM[hs, ti, :], ident[hs, hs])
                        evac(TB.Qs[:, 0:nt], pv)
                    wy_double(blk, NMv, TB.Qs[:, 0:nt, :])
                    with B.bank() as pb:
                        pv = pb[:, 0:nt * 128].rearrange("p (n q c) -> p n q c", n=nt, q=2)
                        for ti in range(nt):
                            for h2 in range(2):
                                hs = hsl[h2]
                                B.mm(pv[hs, ti, 0, :], Kg[hs, hp, css[ti]], ident[hs, hs])
                                B.mm(pv[hs, ti, 1, :], Vg[hs, hp, css[ti]], ident[hs, hs])
                        evac(TB.TM[:, 0:nt, 0:2, :], pv)
                    B.tt(TB.KB3[:, 0:nt, 0, :], TB.TM[:, 0:nt, 0, :], b64(SCD[:, :, 0:1]), ALU.mult)
                    B.tt(TB.KB3[:, 0:nt, 1, :], TB.TM[:, 0:nt, 1, :], b64(SCP[:, :, 0:1]), ALU.mult)
                    B.tt(TB.KB3[:, 0:nt, 2, :], TB.TM[:, 0:nt, 0, :], b64(SCD[:, :, 2:3]), ALU.mult)
                    with B.bank() as pb:
                        pv = pb[:, 0:nt * 64].rearrange("p (n t) -> p n t", n=nt)
                        for ti in range(nt):
                            for h2 in range(2):
                                hs = hsl[h2]
                                B.mm(pv[hs, ti, :], TB.KB3[hs, ti, 0, :], TB.Pm[hs, ti, :])
                        evac(TB.WTs[:, 0:nt], pv)
                    with B.bank() as pb:
                        pv = pb[:, 0:nt * 64].rearrange("p (n t) -> p n t", n=nt)
                        for ti in range(nt):
                            for h2 in range(2):
                                hs = hsl[h2]
                                if blk.kind == 'p':
                                    B.mm(pv[hs, ti, :], TB.Pm[hs, ti, :], TB.KB3[hs, ti, 1, :])
                                else:
                                    B.mm(pv[hs, ti, :], TB.KB3[hs, ti, 1, :], TB.Pm[hs, ti, :])
                        evac(TB.U0s[:, 0:nt], pv)
                    with B.bank() as pb:
                        pv = pb[:, 0:nt * 64].rearrange("p (n t) -> p n t", n=nt)
                        for ti in range(nt):
                            B.mm(pv[:, ti, :], C('selb', rows=6, c0=hp * 128, c1=hp * 128 + 128), GC6[0:6, css[ti]])
                        B.act(TB.EBC[:, 0:nt], pv, AF.Exp)
                    B.tt(TB.QTg[:, 0:nt], Qg[:, hp, 0:nt * 64].rearrange("p (n t) -> p n t", n=nt), TB.EBC[:, 0:nt], ALU.mult)
                    for ti in range(nt):
                        cs = css[ti]
                        seq_step(blk, Mst, TB.WTs[:, ti, :], TB.U0s[:, ti, :], TB.U0Ts[:, ti, :],
                                 ArbT=TB.GUI[:, ti, :],
                                 y_extra=None,
                                 rt_fn=lambda cc, ti=ti: TB.QTg[:, ti, cc],
                                 b_tm=TB.KB3[:, ti, 2, :],
                                 upd_extra=None,
                                 DC_fn=lambda ti=ti: TB.EBC[:, ti, :].rearrange("p (s u) -> p s u", u=blk.seglen)[:, :, blk.seglen - 1],
                                 mode='gdn', ydst=OR[:, hp, cs])
                    if blk.kind == 's':
                        state_io_out(hp, Ms, NSEQ, dr['o_sgdn'][l])
                    elif is_last:
                        state_io_out(hp, Mst, 1, dr['o_pgdn'][l])

                cc0 = A_COLS + B_COLS
                for j in range(8):
                    proj_to(blk, (cc0 + 128 * j, 128), PB[:, j, 0:n])
                run_pairs(blk, gdn_pair, 3)
                stage('gdnchain_%s%d_%d' % (blk.kind, blk.idx, l))
                B.act(SQ0[:, :, 0:n], OR[:, :, 0:n], AF.Square)
                for c in range(3):
                    with B.bank() as pb:
                        B.mm(pb[:, 0:n], bones, SQ0[:, c, 0:n])
                        rsqrt_ln(SQ1[:, c, 0:n], pb[:, 0:n], scale=1.0 / 64, bias=1e-6)
                B.tt(OR[:, :, 0:n], OR[:, :, 0:n], SQ1[:, :, 0:n], ALU.mult)
                B.ts(OR[:, :, 0:n], OR[:, :, 0:n], PV('gnw', l), ALU.mult, eng='pool')
                if (l, blk.kind, blk.idx) == DBG_AT:
                    dbg('ORa', OR[:].rearrange("p c t -> p (c t)"))
                    dbg('ZBa', ZB[:].rearrange("p c t -> p (c t)"))
                B.tt(yT[:, 3:6, 0:n], OR[:, :, 0:n], ZB[:, :, 0:n], ALU.mult)

                stage('gdn_%s%d_%d' % (blk.kind, blk.idx, l))
                (SF, LF, KH_, QS, BC_, QTL, KTL, QH) = [t[:, 0:2, 0:n] for t in SL3[0:8]]
                (KHAT, OH, HT1) = [t[:, 0:2, 0:n] for t in SL3[8:11]]
                lb2 = bc(LBt[:, l, :].unsqueeze(2), [128, 2, n])
                oml2 = bc(OMLt[:, l, :].unsqueeze(2), [128, 2, n])
                B.act(SF, PB[:, 2:4, 0:n], AF.Exp, scale=-1.0)
                sig_finish(SF)
                B.tt(LF, SF, oml2, ALU.mult, eng='pool')
                B.tt(LF, LF, lb2, ALU.add)
                B.act(LF, LF, AF.Ln)
                B.ts(KH_, SF, -1.0, ALU.mult, 1.0, ALU.add, eng='pool')
                B.tt(KH_, KH_, oml2, ALU.mult)
                silu_exp(QS, PB[:, 0:2, 0:n], QS)
                for c in range(2):
                    B.scan(BC_[:, c, :], C('rm_' + v, c1=n), LF[:, c, :])
                sl = blk.seglen
                nsg_all = n // sl
                mid = sl // 2 - 1

                def sv(ap):
                    return ap.rearrange("p c (s u) -> p c s u", u=sl)
                B.tt(sv(HT1), sv(BC_), bc(sv(BC_)[:, :, :, mid:mid + 1], [128, 2, nsg_all, sl]), ALU.subtract, eng='pool')
                B.act(QTL, HT1, AF.Exp)
                B.tt(QTL, QTL, QS, ALU.mult)
                B.act(KTL, HT1, AF.Exp, scale=-1.0)
                B.tt(KTL, KTL, KH_, ALU.mult, eng='pool')
                EB = SL3[11][:, 0:2, 0:n]
                B.act(EB, BC_, AF.Exp)
                B.tt(QH, QS, EB, ALU.mult)
                B.tt(sv(HT1), bc(sv(BC_)[:, :, :, sl - 1:sl], [128, 2, nsg_all, sl]), sv(BC_), ALU.subtract, eng='pool')
                B.act(HT1, HT1, AF.Exp)
                B.tt(KHAT, KH_, HT1, ALU.mult)
                def hgrn_pair(hp):
                    nt = blk.ntile
                    if blk.kind == 's':
                        state_io_in('c', hp, dr['shg'][l])
                        Mst = Ms
                    else:
                        Mst = Mp[('c', hp)]
                    css = [slice(64 * ti, 64 * ti + 64) for ti in range(nt)]
                    with B.bank() as pb:
                        pv = pb[:, 0:nt * 128].rearrange("p (n q c) -> p n q c", n=nt, q=2)
                        for ti in range(nt):
                            for h2 in range(2):
                                hs = hsl[h2]
                                B.mm(pv[hs, ti, 0, :], KHAT[hs, hp, css[ti]], ident[hs, hs])
                                B.mm(pv[hs, ti, 1, :], PB[hs, 4 + hp, css[ti]], ident[hs, hs])
                        evac(TB.TM[:, 0:nt, 0:2, :], pv)
                    with B.bank() as pb:
                        pv = pb[:, 0:nt * 64].rearrange("p (n t) -> p n t", n=nt)
                        for ti in range(nt):
                            for h2 in range(2):
                                hs = hsl[h2]
                                B.mm(pv[hs, ti, :], KTL[hs, hp, css[ti]], QTL[hs, hp, css[ti]])
                        B.ts(TB.GUI[:, 0:nt], pv, 3.0e38, ALU.min, -3.0e38, ALU.max)
                        B.tt(TB.GUI[:, 0:nt], TB.GUI[:, 0:nt], bc(C('IU_' + v).unsqueeze(1), [128, nt, 64]), ALU.mult)
                    if blk.kind == 's':
                        vmask_build(TB.TM[:, 0, 1, :])
                    for ti in range(nt):
                        cs = css[ti]
                        seq_step(blk, Mst, None, None, None,
                                 ArbT=None,
                                 y_extra=(TB.TM[:, ti, 1, :], TB.GUI[:, ti, :]),
                                 rt_fn=lambda cc, hp=hp, cs=cs: QH[:, hp, cs][:, cc],
                                 b_tm=None,
                                 upd_extra=(TB.TM[:, ti, 0, :], TB.TM[:, ti, 1, :]),
                                 DC_fn=lambda hp=hp, cs=cs: segend(EB[:, hp, :], blk, cs),
                                 mode='gdn', ydst=OH[:, hp, cs], Vmask=VM)
                    if blk.kind == 's':
                        state_io_out(hp, Ms, NSEQ, dr['o_shg'][l])
                    elif is_last:
                        state_io_out(hp, Mst, 1, dr['o_phg'][l])

                stage('hgrnprep_%s%d_%d' % (blk.kind, blk.idx, l))
                run_pairs(blk, hgrn_pair, 2)
                stage('hgrnchain_%s%d_%d' % (blk.kind, blk.idx, l))
                SQh = SL3[0][:, 0:2, 0:n]
                SQg = SL3[1][:, 0:2, 0:n]
                B.act(SQh, OH, AF.Square)
                for c in range(2):
                    with B.bank() as pb:
                        B.mm(pb[:, 0:n], bones, SQh[:, c, :])
                        rsqrt_ln(SQg[:, c, :], pb[:, 0:n], scale=1.0 / 64, bias=1e-6)
                B.tt(OH, OH, SQg, ALU.mult)
                B.ts(OH, OH, PV('hnw', l), ALU.mult, eng='pool')
                B.act(SQh, PB[:, 6:8, 0:n], AF.Exp, scale=-1.0)
                sig_finish(SQh)
                if (l, blk.kind, blk.idx) == DBG_AT:
                    dbg('OHa', SL3[9][:].rearrange("p c t -> p (c t)"))
                    dbg('xba', xb[:].rearrange("p c t -> p (c t)"))
                B.tt(yT[:, 6:8, 0:n], OH, SQh, ALU.mult)

                if blk.kind == 's' or is_last:
                    HL = TL[:, :]
                    for kc in range(KC):
                        if blk.kind == 'p':
                            srcl = hT_p[:, kc, 3 + n - 1:3 + n]
                        else:
                            srcl = hT_s[:, kc, :].rearrange("p (s u) -> p s u", u=5)[:, :, 4]
                        B.cp(HL[:, kc * 16:kc * 16 + nsq], srcl, eng='pool')
                    for half in range(2):
                        with B.bank() as pb:
                            for q in range(4):
                                kc = 4 * half + q
                                B.tr(pb[0:nsq, q * 128:(q + 1) * 128], HL[:, kc * 16:kc * 16 + nsq], ident)
                            evac(OSH[0:nsq, 512 * half:512 * half + 512], pb[0:nsq, 0:512])
                    dst = dr['o_sshift'][l] if blk.kind == 's' else dr['o_pshift'][l]
                    odma(dst, OSH[0:nsq, :])
                for m in range(KC):
                    with B.bank() as pb:
                        for kc in range(KC):
                            B.mm(pb[:, 0:n], w_out_sb[:, kc, m * 128:(m + 1) * 128], yT[:, kc, 0:n],
                                 start=(kc == 0), stop=(kc == KC - 1))
                        gate_res(blk, xb[:, m, 0:n], pb[:, 0:n], l, 'GM', m, n, tmpn)
                B.dma(xscr[:, :, blk.tok0:blk.tok0 + n], xb[:, :, 0:n], wr=xkey(0, l, blk.tok0, n))
                stage('blk_%s%d_%d' % (blk.kind, blk.idx, l))

            for bi, blk in enumerate(prompt_blocks):
                process_block(blk, bi == len(prompt_blocks) - 1)
                B.cp(hT_p[:, :, 0:3], hT_p[:, :, BLK:BLK + 3], eng='pool')
            process_block(sample_block, True)
            B.release(mph)
            stage('mix_%d' % l)

            fph = B.mark()
            JG = 6
            NG = (NJ + JG - 1) // JG
            WFI = [B.sb("w_fi%d" % g, [128, KC, 2, (min(NJ, g * JG + JG) - g * JG) * 128], BF16) for g in range(NG)]
            WFO = [B.sb("w_fo%d" % g, [128, min(NJ, g * JG + JG) - g * JG, D], BF16) for g in range(NG)]
            wsrc = dr['w_fi'][l].rearrange("(kc p) n -> p kc n", p=128)
            for g in range(NG):
                j0, j1 = g * JG, min(NJ, g * JG + JG)
                w = (j1 - j0) * 128
                for part in range(2):
                    B.dma(WFI[g][:, :, part, 0:w], wsrc[:, :, part * FF + j0 * 128:part * FF + j0 * 128 + w],
                          eng='pool', cls='w')
            wsrc = dr['w_fo'][l].rearrange("(j p) n -> p j n", p=128)
            for g in range(NG):
                j0, j1 = g * JG, min(NJ, g * JG + JG)
                B.dma(WFO[g][:, 0:j1 - j0, :], wsrc[:, j0:j1, :], eng='pool', cls='w')
            xb2 = B.sb("xb2", [128, KC, FBLK])
            h2T = B.sb("h2T", [128, KC, FBLK], BF16)
            actT = B.sb("actT", [128, NJ, FBLK], BF16)
            sq2 = [B.sb("sq2_%d" % i, [128, FBLK]) for i in range(2)]
            rstd2 = B.sb("rstd2", [128, FBLK])
            tmp2 = [B.sb("tmp2_%d" % i, [128, FBLK]) for i in range(2)]
            sgt = [B.sb("sgt%d" % i, [128, FBLK]) for i in range(2)]
            last_layer = (l == DEPTH - 1)
            if last_layer:
                yfin = B.sb("yfin", [128, KC, FBLK])
                ytm = B.sb("ytm", [128, D])
            for fb in ffn_blocks:
                n = fb.n
                B.dma(xb2[:, :, 0:n], xscr[:, :, fb.tok0:fb.tok0 + n], rd=xkey(1, l, fb.tok0, n))
                hdst = lambda kc, fb=fb, n=n: v3(h2T[:, kc, 0:n], fb)
                norm_mod(fb, xb2, n, l, 'AF', 'BF', hdst, sq2, rstd2, tmp2)
                for j in range(NJ):
                    with B.bank() as pg, B.bank() as pu:
                        for kc in range(KC):
                            B.mm(pg[:, 0:n], WFI[j // JG][:, kc, 0, (j % JG) * 128:(j % JG + 1) * 128], h2T[:, kc, 0:n],
                                 start=(kc == 0), stop=(kc == KC - 1))
                        for kc in range(KC):
                            B.mm(pu[:, 0:n], WFI[j // JG][:, kc, 1, (j % JG) * 128:(j % JG + 1) * 128], h2T[:, kc, 0:n],
                                 start=(kc == 0), stop=(kc == KC - 1))
                        B.act(sgt[j % 2][:, 0:n], pg[:, 0:n], AF.Silu)
                        B.tt(actT[:, j, 0:n], sgt[j % 2][:, 0:n], pu[:, 0:n], ALU.mult)
                for m in range(KC):
                    with B.bank() as pb:
                        for j in range(NJ):
                            B.mm(pb[:, 0:n], WFO[j // JG][:, j % JG, m * 128:(m + 1) * 128], actT[:, j, 0:n],
                                 start=(j == 0), stop=(j == NJ - 1))
                        gate_res(fb, xb2[:, m, 0:n], pb[:, 0:n], l, 'GF', m, n, tmp2)
                if not last_layer:
                    B.dma(xscr[:, :, fb.tok0:fb.tok0 + n], xb2[:, :, 0:n], wr=xkey(1, l, fb.tok0, n))
                else:
                    rms_stats(xb2, n, sq2, rstd2)
                    for kc in range(KC):
                        B.stt(yfin[:, kc, 0:n], xb2[:, kc, 0:n], PV('fnw', 0, c0=kc, c1=kc + 1), rstd2[:, 0:n],
                              ALU.mult, ALU.mult)
                    nt = (n + 127) // 128
                    for ti in range(nt):
                        r = min(128, n - 128 * ti)
                        for g in range(2):
                            with B.bank() as pb:
                                for q in range(4):
                                    kc = 4 * g + q
                                    B.tr(pb[0:r, q * 128:(q + 1) * 128], yfin[:, kc, 128 * ti:128 * ti + r], ident)
                                evac(ytm[0:r, 512 * g:512 * g + 512], pb[0:r, 0:512])
                        if fb.kind == 'p':
                            odma(dr['yp'][fb.tok0 + 128 * ti:fb.tok0 + 128 * ti + r, :], ytm[0:r, :])
                        else:
                            odma(dr['ys'], ytm[0:r, :])
            B.release(fph)
            stage('ffn_%d' % l)

        S.disabled = False
        S.barrier()
        with nc.Block() as block:
            S.emit(block)
    build_program.dbg_list = dbg_list
    build_program.n_ops = S.n_ops
    build_program.sbuf_top = B.top
    return nc


INPUT_ORDER = None


def make_in_maps(inp):
    pv = pack_pvec(inp)
    cst = make_consts()
    f = lambda a: np.ascontiguousarray(np.asarray(a, np.float32))
    shared = {
        'w_ada': f(inp['w_ada']), 'w_in': f(inp['w_in']), 'w_out': f(inp['w_out']),
        'w_fi': f(inp['w_ffn_in']), 'w_fo': f(inp['w_ffn_out']), 'pvec': pv, 'cst': cst,
    }
    maps = []
    for c in range(NCORES):
        s0, s1 = NSEQ * c, NSEQ * (c + 1)
        m = dict(shared)
        m['xp'] = f(inp['x_prompt'][c])
        m['xs'] = f(np.asarray(inp['x_sample'][s0:s1]).reshape(TOKS, D))
        m['c17'] = f(np.concatenate([np.asarray(inp['c_prompt'][c:c + 1]), np.asarray(inp['c_sample'][s0:s1])], axis=0))
        m['shift_s'] = f(np.asarray(inp['state_rwkv_shift'])[:, s0:s1])
        m['srw'] = f(np.asarray(inp['state_rwkv'])[:, s0:s1])
        m['sconv'] = f(np.asarray(inp['state_gdn_conv'])[:, s0:s1])
        m['sgdn'] = f(np.asarray(inp['state_gdn'])[:, s0:s1])
        m['shg'] = f(np.asarray(inp['state_hgrn'])[:, s0:s1])
        maps.append(m)
    return maps


_NC_CACHE = {}


def kernel(**inputs):
    inp = {k: np.asarray(v) for k, v in inputs.items()}
    maps = make_in_maps(inp)
    if 'nc' not in _NC_CACHE:
        _NC_CACHE['nc'] = build_program()
    nc = _NC_CACHE['nc']
    res = run_bass_kernel_spmd(nc, maps, core_ids=list(range(NCORES)))
    R = res.results

    def cat(name, axis):
        return np.concatenate([np.asarray(R[c][name], np.float32) for c in range(NCORES)], axis=axis)

    y_prompt = np.stack([np.asarray(R[c]['yp'], np.float32) for c in range(NCORES)], axis=0)
    y_sample = cat('ys', 0).reshape(NCORES * NSEQ, TSS, D)
    p_shift = cat('o_pshift', 1)
    p_rwkv = cat('o_prw', 1)
    p_conv = cat('o_pconv', 1)
    p_gdn = cat('o_pgdn', 1)
    p_hgrn = cat('o_phg', 1)
    s_shift = cat('o_sshift', 1)
    s_rwkv = cat('o_srw', 1)
    s_conv = cat('o_sconv', 1)
    s_gdn = cat('o_sgdn', 1)
    s_hgrn = cat('o_shg', 1)
    return (y_prompt, y_sample, p_shift, p_rwkv, p_conv, p_gdn, p_hgrn,
            s_shift, s_rwkv, s_conv, s_gdn, s_hgrn)
```

```python
import contextlib
import numpy as np
import concourse.bass as bass
import concourse.mybir as mybir
from concourse.bass_utils import run_bass_kernel_spmd

F32 = mybir.dt.float32
BF16 = mybir.dt.bfloat16
ALU = mybir.AluOpType
AF = mybir.ActivationFunctionType

NCORES = 8
D = 1024
KC = 8
TOKP = 2048
NSEQ = 16
TSS = 4
TOKS = 64
DEPTH = 2
A_COLS = 1280
B_COLS = 1548
C_COLS = 1024
IN_COLS = 3852
FF = 2816
NJ = 22
BLK = 128
FBLK = 256
C0 = 0.6065306597126334
BIG = 1.0e4
SEM_LIMIT = 12000
SB_LO = 16512
SB_HI = 229376

DBG_AT = None
TRACE_LINES = False
FUSE_WAIT = True
DMA_SLOTS = 8
CHAIN_BANKS = {0: [0, 1], 1: [2, 3], 2: [4, 5]}
OP_LINES = {}


class Sched:
    ENGS = ('pe', 'act', 'dve', 'pool', 'sp')

    def __init__(self, nc, stack):
        self.nc = nc
        self.stack = stack
        self.prog = {e: [] for e in self.ENGS}
        self.stream_sem = {}
        self.stream_cnt = {}
        self.last_w = {}
        self.reads = {}
        self.seen = {e: {} for e in self.ENGS}
        self.snap = {}
        self.dma_rr = {}
        self.nsem = 0
        self.n_ops = 0
        self.disabled = False
        self.max_ops = None
        self.trace = None

    def _new_sem(self, stream):
        s = self.stack.enter_context(self.nc.semaphore("s%d" % self.nsem))
        self.nsem += 1
        self.stream_sem.setdefault(stream, []).append(s)
        self.stream_cnt[stream] = 0

    def _event(self, stream, inc):
        if stream not in self.stream_sem or self.stream_cnt[stream] + inc > SEM_LIMIT:
            self._new_sem(stream)
        self.stream_cnt[stream] += inc
        return (stream, len(self.stream_sem[stream]) - 1, self.stream_cnt[stream])

    @staticmethod
    def _key(a):
        return a if isinstance(a, str) else a.name

    def op(self, eng, fn, reads=(), writes=(), dma_class=None):
        if self.disabled or (self.max_ops is not None and self.n_ops >= self.max_ops):
            return None
        if dma_class is not None:
            k = self.dma_rr.get((eng, dma_class), 0)
            self.dma_rr[(eng, dma_class)] = k + 1
            stream = (eng, dma_class, k % DMA_SLOTS)
            inc = 16
        else:
            stream = eng
            inc = 1
        rk = [self._key(r) for r in reads]
        wk = [self._key(w) for w in writes]
        deps = []
        if dma_class is not None and stream in self.stream_sem and self.stream_cnt[stream] > 0:
            deps.append(((stream, len(self.stream_sem[stream]) - 1, self.stream_cnt[stream]), 'slot'))
        for k in rk:
            if k in self.last_w:
                deps.append((self.last_w[k], 'raw'))
        for k in wk:
            if k in self.last_w:
                deps.append((self.last_w[k], 'waw'))
            for ev in self.reads.get(k, ()):
                deps.append((ev, 'war'))
        waits = []
        seen = self.seen[eng]
        for ev, kind in deps:
            st, ep, val = ev
            if st == eng and dma_class is None:
                if eng == 'pe' or kind == 'war':
                    continue
            sk = (st, ep)
            if seen.get(sk, 0) >= val:
                continue
            seen[sk] = val
            waits.append((self.stream_sem[st][ep], val))
            snap = self.snap.get(ev)
            if snap:
                for k2, v2 in snap.items():
                    if seen.get(k2, 0) < v2:
                        seen[k2] = v2
        ev = self._event(stream, inc)
        sem = self.stream_sem[stream][ev[1]]
        self.snap[ev] = dict(seen)
        if TRACE_LINES:
            OP_LINES.setdefault(eng, []).append(getattr(fn, '_ln', None))
        if self.trace is not None:
            import sys as _sys
            self.trace.append((self.n_ops, eng, _sys._getframe(2).f_lineno))
        self.prog[eng].append((waits, fn, sem, inc))
        for k in wk:
            self.last_w[k] = ev
            self.reads[k] = []
        for k in rk:
            if k not in wk:
                self.reads.setdefault(k, []).append(ev)
        self.n_ops += 1
        return ev

    def barrier(self):
        evs = []
        for stream, sems in self.stream_sem.items():
            ep = len(sems) - 1
            if self.stream_cnt[stream] > 0:
                evs.append((stream, ep, self.stream_cnt[stream]))
        for e in self.ENGS:
            waits = []
            for st, ep, val in evs:
                if st == e:
                    continue
                sk = (st, ep)
                if self.seen[e].get(sk, 0) >= val:
                    continue
                self.seen[e][sk] = val
                waits.append((self.stream_sem[st][ep], val))
            if waits:
                self.prog[e].append((waits, None, None, 0))

    def emit(self, block):
        prog = self.prog

        def run(e, engobj):
            for waits, fn, sem, inc in prog[e]:
                if fn is None:
                    for s, v in waits:
                        engobj.wait_ge(s, v)
                    continue
                fuse = FUSE_WAIT and inc == 1 and len(waits) > 0
                for s, v in (waits[:-1] if fuse else waits):
                    engobj.wait_ge(s, v)
                ins = fn(engobj)
                if fuse:
                    ins._wait_ge(waits[-1][0], waits[-1][1])
                ins.then_inc(sem, inc)

        @block.tensor
        def _(eng):
            run('pe', eng)

        @block.scalar
        def _(eng):
            run('act', eng)

        @block.vector
        def _(eng):
            run('dve', eng)

        @block.gpsimd
        def _(eng):
            run('pool', eng)

        @block.sync
        def _(eng):
            run('sp', eng)


def _dtsize(dt):
    return 2 if dt == BF16 else 4


class KB:
    def __init__(self, nc, stack):
        self.nc = nc
        self.S = Sched(nc, stack)
        self.top = SB_LO
        self.uid = 0
        self.banks = [stack.enter_context(nc.psum_tensor("pb%d" % i, [128, 512], F32)) for i in range(8)]
        self.free_banks = list(range(8))
        self.chain = None
        self.chains = {}
        self.chain_bank_cnt = {}

    def emit(self, eng, fn, reads=(), writes=(), dma_class=None):
        if TRACE_LINES:
            import sys as _sys
            fn._ln = _sys._getframe(2).f_lineno
        if self.chain is None:
            self.S.op(eng, fn, reads=reads, writes=writes, dma_class=dma_class)
        else:
            self.chains[self.chain].append((eng, fn, list(reads), list(writes), dma_class))

    @contextlib.contextmanager
    def in_chain(self, k):
        assert self.chain is None
        self.chain = k
        self.chains.setdefault(k, [])
        try:
            yield
        finally:
            self.chain = None

    def flush_chains(self):
        lists = [self.chains[k] for k in sorted(self.chains)]
        self.chains = {}
        pos = [0] * len(lists)
        remaining = sum(len(x) for x in lists)
        while remaining:
            for i, lst in enumerate(lists):
                if pos[i] < len(lst):
                    eng, fn, reads, writes, dc = lst[pos[i]]
                    pos[i] += 1
                    remaining -= 1
                    self.S.op(eng, fn, reads=reads, writes=writes, dma_class=dc)

    def sb(self, name, shape, dt=F32):
        size = int(np.prod(shape[1:])) * _dtsize(dt)
        size = (size + 31) // 32 * 32
        off = self.top
        self.top += size
        assert self.top <= SB_HI, "SBUF overflow at %s: %d" % (name, self.top)
        self.uid += 1
        return self.nc.alloc_sbuf_tensor_at("%s_%d" % (name, self.uid), list(shape), dt, offset=off)

    def mark(self):
        return self.top

    def release(self, m):
        self.S.barrier()
        self.top = m

    @contextlib.contextmanager
    def bank(self):
        if self.chain is not None:
            c = self.chain_bank_cnt.get(self.chain, 0)
            self.chain_bank_cnt[self.chain] = c + 1
            bl = CHAIN_BANKS[self.chain]
            yield self.banks[bl[c % len(bl)]]
            return
        idx = self.free_banks.pop(0)
        try:
            yield self.banks[idx]
        finally:
            self.free_banks.append(idx)

    def mm(self, out, lhsT, rhs, start=True, stop=True):
        self.emit('pe', lambda e: e.matmul(out, lhsT=lhsT, rhs=rhs, start=start, stop=stop),
                  reads=[lhsT, rhs], writes=[out])

    def tr(self, out, in_, ident):
        self.emit('pe', lambda e: e.transpose(out, in_, ident), reads=[in_, ident], writes=[out])

    def act(self, out, in_, func, bias=None, scale=None):
        reads = [in_]
        kw = {}
        if bias is not None:
            kw['bias'] = bias
            if not isinstance(bias, (int, float)):
                reads.append(bias)
        if scale is not None:
            kw['scale'] = scale
            if not isinstance(scale, (int, float)):
                reads.append(scale)
        self.emit('act', lambda e: e.activation(out, in_, func, **kw), reads=reads, writes=[out])

    def cp(self, out, in_, eng='dve'):
        if eng == 'act':
            self.emit('act', lambda e: e.copy(out, in_), reads=[in_], writes=[out])
        else:
            self.emit(eng, lambda e: e.tensor_copy(out, in_), reads=[in_], writes=[out])

    def tt(self, out, a, b, op, eng='dve'):
        self.emit(eng, lambda e: e.tensor_tensor(out, a, b, op), reads=[a, b], writes=[out])

    def ts(self, out, a, s1, op0, s2=None, op1=None, eng='dve'):
        reads = [a] + [s for s in (s1, s2) if s is not None and not isinstance(s, (int, float))]
        if op1 is None:
            self.emit(eng, lambda e: e.tensor_scalar(out, a, s1, None, op0), reads=reads, writes=[out])
        else:
            self.emit(eng, lambda e: e.tensor_scalar(out, a, s1, s2, op0, op1), reads=reads, writes=[out])

    def stt(self, out, in0, scalar, in1, op0, op1):
        reads = [in0, in1] + ([] if isinstance(scalar, (int, float)) else [scalar])
        self.emit('dve', lambda e: e.scalar_tensor_tensor(out, in0, scalar, in1, op0, op1),
                  reads=reads, writes=[out])

    def scan(self, out, d0, d1):
        self.emit('dve', lambda e: e.tensor_tensor_scan(out, d0, d1, 0.0, ALU.mult, ALU.add),
                  reads=[d0, d1], writes=[out])

    def recip(self, out, in_):
        self.emit('dve', lambda e: e.reciprocal(out, in_), reads=[in_], writes=[out])

    def memset(self, out, val, eng='dve'):
        self.emit(eng, lambda e: e.memset(out, val), writes=[out])

    def dma(self, out, in_, eng='sp', cls='io', rd=None, wr=None):
        self.emit(eng, lambda e: e.dma_start(out=out, in_=in_),
                  reads=[in_] if rd is None else rd, writes=[out] if wr is None else wr, dma_class=cls)


def bc(ap, shape):
    return ap.to_broadcast(list(shape))


def _fm(v, nch):
    return np.ascontiguousarray(np.asarray(v, np.float32).reshape(nch, 128).T)


PV_LAYER = [('nmw', 8), ('nfw', 8), ('bada', 48), ('mu', 10), ('w0', 3), ('a0', 3), ('kk', 3), ('ka', 3),
            ('rk', 3), ('lnw', 3), ('lnb', 3), ('convw', 36), ('alog', 1), ('dtb', 1), ('gnw', 1),
            ('lbl', 2), ('hnw', 1), ('lora', 384)]
PV_OFF = {}
_o = 0
for _l in range(DEPTH):
    for _n, _w in PV_LAYER:
        PV_OFF[(_n, _l)] = (_o, _w)
        _o += _w
PV_OFF[('fnw', 0)] = (_o, 8)
_o += 8
NPV = _o


def pack_pvec(inp):
    pv = np.zeros((128, NPV), np.float32)

    def put(name, l, arr):
        o, w = PV_OFF[(name, l)]
        assert arr.shape == (128, w), (name, arr.shape, w)
        pv[:, o:o + w] = arr

    for l in range(DEPTH):
        put('nmw', l, _fm(inp['norm_mix_w'][l], 8))
        put('nfw', l, _fm(inp['norm_ffn_w'][l], 8))
        put('bada', l, _fm(inp['b_ada'][l], 48))
        put('mu', l, _fm(inp['rwkv_mu'][l], 10))
        put('w0', l, _fm(inp['rwkv_w0'][l], 3))
        put('a0', l, _fm(inp['rwkv_a0'][l], 3))
        put('kk', l, _fm(inp['rwkv_k_k'][l], 3))
        put('ka', l, _fm(inp['rwkv_k_a'][l], 3))
        put('rk', l, _fm(np.asarray(inp['rwkv_r_k'][l]).reshape(384), 3))
        put('lnw', l, _fm(inp['rwkv_ln_w'][l], 3))
        put('lnb', l, _fm(inp['rwkv_ln_b'][l], 3))
        cw = np.asarray(inp['gdn_conv_w'][l], np.float32).reshape(4, 9, 128)
        put('convw', l, np.ascontiguousarray(cw.transpose(2, 0, 1)).reshape(128, 36))
        t = np.zeros((128, 1), np.float32)
        t[0:6, 0] = inp['gdn_A_log'][l]
        put('alog', l, t)
        t = np.zeros((128, 1), np.float32)
        t[0:6, 0] = inp['gdn_dt_bias'][l]
        put('dtb', l, t)
        put('gnw', l, np.tile(np.asarray(inp['gdn_norm_w'][l], np.float32), 2).reshape(128, 1))
        put('lbl', l, _fm(inp['hgrn_lb_logits'][l], 2))
        put('hnw', l, np.tile(np.asarray(inp['hgrn_norm_w'][l], np.float32), 2).reshape(128, 1))
        lo = np.zeros((128, 384), np.float32)
        lo[0:32] = inp['rwkv_w2'][l]
        lo[32:64] = inp['rwkv_a2'][l]
        lo[64:128] = inp['rwkv_g2'][l]
        put('lora', l, lo)
    put('fnw', 0, _fm(inp['final_norm_w'], 8))
    return pv


CST_ITEMS = [('ident', 128), ('bones', 128), ('ones', 128), ('ident2', 64),
             ('mask5_p', 320), ('IU_p', 64), ('BSL_p', 64), ('BIU_p', 64), ('BS_p', 64),
             ('mask5_s', 320), ('IU_s', 64), ('BSL_s', 64), ('BIU_s', 64), ('BS_s', 64),
             ('ind', 16), ('osel', 6 * 64), ('selb', 3 * 128), ('rm_p', 384), ('rm_s', 192)]
CST_OFF = {}
_o = 0
for _n, _w in CST_ITEMS:
    CST_OFF[_n] = (_o, _w)
    _o += _w
NCST = _o


def make_consts():
    c = np.zeros((128, NCST), np.float32)

    def put(name, arr):
        o, w = CST_OFF[name]
        assert arr.shape[1] == w, (name, arr.shape, w)
        c[0:arr.shape[0], o:o + w] = arr

    def rep(a):
        return np.concatenate([a, a], axis=0)

    put('ident', np.eye(128, dtype=np.float32))
    p = np.arange(128)
    put('bones', (p[:, None] // 64 == p[None, :] // 64).astype(np.float32))
    put('ones', np.ones((128, 128), np.float32))
    put('ident2', rep(np.eye(64, dtype=np.float32)))
    i = np.arange(64)
    for v, seg in (('p', 64), ('s', 4)):
        same = (i[:, None] // seg == i[None, :] // seg)
        SU = ((i[:, None] < i[None, :]) & same).astype(np.float32)
        IU = ((i[:, None] <= i[None, :]) & same).astype(np.float32)
        SL = ((i[:, None] > i[None, :]) & same).astype(np.float32)
        put('mask5_' + v, rep(np.concatenate([SU, SU, IU, IU, SL], axis=1)))
        put('IU_' + v, rep(IU))
        put('BSL_' + v, rep(BIG * (1.0 - SL)))
        put('BIU_' + v, rep(BIG * (1.0 - IU)))
        put('BS_' + v, rep(same.astype(np.float32)))
    put('ind', rep((i[:, None] // 4 == np.arange(16)[None, :]).astype(np.float32)))
    osel = np.zeros((6, 6, 64), np.float32)
    for hh in range(6):
        osel[hh, hh, :] = 1.0
    put('osel', osel.reshape(6, 6 * 64))
    selb = np.zeros((6, 3, 128), np.float32)
    for hp in range(3):
        for m in range(128):
            selb[2 * hp + m // 64, hp, m] = 1.0
    put('selb', selb.reshape(6, 3 * 128))
    col = np.arange(384)
    put('rm_p', np.tile((col % 64 != 0).astype(np.float32)[None, :], (128, 1)))
    col = np.arange(192)
    put('rm_s', np.tile((col % 4 != 0).astype(np.float32)[None, :], (128, 1)))
    return c


class Blk:
    def __init__(self, kind, idx, n=None):
        self.kind = kind
        self.idx = idx
        if kind == 'p':
            self.n = BLK if n is None else n
            self.nsq = 1
            self.tb = self.n
            self.ntile = self.n // 64
            self.nseg = 1
            self.tok0 = idx * self.n
            self.w = 3 + self.n
            self.v = 'p'
            self.levels = 5
        else:
            self.n = TOKS
            self.nsq = NSEQ
            self.tb = TSS
            self.ntile = 1
            self.nseg = NSEQ
            self.tok0 = TOKP
            self.w = NSEQ * 7
            self.v = 's'
            self.levels = 1
        self.seglen = 64 // self.nseg


def build_program(dbg_names=(), stop_after=None):
    nc = bass.Bass("TRN2", target_bir_lowering=False)
    dr = {}

    def din(name, shape):
        dr[name] = nc.dram_tensor(name, list(shape), F32, kind="ExternalInput").ap()

    def dout(name, shape):
        dr[name] = nc.dram_tensor(name, list(shape), F32, kind="ExternalOutput").ap()

    din('xp', [TOKP, D]); din('xs', [TOKS, D]); din('c17', [17, D])
    din('shift_s', [DEPTH, NSEQ, D]); din('srw', [DEPTH, NSEQ, 6, 64, 64]); din('sconv', [DEPTH, NSEQ, 3, 1152])
    din('sgdn', [DEPTH, NSEQ, 6, 64, 64]); din('shg', [DEPTH, NSEQ, 4, 64, 64])
    din('w_ada', [DEPTH, D, 6 * D]); din('w_in', [DEPTH, D, IN_COLS]); din('w_out', [DEPTH, D, D])
    din('w_fi', [DEPTH, D, 2 * FF]); din('w_fo', [DEPTH, FF, D])
    din('pvec', [128, NPV]); din('cst', [128, NCST])
    dout('yp', [TOKP, D]); dout('ys', [TOKS, D])
    dout('o_pshift', [DEPTH, 1, D]); dout('o_prw', [DEPTH, 1, 6, 64, 64]); dout('o_pconv', [DEPTH, 1, 3, 1152])
    dout('o_pgdn', [DEPTH, 1, 6, 64, 64]); dout('o_phg', [DEPTH, 1, 4, 64, 64])
    dout('o_sshift', [DEPTH, NSEQ, D]); dout('o_srw', [DEPTH, NSEQ, 6, 64, 64]); dout('o_sconv', [DEPTH, NSEQ, 3, 1152])
    dout('o_sgdn', [DEPTH, NSEQ, 6, 64, 64]); dout('o_shg', [DEPTH, NSEQ, 4, 64, 64])
    xscr = nc.dram_tensor("xscr", [128, KC, TOKP + TOKS], F32, kind="Internal").ap()
    dbg_list = []
    dbg_t = {}
    for dn, dshape in dbg_names:
        dbg_t[dn] = nc.dram_tensor("dbg_" + dn, list(dshape), F32, kind="ExternalOutput").ap()
    out_keys = []

    with contextlib.ExitStack() as stack:
        B = KB(nc, stack)
        S = B.S
        okc = [0]

        def odma(dst, src):
            okc[0] += 1
            key = "out#%d" % okc[0]
            B.dma(dst, src, cls='out', wr=[key])

        if stop_after is not None and stop_after.startswith('#'):
            S.max_ops = int(stop_after[1:])

        def stage(name):
            if stop_after is not None and stop_after == name:
                S.disabled = True

        def dbg(name, ap, shape=None):
            if name in dbg_t and name not in dbg_list:
                dbg_list.append(name)
                odma(dbg_t[name], ap)

        cst = B.sb("cst", [128, NCST])
        pvec = B.sb("pvec", [128, NPV])
        modT = B.sb("modT", [128, DEPTH, 48, 17])
        AMt = B.sb("AMt", [128, DEPTH, 8, 17])
        AFt = B.sb("AFt", [128, DEPTH, 8, 17])
        LBt = B.sb("LBt", [128, DEPTH, 2])
        OMLt = B.sb("OMLt", [128, DEPTH, 2])
        B.dma(cst[:], dr['cst'])
        B.dma(pvec[:], dr['pvec'])

        def C(name, rows=128, c0=0, c1=None):
            o, w = CST_OFF[name]
            c1 = w if c1 is None else c1
            return cst[0:rows, o + c0:o + c1]

        def PV(name, l, rows=128, c0=0, c1=None):
            o, w = PV_OFF[(name, l)]
            c1 = w if c1 is None else c1
            return pvec[0:rows, o + c0:o + c1]

        ident = C('ident')
        ones = C('ones')
        bones = C('bones')
        evac_rr = [0]

        def evac(out, in_):
            evac_rr[0] += 1
            B.cp(out, in_, eng='act')

        def rsqrt_ln(out, in_, scale=1.0, bias=0.0):
            B.act(out, in_, AF.Ln, bias=bias, scale=scale)
            B.act(out, out, AF.Exp, scale=-0.5)

        def sig_finish(t, eng='dve'):
            B.ts(t, t, 1.0, ALU.add, eng=eng)
            B.recip(t, t)

        def silu_exp(out, x, tmp):
            B.act(tmp, x, AF.Exp, scale=-1.0)
            sig_finish(tmp)
            B.tt(out, x, tmp, ALU.mult)

        m0 = B.mark()
        c17t = B.sb("c17t", [17, D])
        cact = B.sb("cact", [17, D])
        cT = B.sb("cT", [128, KC, 17])
        wada = [B.sb("wada%d" % i, [128, KC, 512]) for i in range(4)]
        msm = [B.sb("msm%d" % i, [17, 512]) for i in range(2)]
        B.dma(c17t[:], dr['c17'])
        B.act(cact[:], c17t[:], AF.Silu)
        with B.bank() as pb:
            for kc in range(KC):
                B.tr(pb[:, kc * 17:(kc + 1) * 17], cact[0:17, kc * 128:(kc + 1) * 128], ident[0:17, 0:17])
            B.cp(cT[:].rearrange("p k s -> p (k s)"), pb[:, 0:KC * 17])
        stage('p1')
        nblk = 0
        for l in range(DEPTH):
            wsrc = dr['w_ada'][l].rearrange("(kc p) n -> p kc n", p=128)
            for cb in range(12):
                wt = wada[nblk % 4]
                ms = msm[nblk % 2]
                B.dma(wt[:], wsrc[:, :, cb * 512:(cb + 1) * 512], cls='w')
                nblk += 1
                with B.bank() as pb:
                    for kc in range(KC):
                        B.mm(pb[0:17, :], cT[:, kc, :], wt[:, kc, :], start=(kc == 0), stop=(kc == KC - 1))
                    B.cp(ms[:], pb[0:17, :], eng='act')
                with B.bank() as pb:
                    for q in range(4):
                        B.tr(pb[:, q * 17:(q + 1) * 17], ms[0:17, q * 128:(q + 1) * 128], ident[0:17, 0:17])
                    B.tt(modT[:, l, 4 * cb:4 * cb + 4, :], pb[:, 0:68].rearrange("p (q s) -> p q s", s=17),
                         bc(PV('bada', l, c0=4 * cb, c1=4 * cb + 4).unsqueeze(2), [128, 4, 17]), ALU.add)
                stage('p2_%d_%d' % (l, cb))
            B.ts(AMt[:, l], modT[:, l, 8:16, :], 1.0, ALU.add)
            B.tt(AMt[:, l], AMt[:, l], bc(PV('nmw', l).unsqueeze(2), [128, 8, 17]), ALU.mult)
            B.ts(AFt[:, l], modT[:, l, 32:40, :], 1.0, ALU.add)
            B.tt(AFt[:, l], AFt[:, l], bc(PV('nfw', l).unsqueeze(2), [128, 8, 17]), ALU.mult)
        B.memset(LBt[:], 0.0)
        B.tt(LBt[:, 1, :], PV('lbl', 1), PV('lbl', 0), ALU.subtract)
        B.act(LBt[:, 1, :], LBt[:, 1, :], AF.Sigmoid)
        B.ts(OMLt[:], LBt[:], -1.0, ALU.mult, 1.0, ALU.add)
        dbg('modT', modT[:].rearrange("p l m s -> p (l m s)"), [128, DEPTH * 48 * 17])
        B.release(m0)
        stage('prologue')

        def MOD(l, which):
            if which == 'AM':
                return AMt[:, l]
            if which == 'AF':
                return AFt[:, l]
            lo = {'BM': 0, 'GM': 16, 'BF': 24, 'GF': 40}[which]
            return modT[:, l, lo:lo + 8, :]

        def seqcols(blk):
            return (0, 1) if blk.kind == 'p' else (1, 17)

        def v3(ap, blk):
            return ap.rearrange("p (s t) -> p s t", t=blk.tb)

        def rms_stats(xb, n, sq, rstd):
            with B.bank() as pb:
                for kc in range(KC):
                    B.act(sq[kc % 2][:, 0:n], xb[:, kc, 0:n], AF.Square)
                    B.mm(pb[:, 0:n], ones, sq[kc % 2][:, 0:n], start=(kc == 0), stop=(kc == KC - 1))
                rsqrt_ln(rstd[:, 0:n], pb[:, 0:n], scale=1.0 / D, bias=1e-6)

        def norm_mod(blk, xb, n, l, Aw, Bw, hdst_fn, sq, rstd, tmpn):
            rms_stats(xb, n, sq, rstd)
            s0, s1 = seqcols(blk)
            A = MOD(l, Aw)
            Bm = MOD(l, Bw)
            for kc in range(KC):
                tn = tmpn[kc % 2]
                B.tt(tn[:, 0:n], xb[:, kc, 0:n], rstd[:, 0:n], ALU.mult, eng='pool')
                if blk.kind == 'p':
                    B.act(hdst_fn(kc), tn[:, 0:n].rearrange("p (s t) -> p s t", s=1), AF.Identity,
                          bias=Bm[:, kc, s0:s1], scale=A[:, kc, s0:s1])
                else:
                    t3 = v3(tn[:, 0:n], blk)
                    B.tt(t3, t3, bc(A[:, kc, s0:s1].unsqueeze(2), [128, blk.nsq, blk.tb]), ALU.mult)
                    B.tt(hdst_fn(kc), t3, bc(Bm[:, kc, s0:s1].unsqueeze(2), [128, blk.nsq, blk.tb]), ALU.add)

        def gate_res(blk, xdst, pbsrc, l, Gw, kc, n, tmpn):
            s0, s1 = seqcols(blk)
            G = MOD(l, Gw)
            if blk.kind == 'p':
                B.stt(xdst, pbsrc, G[:, kc, s0:s1], xdst, ALU.mult, ALU.add)
            else:
                tn = tmpn[kc % 2]
                t3 = v3(tn[:, 0:n], blk)
                B.tt(t3, v3(pbsrc, blk), bc(G[:, kc, s0:s1].unsqueeze(2), [128, blk.nsq, blk.tb]), ALU.mult)
                B.tt(xdst, xdst, tn[:, 0:n], ALU.add)

        prompt_blocks = [Blk('p', i) for i in range(TOKP // BLK)]
        sample_block = Blk('s', 0)
        ffn_blocks = [Blk('p', i, n=FBLK) for i in range(TOKP // FBLK)] + [Blk('s', 0)]

        def xkey(phase, lay, tok0, n):
            return ["xscr#%d" % g for g in range(tok0 // 64, (tok0 + n + 63) // 64)]

        hsl = [slice(0, 64), slice(64, 128)]

        for l in range(DEPTH):
            mph = B.mark()
            WIN = [B.sb("w_inA", [128, KC, A_COLS], BF16), B.sb("w_inB", [128, KC, B_COLS], BF16),
                   B.sb("w_inC", [128, KC, C_COLS], BF16)]
            WIN_OFF = [0, A_COLS, A_COLS + B_COLS, IN_COLS]
            w_out_sb = B.sb("w_out", [128, KC, D], BF16)
            wsrc = dr['w_in'][l].rearrange("(kc p) n -> p kc n", p=128)
            for gi in range(3):
                for kc in range(0, KC, 4):
                    B.dma(WIN[gi][:, kc:kc + 4, :], wsrc[:, kc:kc + 4, WIN_OFF[gi]:WIN_OFF[gi + 1]], eng='pool', cls='w')

            def w_in_ap(kc, c0, m):
                gi = 0 if c0 < WIN_OFF[1] else (1 if c0 < WIN_OFF[2] else 2)
                assert c0 + m <= WIN_OFF[gi + 1]
                return WIN[gi][:, kc, c0 - WIN_OFF[gi]:c0 - WIN_OFF[gi] + m]
            wsrc = dr['w_out'][l].rearrange("(kc p) n -> p kc n", p=128)
            for kc in range(0, KC, 4):
                B.dma(w_out_sb[:, kc:kc + 4, :], wsrc[:, kc:kc + 4, :], eng='pool', cls='w')

            xb = B.sb("xb", [128, KC, BLK])
            STAGE = B.sb("STAGE", [128, 1152])
            xtm = STAGE[:, 0:D]
            sq = [B.sb("sq%d" % i, [128, BLK]) for i in range(2)]
            rstd = B.sb("rstd", [128, BLK])
            tmpn = [B.sb("tmpn%d" % i, [128, BLK]) for i in range(2)]
            hT_p = B.sb("hT_p", [128, KC, 8 + BLK], BF16)
            hT_s = B.sb("hT_s", [128, KC, NSEQ * 5], BF16)
            PB = B.sb("PB", [128, 10, 3 + BLK])
            ZB = B.sb("ZB", [128, 3, BLK])
            BETA = B.sb("BETA", [6, BLK])
            APRE = B.sb("APRE", [6, BLK])
            yT = B.sb("yT", [128, 8, BLK], BF16)
            SL3 = [B.sb("slot%d" % i, [128, 3, BLK]) for i in range(12)]
            XS = B.sb("XS", [128, 10, BLK])
            TL = B.sb("TL", [128, BLK])
            class _TS:
                pass
            NT = BLK // 64
            TSets = []
            for ci in range(3):
                t_ = _TS()
                t_.TM = B.sb("TM", [128, NT, 4, 64])
                t_.A4 = B.sb("A4", [128, NT, 4, 64])
                t_.NM = B.sb("NM", [128, NT, 64])
                t_.Pm = B.sb("Pm", [128, NT, 64])
                t_.NQ = [B.sb("NQ%d" % i, [128, NT, 2, 64]) for i in range(2)]
                t_.WTs = B.sb("WTs", [128, NT, 64])
                t_.Xs = B.sb("Xs", [128, NT, 64])
                t_.U0s = B.sb("U0s", [128, NT, 64])
                t_.U0Ts = t_.U0s
                t_.Us = B.sb("Us", [128, 64])
                t_.UTs = B.sb("UTs", [128, 64])
                t_.SCP = B.sb("SCP", [128, NT, 4])
                t_.SCD = B.sb("SCD", [128, NT, 4])
                t_.GSL = B.sb("GSL", [128, NT, 64])
                t_.GUI = B.sb("GUI", [128, NT, 64])
                t_.Qs = B.sb("Qs", [128, NT, 64])
                t_.KB3 = B.sb("KB3", [128, NT, 3, 64])
                t_.EBC = B.sb("EBC", [128, NT, 64])
                t_.QTg = B.sb("QTg", [128, NT, 64])
                TSets.append(t_)
            TB = _TS()

            def use_set(ci):
                TB.__dict__.update(TSets[ci].__dict__)
            use_set(0)
            SVM = B.sb("SVM", [128, 32, 64])
            UM = SVM[:, 0:16, :]
            VM = SVM[:, 16:32, :]
            Mp = {}
            for mx, npair in (('a', 3), ('b', 3), ('c', 2)):
                for hp in range(npair):
                    Mp[(mx, hp)] = B.sb("Mp_%s%d" % (mx, hp), [128, 1, 64])
            Ms = B.sb("Ms", [128, NSEQ, 64])
            STG = SVM[0:64, :, :].rearrange("p (s h) k -> p s h k", h=2)
            OSH = xtm[0:16, :]
            OCV = STAGE[0:48, :]
            G6 = B.sb("G6", [6, BLK])
            GC6 = B.sb("GC6", [6, BLK])
            BE6 = B.sb("BE6", [6, BLK])
            EA6 = B.sb("EA6", [6, 1])

            NEGB = B.sb("NEGB", [128, 6])
            B.ts(NEGB[:, 0:3], PV('w0', l), -1.0, ALU.mult)
            B.ts(NEGB[:, 3:6], PV('a0', l), -1.0, ALU.mult)
            B.memset(TL[:], 0.0, eng='pool')
            B.memset(hT_p[:], 0.0)
            for key in Mp:
                B.memset(Mp[key][:], 0.0, eng='pool')

            def load_x_block(blk):
                n = blk.n
                if l == 0:
                    if blk.kind == 'p':
                        for ti in range(n // 128):
                            B.dma(xtm[:], dr['xp'][blk.tok0 + ti * 128: blk.tok0 + (ti + 1) * 128, :])
                            for g in range(2):
                                with B.bank() as pb:
                                    for q in range(4):
                                        kc = 4 * g + q
                                        B.tr(pb[:, q * 128:(q + 1) * 128], xtm[:, kc * 128:(kc + 1) * 128], ident)
                                    B.cp(xb[:, 4 * g:4 * g + 4, ti * 128:(ti + 1) * 128],
                                         pb[:, 0:512].rearrange("p (q t) -> p q t", q=4), eng=('act' if g else 'dve'))
                    else:
                        B.dma(xtm[0:64, :], dr['xs'])
                        with B.bank() as pb:
                            for kc in range(KC):
                                B.tr(pb[:, kc * 64:(kc + 1) * 64], xtm[0:64, kc * 128:(kc + 1) * 128], ident[0:64, 0:64])
                            B.cp(xb[:, :, 0:64], pb[:, 0:512].rearrange("p (q t) -> p q t", q=8))
                else:
                    B.dma(xb[:, :, 0:n], xscr[:, :, blk.tok0:blk.tok0 + n], rd=xkey(0, l, blk.tok0, n))

            def PBv(blk, j0, j1, off):
                if blk.kind == 'p':
                    return PB[:, j0:j1, off:off + blk.tb].unsqueeze(2)
                return PB[:, j0:j1, 0:blk.w].rearrange("p j (s u) -> p j s u", u=7)[:, :, :, off:off + blk.tb]

            def c4(ap, blk):
                return ap.rearrange("p j (s t) -> p j s t", t=blk.tb)

            def proj(blk, cols, dst_fn):
                c0, m = cols
                with B.bank() as pb:
                    src = hT_p if blk.kind == 'p' else hT_s
                    N = 4 + blk.n if blk.kind == 'p' else NSEQ * 5
                    for kc in range(KC):
                        B.mm(pb[0:m, 0:N], w_in_ap(kc, c0, m), src[:, kc, 0:N],
                             start=(kc == 0), stop=(kc == KC - 1))
                    dst_fn(pb)

            def proj_to_PB(blk, cols, j, halo):
                def f(pb):
                    m = cols[1]
                    if blk.kind == 'p':
                        evac(PB[0:m, j, 3 - halo:3 + blk.n], pb[0:m, 3 - halo:3 + blk.n])
                    else:
                        h = 1 if halo == 1 else 0
                        src = pb[0:m, 0:NSEQ * 5].rearrange("p (s u) -> p s u", u=5)[:, :, 1 - h:5]
                        dst = PB[0:m, j, 0:blk.w].rearrange("p (s u) -> p s u", u=7)[:, :, 3 - h:7]
                        evac(dst, src)
                proj(blk, cols, f)

            def proj_to(blk, cols, dst):
                def f(pb):
                    m = cols[1]
                    if blk.kind == 'p':
                        evac(dst, pb[0:m, 3:3 + blk.n])
                    else:
                        src = pb[0:m, 0:NSEQ * 5].rearrange("p (s u) -> p s u", u=5)[:, :, 1:5]
                        evac(dst.rearrange("p (s t) -> p s t", t=4), src)
                proj(blk, cols, f)

            def wy_double(blk, N0, Q0):
                nt = blk.ntile
                B.tt(TB.Pm[:, 0:nt, :], Q0, bc(C('ident2').unsqueeze(1), [128, nt, 64]), ALU.add)
                Ncur = N0
                Qcur = Q0
                for lev in range(blk.levels):
                    last = (lev == blk.levels - 1)
                    nq = TB.NQ[lev % 2]
                    with B.bank() as pb:
                        pv = pb[:, 0:nt * 128].rearrange("p (n a t) -> p n a t", n=nt, a=2)
                        for ti in range(nt):
                            for h2 in range(2):
                                hs = hsl[h2]
                                B.mm(pv[hs, ti, 0, :], Qcur[hs, ti, :], Ncur[hs, ti, :])
                                if not last:
                                    B.mm(pv[hs, ti, 1, :], Ncur[hs, ti, :], Qcur[hs, ti, :])
                        if last:
                            evac(nq[:, 0:nt, 0, :], pv[:, :, 0, :])
                        else:
                            evac(nq[:, 0:nt], pv)
                    with B.bank() as pb:
                        pv = pb[:, 0:nt * 64].rearrange("p (n t) -> p n t", n=nt)
                        for ti in range(nt):
                            for h2 in range(2):
                                hs = hsl[h2]
                                B.mm(pv[hs, ti, :], nq[hs, ti, 0, :], TB.Pm[hs, ti, :])
                        B.tt(TB.Pm[:, 0:nt, :], TB.Pm[:, 0:nt, :], pv, ALU.add)
                    Ncur = nq[:, 0:nt, 0, :]
                    Qcur = nq[:, 0:nt, 1, :]

            def seq_step(blk, M, WT, U0, U0T, ArbT, y_extra, rt_fn, b_tm, upd_extra, DC_fn, mode, ydst,
                         Vmask=None):
                nseg = blk.nseg
                sl = blk.seglen
                if WT is not None:
                    if blk.kind == 'p':
                        with B.bank() as pb:
                            for h2 in range(2):
                                hs = hsl[h2]
                                B.mm(pb[hs, 0:64], WT[hs, :], M[hs, 0, :])
                            B.tt(TB.Us[:], pb[:, 0:64], U0[:], ALU.add)
                    else:
                        with B.bank() as pb:
                            for h2 in range(2):
                                hs = hsl[h2]
                                for sg in range(nseg):
                                    B.mm(pb[hs, sg * sl:(sg + 1) * sl], M[hs, sg, :], WT[hs, sg * sl:(sg + 1) * sl])
                            B.tt(TB.UTs[:], pb[:, 0:64], U0T[:], ALU.add)
                        with B.bank() as pb:
                            for h2 in range(2):
                                hs = hsl[h2]
                                B.mm(pb[hs, 0:64], TB.UTs[hs, :], ident[hs, hs])
                            evac(TB.Us[:], pb[:, 0:64])
                with B.bank() as pb:
                    for h2 in range(2):
                        hs = hsl[h2]
                        first = True
                        if WT is not None:
                            B.mm(pb[hs, 0:64], TB.Us[hs, :], ArbT[hs, :], start=True, stop=False)
                            first = False
                        if y_extra is not None:
                            B.mm(pb[hs, 0:64], y_extra[0][hs, :], y_extra[1][hs, :], start=first, stop=False)
                            first = False
                        for sg in range(nseg):
                            cc = slice(sg * sl, (sg + 1) * sl)
                            B.mm(pb[hs, cc], M[hs, sg, :], rt_fn(cc)[hs, :], start=False, stop=True)
                    evac(ydst, pb[:, 0:64])
                nb = (nseg * 64 + 511) // 512
                with B.bank() as pb0, B.bank() as pb1:
                    pbs = [pb0, pb1]
                    if blk.kind == 's' and WT is not None:
                        B.tt(UM[:], bc(TB.Us[:].unsqueeze(1), [128, 16, 64]),
                             bc(C('ind').unsqueeze(2), [128, 16, 64]), ALU.mult, eng='pool')
                    for h2 in range(2):
                        hs = hsl[h2]
                        for half in range(nb):
                            ncols = min(512, nseg * 64 - half * 512)
                            g0 = half * 8
                            first = True
                            if WT is not None:
                                if blk.kind == 'p':
                                    rhsU = TB.Us[hs, :]
                                else:
                                    rhsU = UM[hs, g0:g0 + 8, :].rearrange("p s v -> p (s v)")
                                B.mm(pbs[half][hs, 0:ncols], b_tm[hs, :], rhsU, start=True, stop=(upd_extra is None))
                                first = False
                            if upd_extra is not None:
                                lt, rv = upd_extra
                                if blk.kind == 'p':
                                    rhsV = rv[hs, :]
                                else:
                                    rhsV = Vmask[hs, g0:g0 + 8, :].rearrange("p s v -> p (s v)")
                                B.mm(pbs[half][hs, 0:ncols], lt[hs, :], rhsV, start=first, stop=True)
                    DC = DC_fn()
                    for half in range(nb):
                        nsg = min(8, nseg - half * 8)
                        g0 = half * 8
                        Mv = M[:, g0:g0 + nsg, :]
                        pv = pbs[half][:, 0:nsg * 64].rearrange("p (s v) -> p s v", v=64)
                        DCb = bc(DC[:, g0:g0 + nsg].unsqueeze(2), [128, nsg, 64])
                        if mode == 'rwkv' and blk.kind == 'p':
                            B.tt(M[:, 0, :], M[:, 0, :], pbs[half][:, 0:64], ALU.add)
                            B.ts(M[:, 0, :], M[:, 0, :], DC[:, 0:1], ALU.mult)
                        elif mode == 'rwkv':
                            B.tt(Mv, Mv, pv, ALU.add)
                            B.tt(Mv, Mv, DCb, ALU.mult, eng='pool')
                        elif blk.kind == 'p':
                            B.stt(M[:, 0, :], M[:, 0, :], DC[:, 0:1], pbs[half][:, 0:64], ALU.mult, ALU.add)
                        else:
                            B.tt(Mv, Mv, DCb, ALU.mult, eng='pool')
                            B.tt(Mv, Mv, pv, ALU.add)

            def segend(ap2d, blk, cs):
                t = ap2d[:, cs]
                return t.rearrange("p (s u) -> p s u", u=blk.seglen)[:, :, blk.seglen - 1]

            def vmask_build(v_tm):
                B.tt(VM[:], bc(v_tm.unsqueeze(1), [128, 16, 64]),
                     bc(C('ind').unsqueeze(2), [128, 16, 64]), ALU.mult, eng='pool')

            def state_io_in(mx, hp, src):
                if mx == 'a':
                    for h2 in range(2):
                        B.dma(STG[:, :, h2, :], src[:, 2 * hp + h2, :, :].rearrange("s v k -> v s k"))
                    for g in range(2):
                        with B.bank() as pb:
                            for q in range(8):
                                sg = 8 * g + q
                                B.tr(pb[:, q * 64:(q + 1) * 64], STG[:, sg].rearrange("p h k -> p (h k)"),
                                     ident[0:64, 0:64])
                            evac(Ms[:, 8 * g:8 * g + 8, :].rearrange("p s v -> p (s v)"), pb[:, 0:512])
                else:
                    for h2 in range(2):
                        B.dma(Ms[64 * h2:64 * h2 + 64, :, :], src[:, 2 * hp + h2, :, :].rearrange("s k v -> k s v"))

            def state_io_out(hp, M, nsq, dst):
                for h2 in range(2):
                    odma(dst[:, 2 * hp + h2, :, :].rearrange("s k v -> k s v"), M[64 * h2:64 * h2 + 64, 0:nsq, :])

            def state_out_rwkv(hp, M, nsq, dst):
                so = hp if nsq == 1 else 0
                for g in range((nsq + 3) // 4):
                    ns = min(4, nsq - 4 * g)
                    with B.bank() as pb:
                        for q in range(ns):
                            B.tr(pb[0:64, q * 128:(q + 1) * 128], M[:, 4 * g + q, :], ident)
                        evac(STG[:, so + 4 * g:so + 4 * g + ns].rearrange("p s h k -> p (s h k)"), pb[0:64, 0:ns * 128])
                for h2 in range(2):
                    odma(dst[:, 2 * hp + h2, :, :].rearrange("s v k -> v s k"), STG[:, so:so + nsq, h2, :])

            def run_pairs(blk, fn, npair):
                if blk.kind == 's':
                    for hp in range(npair):
                        use_set(0)
                        fn(hp)
                    return
                for hp in range(npair):
                    use_set(hp)
                    with B.in_chain(hp):
                        fn(hp)
                B.flush_chains()
                use_set(0)

            def process_block(blk, is_last):
                n = blk.n
                v = blk.v
                T = blk.tb
                nsq = blk.nsq
                load_x_block(blk)
                if blk.kind == 'p':
                    hdst = lambda kc: hT_p[:, kc, 3:3 + n].rearrange("p (s t) -> p s t", s=1)
                else:
                    B.dma(OSH[:], dr['shift_s'][l])
                    with B.bank() as pb:
                        for kc in range(KC):
                            B.tr(pb[:, kc * 16:(kc + 1) * 16], OSH[0:16, kc * 128:(kc + 1) * 128], ident[0:16, 0:16])
                        B.cp(hT_s[:].rearrange("p k (s u) -> p k s u", u=5)[:, :, :, 0],
                             pb[:, 0:128].rearrange("p (k s) -> p k s", s=16))
                    hdst = lambda kc: hT_s[:, kc, :].rearrange("p (s u) -> p s u", u=5)[:, :, 1:5]
                if (l, blk.kind, blk.idx) == DBG_AT:
                    dbg('xin', xb[:].rearrange("p c t -> p (c t)"))
                norm_mod(blk, xb, n, l, 'AM', 'BM', hdst, sq, rstd, tmpn)
                if (l, blk.kind, blk.idx) == DBG_AT:
                    dbg('rstd', rstd[:])
                stage('h_%s%d_%d' % (blk.kind, blk.idx, l))

                for j in range(10):
                    proj_to_PB(blk, (128 * j, 128), j, 1)
                (SG, AA, G_, KK, K2, BBt, CUM, EP, EM, EPV, T1) = SL3[0:11]
                KT, BT_, AT = K2, BBt, KK
                cur = PBv(blk, 0, 10, 3)
                prv = PBv(blk, 0, 10, 2)
                XS4 = c4(XS[:, :, 0:n], blk)
                mu4 = bc(PV('mu', l).unsqueeze(2).unsqueeze(3), [128, 10, nsq, T])
                B.tt(XS4, prv, cur, ALU.subtract, eng='pool')
                B.tt(XS4, XS4, mu4, ALU.mult)
                B.tt(XS4, XS4, cur, ALU.add, eng='pool')
                if blk.kind == 'p' and blk.idx == 0 and l == 0:
                    dbg('XS0', XS[:].rearrange("p j t -> p (j t)"), [128, 10 * BLK])
                if (l, blk.kind, blk.idx) == DBG_AT:
                    dbg('XSa', XS[:].rearrange("p j t -> p (j t)"))
                B.act(TL[0:32, 0:n], XS[0:32, 9, 0:n], AF.Exp, scale=2.0)
                B.act(TL[64:128, 0:n], XS[64:128, 9, 0:n], AF.Exp, scale=-1.0)
                B.ts(TL[:, 0:n], TL[:, 0:n], 1.0, ALU.add)
                B.recip(TL[0:32, 0:n], TL[0:32, 0:n])
                B.recip(TL[64:128, 0:n], TL[64:128, 0:n])
                B.ts(TL[0:32, 0:n], TL[0:32, 0:n], -2.0, ALU.mult, 1.0, ALU.add)
                lora = PV('lora', l)
                for c in range(3):
                    with B.bank() as pb:
                        B.mm(pb[:, 0:n], lora[0:32, c * 128:(c + 1) * 128], TL[0:32, 0:n])
                        B.act(SG[:, c, 0:n], pb[:, 0:n], AF.Exp, bias=NEGB[:, c:c + 1], scale=-1.0)
                    with B.bank() as pb:
                        B.mm(pb[:, 0:n], lora[32:64, c * 128:(c + 1) * 128], XS[32:64, 9, 0:n])
                        B.act(AA[:, c, 0:n], pb[:, 0:n], AF.Exp, bias=NEGB[:, 3 + c:4 + c], scale=-1.0)
                    with B.bank() as pb:
                        B.mm(pb[:, 0:n], lora[64:128, c * 128:(c + 1) * 128], TL[64:128, 0:n])
                        evac(G_[:, c, 0:n], pb[:, 0:n])
                sig_finish(SG[:, :, 0:n])
                sig_finish(AA[:, :, 0:n], eng='pool')
                for c in range(3):
                    B.scan(CUM[:, c, 0:n], C('rm_' + v, c1=n), SG[:, c, 0:n])
                B.act(EP[:, :, 0:n], CUM[:, :, 0:n], AF.Exp, scale=-C0)
                B.act(EM[:, :, 0:n], CUM[:, :, 0:n], AF.Exp, scale=C0)
                B.tt(EPV[:, :, 0:n], CUM[:, :, 0:n], SG[:, :, 0:n], ALU.subtract, eng='pool')
                B.act(EPV[:, :, 0:n], EPV[:, :, 0:n], AF.Exp, scale=-C0)
                r_ = XS[:, 0:3, 0:n]
                k_ = XS[:, 3:6, 0:n]

                def b3(name):
                    return bc(PV(name, l).unsqueeze(2), [128, 3, n])
                B.tt(KK[:, :, 0:n], k_, b3('kk'), ALU.mult)
                B.act(T1[:, :, 0:n], KK[:, :, 0:n], AF.Square)
                for c in range(3):
                    with B.bank() as pb:
                        B.mm(pb[:, 0:n], bones, T1[:, c, 0:n])
                        rsqrt_ln(T1[:, c, 0:n], pb[:, 0:n], bias=1e-6)
                B.tt(KK[:, :, 0:n], KK[:, :, 0:n], T1[:, :, 0:n], ALU.mult)
                B.tt(K2[:, :, 0:n], AA[:, :, 0:n], b3('ka'), ALU.mult, eng='pool')
                B.tt(K2[:, :, 0:n], K2[:, :, 0:n], b3('ka'), ALU.subtract, eng='pool')
                B.stt(K2[:, :, 0:n], K2[:, :, 0:n], 1.0, k_, ALU.add, ALU.mult)
                B.tt(BBt[:, :, 0:n], KK[:, :, 0:n], AA[:, :, 0:n], ALU.mult, eng='pool')
                BON = AA
                B.tt(T1[:, :, 0:n], r_, K2[:, :, 0:n], ALU.mult)
                B.tt(T1[:, :, 0:n], T1[:, :, 0:n], b3('rk'), ALU.mult, eng='pool')
                for c in range(3):
                    with B.bank() as pb:
                        B.mm(pb[:, 0:n], bones, T1[:, c, 0:n])
                        B.tt(BON[:, c, 0:n], pb[:, 0:n], XS[:, 6 + c, 0:n], ALU.mult)
                RT = XS
                B.tt(RT[:, 0:3, 0:n], r_, EP[:, :, 0:n], ALU.mult)
                B.tt(KT[:, :, 0:n], K2[:, :, 0:n], EM[:, :, 0:n], ALU.mult, eng='pool')
                B.tt(BT_[:, :, 0:n], BBt[:, :, 0:n], EM[:, :, 0:n], ALU.mult)
                B.stt(AT[:, :, 0:n], KK[:, :, 0:n], -1.0, EPV[:, :, 0:n], ALU.mult, ALU.mult)
                YR = SG
                stage('rwkvprep_%s%d_%d' % (blk.kind, blk.idx, l))
                def rwkv_pair(hp):
                    nt = blk.ntile
                    if blk.kind == 's':
                        state_io_in('a', hp, dr['srw'][l])
                        Mst = Ms
                    else:
                        Mst = Mp[('a', hp)]
                    css = [slice(64 * ti, 64 * ti + 64) for ti in range(nt)]
                    with B.bank() as pb:
                        pv = pb[:, 0:nt * 256].rearrange("p (n q c) -> p n q c", n=nt, q=4)
                        for ti in range(nt):
                            for h2 in range(2):
                                hs = hsl[h2]
                                for q, src in enumerate((AT, BT_, KT, None)):
                                    s_ap = XS[hs, 6 + hp, css[ti]] if src is None else src[hs, hp, css[ti]]
                                    B.mm(pv[hs, ti, q, :], s_ap, ident[hs, hs])
                        evac(TB.TM[:, 0:nt], pv)
                    with B.bank() as pb1, B.bank() as pb2:
                        p4 = pb1[:, 0:nt * 256].rearrange("p (n a t) -> p n a t", n=nt, a=4)
                        pn = pb2[:, 0:nt * 64].rearrange("p (n t) -> p n t", n=nt)
                        for ti in range(nt):
                            for h2 in range(2):
                                hs = hsl[h2]
                                at = AT[hs, hp, css[ti]]
                                bt = BT_[hs, hp, css[ti]]
                                kt = KT[hs, hp, css[ti]]
                                rt = RT[hs, hp, css[ti]]
                                B.mm(p4[hs, ti, 0, :], bt, at)
                                B.mm(p4[hs, ti, 1, :], kt, at)
                                B.mm(p4[hs, ti, 2, :], bt, rt)
                                B.mm(p4[hs, ti, 3, :], kt, rt)
                                B.mm(pn[hs, ti, :], at, bt)
                        m4 = C('mask5_' + v, c1=256).rearrange("p (a t) -> p a t", a=4)
                        B.tt(TB.A4[:, 0:nt], p4, bc(m4.unsqueeze(1), [128, nt, 4, 64]), ALU.mult)
                        B.tt(TB.NM[:, 0:nt], pn, bc(C('mask5_' + v, c0=256, c1=320).unsqueeze(1), [128, nt, 64]), ALU.mult)
                    wy_double(blk, TB.NM[:, 0:nt, :], TB.A4[:, 0:nt, 0, :])
                    with B.bank() as pb:
                        pv = pb[:, 0:nt * 64].rearrange("p (n t) -> p n t", n=nt)
                        for ti in range(nt):
                            for h2 in range(2):
                                hs = hsl[h2]
                                B.mm(pv[hs, ti, :], TB.TM[hs, ti, 0, :], TB.Pm[hs, ti, :])
                        evac(TB.WTs[:, 0:nt], pv)
                    with B.bank() as pb:
                        pv = pb[:, 0:nt * 64].rearrange("p (n t) -> p n t", n=nt)
                        for ti in range(nt):
                            for h2 in range(2):
                                hs = hsl[h2]
                                B.mm(pv[hs, ti, :], TB.A4[hs, ti, 1, :], TB.TM[hs, ti, 3, :])
                        evac(TB.Xs[:, 0:nt], pv)
                    with B.bank() as pb:
                        pv = pb[:, 0:nt * 64].rearrange("p (n t) -> p n t", n=nt)
                        for ti in range(nt):
                            for h2 in range(2):
                                hs = hsl[h2]
                                if blk.kind == 'p':
                                    B.mm(pv[hs, ti, :], TB.Pm[hs, ti, :], TB.Xs[hs, ti, :])
                                else:
                                    B.mm(pv[hs, ti, :], TB.Xs[hs, ti, :], TB.Pm[hs, ti, :])
                        evac(TB.U0s[:, 0:nt], pv)
                    if blk.kind == 's':
                        vmask_build(TB.TM[:, 0, 3, :])
                    for ti in range(nt):
                        cs = css[ti]
                        seq_step(blk, Mst, TB.WTs[:, ti, :], TB.U0s[:, ti, :], TB.U0Ts[:, ti, :],
                                 ArbT=TB.A4[:, ti, 2, :],
                                 y_extra=(TB.TM[:, ti, 3, :], TB.A4[:, ti, 3, :]),
                                 rt_fn=lambda cc, hp=hp, cs=cs: RT[:, hp, cs][:, cc],
                                 b_tm=TB.TM[:, ti, 1, :],
                                 upd_extra=(TB.TM[:, ti, 2, :], TB.TM[:, ti, 3, :]),
                                 DC_fn=lambda hp=hp, cs=cs: segend(EP[:, hp, :], blk, cs),
                                 mode='rwkv', ydst=YR[:, hp, cs], Vmask=VM)
                    if blk.kind == 's':
                        state_out_rwkv(hp, Ms, NSEQ, dr['o_srw'][l])
                    elif is_last:
                        state_out_rwkv(hp, Mst, 1, dr['o_prw'][l])

                bb0 = A_COLS
                for j in range(9):
                    proj_to_PB(blk, (bb0 + 128 * j, 128), j, 3)
                proj_to(blk, (bb0 + 1152, 6), BETA[0:6, 0:n])
                proj_to(blk, (bb0 + 1158, 6), APRE[0:6, 0:n])
                for j in range(3):
                    proj_to(blk, (bb0 + 1164 + 128 * j, 128), ZB[:, j, 0:n])
                run_pairs(blk, rwkv_pair, 3)
                stage('rwkvchain_%s%d_%d' % (blk.kind, blk.idx, l))
                for c in range(3):
                    with B.bank() as pb:
                        B.mm(pb[:, 0:n], bones, YR[:, c, 0:n])
                        B.act(CUM[:, c, 0:n], pb[:, 0:n], AF.Copy, scale=1.0 / 64)
                mean = CUM[:, :, 0:n]
                B.tt(YR[:, :, 0:n], YR[:, :, 0:n], mean, ALU.subtract, eng='pool')
                B.act(T1[:, :, 0:n], YR[:, :, 0:n], AF.Square)
                for c in range(3):
                    with B.bank() as pb:
                        B.mm(pb[:, 0:n], bones, T1[:, c, 0:n])
                        rsqrt_ln(T1[:, c, 0:n], pb[:, 0:n], scale=1.0 / 64, bias=64e-5)
                B.tt(YR[:, :, 0:n], YR[:, :, 0:n], T1[:, :, 0:n], ALU.mult)
                B.tt(YR[:, :, 0:n], YR[:, :, 0:n], b3('lnw'), ALU.mult, eng='pool')
                B.tt(YR[:, :, 0:n], YR[:, :, 0:n], b3('lnb'), ALU.add)
                B.tt(YR[:, :, 0:n], YR[:, :, 0:n], BON[:, :, 0:n], ALU.add, eng='pool')
                if (l, blk.kind, blk.idx) == DBG_AT:
                    dbg('YRa', YR[:].rearrange("p c t -> p (c t)"))
                    dbg('T1a', T1[:].rearrange("p c t -> p (c t)"))
                    dbg('Ga', G_[:].rearrange("p c t -> p (c t)"))
                B.tt(yT[:, 0:3, 0:n], YR[:, :, 0:n], G_[:, :, 0:n], ALU.mult)

                stage('rwkv_%s%d_%d' % (blk.kind, blk.idx, l))
                if blk.kind == 's':
                    B.dma(OCV[:], dr['sconv'][l].rearrange("s j c -> (s j) c"))
                    for g in range(3):
                        with B.bank() as pb:
                            for q in range(3):
                                j = 3 * g + q
                                B.tr(pb[:, q * 48:(q + 1) * 48], OCV[0:48, j * 128:(j + 1) * 128], ident[0:48, 0:48])
                            dst = PB[:, 3 * g:3 * g + 3, 0:blk.w].rearrange("p j (s u) -> p j s u", u=7)[:, :, :, 0:3]
                            evac(dst, pb[:, 0:144].rearrange("p (j s u) -> p j s u", j=3, u=3))
                if blk.kind == 's' or is_last:
                    nr = 3 * nsq
                    for g in range(3):
                        with B.bank() as pb:
                            for q in range(3):
                                j = 3 * g + q
                                if blk.kind == 'p':
                                    src = PB[:, j, n:n + 3]
                                else:
                                    src = TL[:, 48 * (j % 2):48 * (j % 2) + 48]
                                    B.cp(src.rearrange("p (s u) -> p s u", u=3),
                                         PB[:, j, 0:blk.w].rearrange("p (s u) -> p s u", u=7)[:, :, 4:7], eng='pool')
                                B.tr(pb[0:nr, q * 128:(q + 1) * 128], src, ident)
                            evac(OCV[0:nr, 384 * g:384 * g + 384], pb[0:nr, 0:384])
                    dst = dr['o_sconv'][l] if blk.kind == 's' else dr['o_pconv'][l]
                    odma(dst.rearrange("s j c -> (s j) c"), OCV[0:nr, :])
                (QKV0, QKV1, QKV2, CA0, CA1, CA2, SQ0, SQ1) = SL3[0:8]
                cw = PV('convw', l).rearrange("p (j c) -> p j c", j=4)
                ACC = [QKV0, QKV1, QKV2]
                TMPc = [CA0, CA1, CA2]
                for g in range(3):
                    acc4 = c4(ACC[g][:, :, 0:n], blk)
                    tmp4 = c4(TMPc[g][:, :, 0:n], blk)
                    for j in range(4):
                        src = PBv(blk, 3 * g, 3 * g + 3, j)
                        wj = bc(cw[:, j, 3 * g:3 * g + 3].unsqueeze(2).unsqueeze(3), [128, 3, nsq, T])
                        if j == 0:
                            B.tt(acc4, src, wj, ALU.mult, eng=('pool' if g == 1 else 'dve'))
                        else:
                            B.tt(tmp4, src, wj, ALU.mult, eng='pool')
                            B.tt(acc4, acc4, tmp4, ALU.add)
                    silu_exp(ACC[g][:, :, 0:n], ACC[g][:, :, 0:n], TMPc[g][:, :, 0:n])
                Qg, Kg, Vg = QKV0, QKV1, QKV2
                for arr, sc_, bi_ in ((Qg, 64.0, 64e-6), (Kg, 1.0, 1e-6)):
                    B.act(SQ0[:, :, 0:n], arr[:, :, 0:n], AF.Square)
                    for c in range(3):
                        with B.bank() as pb:
                            B.mm(pb[:, 0:n], bones, SQ0[:, c, 0:n])
                            rsqrt_ln(SQ1[:, c, 0:n], pb[:, 0:n], scale=sc_, bias=bi_)
                    B.tt(arr[:, :, 0:n], arr[:, :, 0:n], SQ1[:, :, 0:n], ALU.mult)
                B.act(BE6[:, 0:n], BETA[0:6, 0:n], AF.Exp, scale=-1.0)
                sig_finish(BE6[:, 0:n])
                B.act(G6[:, 0:n], APRE[0:6, 0:n], AF.Exp, bias=PV('dtb', l, rows=6))
                B.act(G6[:, 0:n], G6[:, 0:n], AF.Ln, bias=1.0)
                B.act(EA6[:], PV('alog', l, rows=6), AF.Exp)
                B.ts(G6[:, 0:n], G6[:, 0:n], EA6[:, 0:1], ALU.mult, -1.0, ALU.mult)
                B.scan(GC6[:, 0:n], C('rm_' + v, rows=6, c1=n), G6[:, 0:n])
                silu_exp(ZB[:, :, 0:n], ZB[:, :, 0:n], CA0[:, :, 0:n])
                OR = SL3[8]
                stage('gdnprep_%s%d_%d' % (blk.kind, blk.idx, l))
                def gdn_pair(hp):
                    nt = blk.ntile
                    Mst = Ms if blk.kind == 's' else Mp[('b', hp)]
                    if blk.kind == 's':
                        state_io_in('b', hp, dr['sgdn'][l])
                    css = [slice(64 * ti, 64 * ti + 64) for ti in range(nt)]
                    SCP = TB.SCP[:, 0:nt, :]
                    SCD = TB.SCD[:, 0:nt, :]
                    with B.bank() as pb:
                        pv = pb[:, 0:nt * 4].rearrange("p (n q) -> p n q", n=nt)
                        for ti in range(nt):
                            for h2 in range(2):
                                hs = hsl[h2]
                                hh = 2 * hp + h2
                                for q, src in enumerate((BE6, GC6, G6)):
                                    B.mm(pv[hs, ti, q:q + 1], src[0:6, css[ti]], ident[0:6, hh:hh + 1])
                        evac(SCP[:, :, 0:3], pv[:, :, 0:3])
                    with B.bank() as pb:
                        pv = pb[:, 0:nt * 4].rearrange("p (n q) -> p n q", n=nt)
                        for ti in range(nt):
                            for h2 in range(2):
                                hs = hsl[h2]
                                B.mm(pv[hs, ti, 0:1], C('BS_' + v)[hs, :], TB.SCP[hs, ti, 2:3])
                        B.tt(SCD[:, :, 2:3], pv[:, :, 0:1], SCP[:, :, 1:2], ALU.subtract)
                    B.act(SCD[:, :, 2:3], SCD[:, :, 2:3], AF.Exp)
                    B.act(SCD[:, :, 3:4], SCP[:, :, 1:2], AF.Exp)
                    B.stt(SCD[:, :, 0:1], SCP[:, :, 0:1], -1.0, SCD[:, :, 3:4], ALU.mult, ALU.mult)
                    B.ts(SCD[:, :, 1:2], SCP[:, :, 0:1], -1.0, ALU.mult)
                    GSL = TB.GSL[:, 0:nt, :]
                    GUI = TB.GUI[:, 0:nt, :]
                    NMv = TB.NM[:, 0:nt, :]

                    def b64(ap):
                        return bc(ap, [128, nt, 64])
                    with B.bank() as pb:
                        p3 = pb[:, 0:nt * 192].rearrange("p (n a t) -> p n a t", n=nt, a=3)
                        for ti in range(nt):
                            for h2 in range(2):
                                hs = hsl[h2]
                                hh = 2 * hp + h2
                                B.mm(p3[hs, ti, 0, :], C('osel', rows=6, c0=hh * 64, c1=hh * 64 + 64), GC6[0:6, css[ti]])
                                B.mm(p3[hs, ti, 1, :], Kg[hs, hp, css[ti]], Kg[hs, hp, css[ti]])
                                B.mm(p3[hs, ti, 2, :], Kg[hs, hp, css[ti]], Qg[hs, hp, css[ti]])
                        B.tt(GSL, p3[:, :, 0, :], b64(SCP[:, :, 1:2]), ALU.subtract)
                        B.tt(GUI, GSL, bc(C('BIU_' + v).unsqueeze(1), [128, nt, 64]), ALU.subtract)
                        B.tt(GSL, GSL, bc(C('BSL_' + v).unsqueeze(1), [128, nt, 64]), ALU.add)
                        B.act(GSL, GSL, AF.Exp, scale=-1.0)
                        B.act(GUI, GUI, AF.Exp)
                        B.tt(NMv, p3[:, :, 1, :], b64(SCD[:, :, 1:2]), ALU.mult)
                        B.tt(NMv, NMv, GSL, ALU.mult)
                        B.tt(GUI, p3[:, :, 2, :], GUI, ALU.mult)
                    with B.bank() as pb:
                        pv = pb[:, 0:nt * 64].rearrange("p (n t) -> p n t", n=nt)
                        for ti in range(nt):
                            for h2 in range(2):
                                hs = hsl[h2]
                                B.mm(pv[hs, ti, :], TB.NM[hs, ti, :], ident[hs, hs])
                        evac(TB.Qs[:, 0:nt], pv)
                    wy_double(blk, NMv, TB.Qs[:, 0:nt, :])
                    with B.bank() as pb:
                        pv = pb[:, 0:nt * 128].rearrange("p (n q c) -> p n q c", n=nt, q=2)
                        for ti in range(nt):
                            for h2 in range(2):
                                hs = hsl[h2]
                                B.mm(pv[hs, ti, 0, :], Kg[hs, hp, css[ti]], ident[hs, hs])
                                B.mm(pv[hs, ti, 1, :], Vg[hs, hp, css[ti]], ident[hs, hs])
                        evac(TB.TM[:, 0:nt, 0:2, :], pv)
                    B.tt(TB.KB3[:, 0:nt, 0, :], TB.TM[:, 0:nt, 0, :], b64(SCD[:, :, 0:1]), ALU.mult)
                    B.tt(TB.KB3[:, 0:nt, 1, :], TB.TM[:, 0:nt, 1, :], b64(SCP[:, :, 0:1]), ALU.mult)
                    B.tt(TB.KB3[:, 0:nt, 2, :], TB.TM[:, 0:nt, 0, :], b64(SCD[:, :, 2:3]), ALU.mult)
                    with B.bank() as pb:
                        pv = pb[:, 0:nt * 64].rearrange("p (n t) -> p n t", n=nt)
                        for ti in range(nt):
                            for h2 in range(2):
                                hs = hsl[h2]
                                B.mm(pv[hs, ti, :], TB.KB3[hs, ti, 0, :], TB.Pm[hs, ti, :])
                        evac(TB.WTs[:, 0:nt], pv)
                    with B.bank() as pb:
                        pv = pb[:, 0:nt * 64].rearrange("p (n t) -> p n t", n=nt)
                        for ti in range(nt):
                            for h2 in range(2):
                                hs = hsl[h2]
                                if blk.kind == 'p':
                                    B.mm(pv[hs, ti, :], TB.Pm[hs, ti, :], TB.KB3[hs, ti, 1, :])
                                else:
                                    B.mm(pv[hs, ti, :], TB.KB3[hs, ti, 1, :], TB.Pm[hs, ti, :])
                        evac(TB.U0s[:, 0:nt], pv)
                    with B.bank() as pb:
                        pv = pb[:, 0:nt * 64].rearrange("p (n t) -> p n t", n=nt)
                        for ti in range(nt):
                            B.mm(pv[:, ti, :], C('selb', rows=6, c0=hp * 128, c1=hp * 128 + 128), GC6[0:6, css[ti]])
                        B.act(TB.EBC[:, 0:nt], pv, AF.Exp)
                    B.tt(TB.QTg[:, 0:nt], Qg[:, hp, 0:nt * 64].rearrange("p (n t) -> p n t", n=nt), TB.EBC[:, 0:nt], ALU.mult)
                    for ti in range(nt):
                        cs = css[ti]
                        seq_step(blk, Mst, TB.WTs[:, ti, :], TB.U0s[:, ti, :], TB.U0Ts[:, ti, :],
                                 ArbT=TB.GUI[:, ti, :],
                                 y_extra=None,
                                 rt_fn=lambda cc, ti=ti: TB.QTg[:, ti, cc],
                                 b_tm=TB.KB3[:, ti, 2, :],
                                 upd_extra=None,
                                 DC_fn=lambda ti=ti: TB.EBC[:, ti, :].rearrange("p (s u) -> p s u", u=blk.seglen)[:, :, blk.seglen - 1],
                                 mode='gdn', ydst=OR[:, hp, cs])
                    if blk.kind == 's':
                        state_io_out(hp, Ms, NSEQ, dr['o_sgdn'][l])
                    elif is_last:
                        state_io_out(hp, Mst, 1, dr['o_pgdn'][l])

                cc0 = A_COLS + B_COLS
                for j in range(8):
                    proj_to(blk, (cc0 + 128 * j, 128), PB[:, j, 0:n])
                run_pairs(blk, gdn_pair, 3)
                stage('gdnchain_%s%d_%d' % (blk.kind, blk.idx, l))
                B.act(SQ0[:, :, 0:n], OR[:, :, 0:n], AF.Square)
                for c in range(3):
                    with B.bank() as pb:
                        B.mm(pb[:, 0:n], bones, SQ0[:, c, 0:n])
                        rsqrt_ln(SQ1[:, c, 0:n], pb[:, 0:n], scale=1.0 / 64, bias=1e-6)
                B.tt(OR[:, :, 0:n], OR[:, :, 0:n], SQ1[:, :, 0:n], ALU.mult)
                B.ts(OR[:, :, 0:n], OR[:, :, 0:n], PV('gnw', l), ALU.mult, eng='pool')
                if (l, blk.kind, blk.idx) == DBG_AT:
                    dbg('ORa', OR[:].rearrange("p c t -> p (c t)"))
                    dbg('ZBa', ZB[:].rearrange("p c t -> p (c t)"))
                B.tt(yT[:, 3:6, 0:n], OR[:, :, 0:n], ZB[:, :, 0:n], ALU.mult)

                stage('gdn_%s%d_%d' % (blk.kind, blk.idx, l))
                (SF, LF, KH_, QS, BC_, QTL, KTL, QH) = [t[:, 0:2, 0:n] for t in SL3[0:8]]
                (KHAT, OH, HT1) = [t[:, 0:2, 0:n] for t in SL3[8:11]]
                lb2 = bc(LBt[:, l, :].unsqueeze(2), [128, 2, n])
                oml2 = bc(OMLt[:, l, :].unsqueeze(2), [128, 2, n])
                B.act(SF, PB[:, 2:4, 0:n], AF.Exp, scale=-1.0)
                sig_finish(SF)
                B.tt(LF, SF, oml2, ALU.mult, eng='pool')
                B.tt(LF, LF, lb2, ALU.add)
                B.act(LF, LF, AF.Ln)
                B.ts(KH_, SF, -1.0, ALU.mult, 1.0, ALU.add, eng='pool')
                B.tt(KH_, KH_, oml2, ALU.mult)
                silu_exp(QS, PB[:, 0:2, 0:n], QS)
                for c in range(2):
                    B.scan(BC_[:, c, :], C('rm_' + v, c1=n), LF[:, c, :])
                sl = blk.seglen
                nsg_all = n // sl
                mid = sl // 2 - 1

                def sv(ap):
                    return ap.rearrange("p c (s u) -> p c s u", u=sl)
                B.tt(sv(HT1), sv(BC_), bc(sv(BC_)[:, :, :, mid:mid + 1], [128, 2, nsg_all, sl]), ALU.subtract, eng='pool')
                B.act(QTL, HT1, AF.Exp)
                B.tt(QTL, QTL, QS, ALU.mult)
                B.act(KTL, HT1, AF.Exp, scale=-1.0)
                B.tt(KTL, KTL, KH_, ALU.mult, eng='pool')
                EB = SL3[11][:, 0:2, 0:n]
                B.act(EB, BC_, AF.Exp)
                B.tt(QH, QS, EB, ALU.mult)
                B.tt(sv(HT1), bc(sv(BC_)[:, :, :, sl - 1:sl], [128, 2, nsg_all, sl]), sv(BC_), ALU.subtract, eng='pool')
                B.act(HT1, HT1, AF.Exp)
                B.tt(KHAT, KH_, HT1, ALU.mult)
                def hgrn_pair(hp):
                    nt = blk.ntile
                    if blk.kind == 's':
                        state_io_in('c', hp, dr['shg'][l])
                        Mst = Ms
                    else:
                        Mst = Mp[('c', hp)]
                    css = [slice(64 * ti, 64 * ti + 64) for ti in range(nt)]
                    with B.bank() as pb:
                        pv = pb[:, 0:nt * 128].rearrange("p (n q c) -> p n q c", n=nt, q=2)
                        for ti in range(nt):
                            for h2 in range(2):
                                hs = hsl[h2]
                                B.mm(pv[hs, ti, 0, :], KHAT[hs, hp, css[ti]], ident[hs, hs])
                                B.mm(pv[hs, ti, 1, :], PB[hs, 4 + hp, css[ti]], ident[hs, hs])
                        evac(TB.TM[:, 0:nt, 0:2, :], pv)
                    with B.bank() as pb:
                        pv = pb[:, 0:nt * 64].rearrange("p (n t) -> p n t", n=nt)
                        for ti in range(nt):
                            for h2 in range(2):
                                hs = hsl[h2]
                                B.mm(pv[hs, ti, :], KTL[hs, hp, css[ti]], QTL[hs, hp, css[ti]])
                        B.ts(TB.GUI[:, 0:nt], pv, 3.0e38, ALU.min, -3.0e38, ALU.max)
                        B.tt(TB.GUI[:, 0:nt], TB.GUI[:, 0:nt], bc(C('IU_' + v).unsqueeze(1), [128, nt, 64]), ALU.mult)
                    if blk.kind == 's':
                        vmask_build(TB.TM[:, 0, 1, :])
                    for ti in range(nt):
                        cs = css[ti]
                        seq_step(blk, Mst, None, None, None,
                                 ArbT=None,
                                 y_extra=(TB.TM[:, ti, 1, :], TB.GUI[:, ti, :]),
                                 rt_fn=lambda cc, hp=hp, cs=cs: QH[:, hp, cs][:, cc],
                                 b_tm=None,
                                 upd_extra=(TB.TM[:, ti, 0, :], TB.TM[:, ti, 1, :]),
                                 DC_fn=lambda hp=hp, cs=cs: segend(EB[:, hp, :], blk, cs),
                                 mode='gdn', ydst=OH[:, hp, cs], Vmask=VM)
                    if blk.kind == 's':
                        state_io_out(hp, Ms, NSEQ, dr['o_shg'][l])
                    elif is_last:
                        state_io_out(hp, Mst, 1, dr['o_phg'][l])

                stage('hgrnprep_%s%d_%d' % (blk.kind, blk.idx, l))
                run_pairs(blk, hgrn_pair, 2)
                stage('hgrnchain_%s%d_%d' % (blk.kind, blk.idx, l))
                SQh = SL3[0][:, 0:2, 0:n]
                SQg = SL3[1][:, 0:2, 0:n]
                B.act(SQh, OH, AF.Square)
                for c in range(2):
                    with B.bank() as pb:
                        B.mm(pb[:, 0:n], bones, SQh[:, c, :])
                        rsqrt_ln(SQg[:, c, :], pb[:, 0:n], scale=1.0 / 64, bias=1e-6)
                B.tt(OH, OH, SQg, ALU.mult)
                B.ts(OH, OH, PV('hnw', l), ALU.mult, eng='pool')
                B.act(SQh, PB[:, 6:8, 0:n], AF.Exp, scale=-1.0)
                sig_finish(SQh)
                if (l, blk.kind, blk.idx) == DBG_AT:
                    dbg('OHa', SL3[9][:].rearrange("p c t -> p (c t)"))
                    dbg('xba', xb[:].rearrange("p c t -> p (c t)"))
                B.tt(yT[:, 6:8, 0:n], OH, SQh, ALU.mult)

                if blk.kind == 's' or is_last:
                    HL = TL[:, :]
                    for kc in range(KC):
                        if blk.kind == 'p':
                            srcl = hT_p[:, kc, 3 + n - 1:3 + n]
                        else:
                            srcl = hT_s[:, kc, :].rearrange("p (s u) -> p s u", u=5)[:, :, 4]
                        B.cp(HL[:, kc * 16:kc * 16 + nsq], srcl, eng='pool')
                    for half in range(2):
                        with B.bank() as pb:
                            for q in range(4):
                                kc = 4 * half + q
                                B.tr(pb[0:nsq, q * 128:(q + 1) * 128], HL[:, kc * 16:kc * 16 + nsq], ident)
                            evac(OSH[0:nsq, 512 * half:512 * half + 512], pb[0:nsq, 0:512])
                    dst = dr['o_sshift'][l] if blk.kind == 's' else dr['o_pshift'][l]
                    odma(dst, OSH[0:nsq, :])
                for m in range(KC):
                    with B.bank() as pb:
                        for kc in range(KC):
                            B.mm(pb[:, 0:n], w_out_sb[:, kc, m * 128:(m + 1) * 128], yT[:, kc, 0:n],
                                 start=(kc == 0), stop=(kc == KC - 1))
                        gate_res(blk, xb[:, m, 0:n], pb[:, 0:n], l, 'GM', m, n, tmpn)
                B.dma(xscr[:, :, blk.tok0:blk.tok0 + n], xb[:, :, 0:n], wr=xkey(0, l, blk.tok0, n))
                stage('blk_%s%d_%d' % (blk.kind, blk.idx, l))

            for bi, blk in enumerate(prompt_blocks):
                process_block(blk, bi == len(prompt_blocks) - 1)
                B.cp(hT_p[:, :, 0:3], hT_p[:, :, BLK:BLK + 3], eng='pool')
            process_block(sample_block, True)
            B.release(mph)
            stage('mix_%d' % l)

            fph = B.mark()
            JG = 6
            NG = (NJ + JG - 1) // JG
            WFI = [B.sb("w_fi%d" % g, [128, KC, 2, (min(NJ, g * JG + JG) - g * JG) * 128], BF16) for g in range(NG)]
            WFO = [B.sb("w_fo%d" % g, [128, min(NJ, g * JG + JG) - g * JG, D], BF16) for g in range(NG)]
            wsrc = dr['w_fi'][l].rearrange("(kc p) n -> p kc n", p=128)
            for g in range(NG):
                j0, j1 = g * JG, min(NJ, g * JG + JG)
                w = (j1 - j0) * 128
                for part in range(2):
                    B.dma(WFI[g][:, :, part, 0:w], wsrc[:, :, part * FF + j0 * 128:part * FF + j0 * 128 + w],
                          eng='pool', cls='w')
            wsrc = dr['w_fo'][l].rearrange("(j p) n -> p j n", p=128)
            for g in range(NG):
                j0, j1 = g * JG, min(NJ, g * JG + JG)
                B.dma(WFO[g][:, 0:j1 - j0, :], wsrc[:, j0:j1, :], eng='pool', cls='w')
            xb2 = B.sb("xb2", [128, KC, FBLK])
            h2T = B.sb("h2T", [128, KC, FBLK], BF16)
            actT = B.sb("actT", [128, NJ, FBLK], BF16)
            sq2 = [B.sb("sq2_%d" % i, [128, FBLK]) for i in range(2)]
            rstd2 = B.sb("rstd2", [128, FBLK])
            tmp2 = [B.sb("tmp2_%d" % i, [128, FBLK]) for i in range(2)]
            sgt = [B.sb("sgt%d" % i, [128, FBLK]) for i in range(2)]
            last_layer = (l == DEPTH - 1)
            if last_layer:
                yfin = B.sb("yfin", [128, KC, FBLK])
                ytm = B.sb("ytm", [128, D])
            for fb in ffn_blocks:
                n = fb.n
                B.dma(xb2[:, :, 0:n], xscr[:, :, fb.tok0:fb.tok0 + n], rd=xkey(1, l, fb.tok0, n))
                hdst = lambda kc, fb=fb, n=n: v3(h2T[:, kc, 0:n], fb)
                norm_mod(fb, xb2, n, l, 'AF', 'BF', hdst, sq2, rstd2, tmp2)
                for j in range(NJ):
                    with B.bank() as pg, B.bank() as pu:
                        for kc in range(KC):
                            B.mm(pg[:, 0:n], WFI[j // JG][:, kc, 0, (j % JG) * 128:(j % JG + 1) * 128], h2T[:, kc, 0:n],
                                 start=(kc == 0), stop=(kc == KC - 1))
                        for kc in range(KC):
                            B.mm(pu[:, 0:n], WFI[j // JG][:, kc, 1, (j % JG) * 128:(j % JG + 1) * 128], h2T[:, kc, 0:n],
                                 start=(kc == 0), stop=(kc == KC - 1))
                        B.act(sgt[j % 2][:, 0:n], pg[:, 0:n], AF.Silu)
                        B.tt(actT[:, j, 0:n], sgt[j % 2][:, 0:n], pu[:, 0:n], ALU.mult)
                for m in range(KC):
                    with B.bank() as pb:
                        for j in range(NJ):
                            B.mm(pb[:, 0:n], WFO[j // JG][:, j % JG, m * 128:(m + 1) * 128], actT[:, j, 0:n],
                                 start=(j == 0), stop=(j == NJ - 1))
                        gate_res(fb, xb2[:, m, 0:n], pb[:, 0:n], l, 'GF', m, n, tmp2)
                if not last_layer:
                    B.dma(xscr[:, :, fb.tok0:fb.tok0 + n], xb2[:, :, 0:n], wr=xkey(1, l, fb.tok0, n))
                else:
                    rms_stats(xb2, n, sq2, rstd2)
                    for kc in range(KC):
                        B.stt(yfin[:, kc, 0:n], xb2[:, kc, 0:n], PV('fnw', 0, c0=kc, c1=kc + 1), rstd2[:, 0:n],
                              ALU.mult, ALU.mult)
                    nt = (n + 127) // 128
                    for ti in range(nt):
                        r = min(128, n - 128 * ti)
                        for g in range(2):
                            with B.bank() as pb:
                                for q in range(4):
                                    kc = 4 * g + q
                                    B.tr(pb[0:r, q * 128:(q + 1) * 128], yfin[:, kc, 128 * ti:128 * ti + r], ident)
                                evac(ytm[0:r, 512 * g:512 * g + 512], pb[0:r, 0:512])
                        if fb.kind == 'p':
                            odma(dr['yp'][fb.tok0 + 128 * ti:fb.tok0 + 128 * ti + r, :], ytm[0:r, :])
                        else:
                            odma(dr['ys'], ytm[0:r, :])
            B.release(fph)
            stage('ffn_%d' % l)

        S.disabled = False
        S.barrier()
        with nc.Block() as block:
            S.emit(block)
    build_program.dbg_list = dbg_list
    build_program.n_ops = S.n_ops
    build_program.sbuf_top = B.top
    return nc


INPUT_ORDER = None


def make_in_maps(inp):
    pv = pack_pvec(inp)
    cst = make_consts()
    f = lambda a: np.ascontiguousarray(np.asarray(a, np.float32))
    shared = {
        'w_ada': f(inp['w_ada']), 'w_in': f(inp['w_in']), 'w_out': f(inp['w_out']),
        'w_fi': f(inp['w_ffn_in']), 'w_fo': f(inp['w_ffn_out']), 'pvec': pv, 'cst': cst,
    }
    maps = []
    for c in range(NCORES):
        s0, s1 = NSEQ * c, NSEQ * (c + 1)
        m = dict(shared)
        m['xp'] = f(inp['x_prompt'][c])
        m['xs'] = f(np.asarray(inp['x_sample'][s0:s1]).reshape(TOKS, D))
        m['c17'] = f(np.concatenate([np.asarray(inp['c_prompt'][c:c + 1]), np.asarray(inp['c_sample'][s0:s1])], axis=0))
        m['shift_s'] = f(np.asarray(inp['state_rwkv_shift'])[:, s0:s1])
        m['srw'] = f(np.asarray(inp['state_rwkv'])[:, s0:s1])
        m['sconv'] = f(np.asarray(inp['state_gdn_conv'])[:, s0:s1])
        m['sgdn'] = f(np.asarray(inp['state_gdn'])[:, s0:s1])
        m['shg'] = f(np.asarray(inp['state_hgrn'])[:, s0:s1])
        maps.append(m)
    return maps


_NC_CACHE = {}


def kernel(**inputs):
    inp = {k: np.asarray(v) for k, v in inputs.items()}
    maps = make_in_maps(inp)
    if 'nc' not in _NC_CACHE:
        _NC_CACHE['nc'] = build_program()
    nc = _NC_CACHE['nc']
    res = run_bass_kernel_spmd(nc, maps, core_ids=list(range(NCORES)))
    R = res.results

    def cat(name, axis):
        return np.concatenate([np.asarray(R[c][name], np.float32) for c in range(NCORES)], axis=axis)

    y_prompt = np.stack([np.asarray(R[c]['yp'], np.float32) for c in range(NCORES)], axis=0)
    y_sample = cat('ys', 0).reshape(NCORES * NSEQ, TSS, D)
    p_shift = cat('o_pshift', 1)
    p_rwkv = cat('o_prw', 1)
    p_conv = cat('o_pconv', 1)
    p_gdn = cat('o_pgdn', 1)
    p_hgrn = cat('o_phg', 1)
    s_shift = cat('o_sshift', 1)
    s_rwkv = cat('o_srw', 1)
    s_conv = cat('o_sconv', 1)
    s_gdn = cat('o_sgdn', 1)
    s_hgrn = cat('o_shg', 1)
    return (y_prompt, y_sample, p_shift, p_rwkv, p_conv, p_gdn, p_hgrn,
            s_shift, s_rwkv, s_conv, s_gdn, s_hgrn)
```
